# Optimizing a Trainium2 kernel written in Bass

```python
import math
import jax
import jax.numpy as jnp
from jax import lax
import numpy as np

D_MODEL = 2048
BATCH = 4
SEQ = 2048
DEPTH = 4
DEC_BATCH = 32
DEC_SEQ = 8
PAST_LEN = 16384
PAGE_SIZE = 128

HEAD_DIM = 64
A_HEADS = 4
A_DK = 128
A_DV = 128
A_QK = A_HEADS * A_DK
A_VW = A_HEADS * A_DV
A_CONV_CH = 2 * A_QK + A_VW
CONV_W = 4
GDN_CHUNK = 64
B_HEADS = 8
B_WIDTH = B_HEADS * HEAD_DIM
B_BRANCHES = ((128, 1), (512, 4), (2048, 16))
B_CACHE = 2048
C_WIDTH = 512
C_GROUP = 16
C_GROUPS = C_WIDTH // C_GROUP
C_STATE = 64
D_HEADS = 8
D_KV_HEADS = 2
D_GROUP = D_HEADS // D_KV_HEADS
D_WIDTH = D_HEADS * HEAD_DIM
D_KV_WIDTH = D_KV_HEADS * HEAD_DIM
D_WINDOW = 128
BAND = 128
MIX_WIDTH = A_VW + B_WIDTH + C_WIDTH + D_WIDTH
IN_SIZES = (A_CONV_CH, A_VW, A_HEADS, A_HEADS, 3 * B_WIDTH, C_WIDTH, D_WIDTH, D_KV_WIDTH, D_KV_WIDTH)
IN_COLS = sum(IN_SIZES)
PEER_HEADS = 8
PEER_HALF = 128
PEER_QDIM = 2 * PEER_HALF
N_KEYS = 128
N_EXPERTS = N_KEYS * N_KEYS
PEER_TOPK = 16
PEER_BLOCK = 128
DEEPNORM_ALPHA = (2 * DEPTH) ** 0.25
DEEPNORM_BETA = (8 * DEPTH) ** -0.25
ATTN_SCALE = HEAD_DIM ** -0.5
LN_EPS = 1e-5
RMS_EPS = 1e-6
F32 = jnp.float32

kernel_name = 'hybrid_gdn_dilated_s5_swa_peer_step'


def layer_norm(x, g, b):
    xf = x.astype(F32)
    mu = jnp.mean(xf, axis=-1, keepdims=True)
    var = jnp.mean(jnp.square(xf - mu), axis=-1, keepdims=True)
    return ((xf - mu) * lax.rsqrt(var + LN_EPS) * g.astype(F32) + b.astype(F32)).astype(x.dtype)


def l2_normalize(x):
    return x * lax.rsqrt(jnp.sum(jnp.square(x), axis=-1, keepdims=True) + RMS_EPS)


def split_offsets(sizes):
    offs, acc = [], 0
    for s in sizes[:-1]:
        acc += s
        offs.append(acc)
    return offs


def attend_stats(q, k, v, mask):
    s = jnp.einsum('...qhgd,...khd->...hgqk', q.astype(F32), k.astype(F32)) * ATTN_SCALE
    s = jnp.where(mask, s, -jnp.inf)
    m = jnp.max(s, axis=-1)
    p = jnp.exp(s - m[..., None])
    l = jnp.sum(p, axis=-1)
    acc = jnp.einsum('...hgqk,...khd->...qhgd', p, v.astype(F32))
    return acc, jnp.moveaxis(m, -1, -3), jnp.moveaxis(l, -1, -3)


def banded_stats(q, k, v, max_dist):
    Bsz, L = q.shape[:2]
    pad = (-L) % BAND
    padl = lambda t: jnp.pad(t, ((0, 0), (0, pad)) + ((0, 0),) * (t.ndim - 2))
    q, k, v = padl(q), padl(k), padl(v)
    nb = (L + pad) // BAND

    def two_blocks(t):
        cur = t.reshape((Bsz, nb, BAND) + t.shape[2:])
        prev = jnp.pad(cur, ((0, 0), (1, 0)) + ((0, 0),) * (cur.ndim - 2))[:, :-1]
        return jnp.concatenate([prev, cur], axis=2)

    qb = q.reshape((Bsz, nb, BAND) + q.shape[2:])
    qi = jnp.arange(BAND)[:, None]
    kj = jnp.arange(2 * BAND)[None, :]
    dist = BAND + qi - kj
    blk = jnp.arange(nb)[:, None, None]
    mask = (dist >= 0) & (dist <= max_dist) & ((blk > 0) | (kj >= BAND))
    acc, m, l = attend_stats(qb, two_blocks(k), two_blocks(v), mask[:, None, None])
    unblock = lambda t: t.reshape((Bsz, nb * BAND) + t.shape[3:])[:, :L]
    return unblock(acc), unblock(m), unblock(l)


def merge_by_denominator(stats):
    m_max = jnp.max(jnp.stack([m for _, m, _ in stats]), axis=0)
    num = None
    den = None
    for acc, m, l in stats:
        wgt = jnp.exp(m - m_max)
        num = acc * wgt[..., None] if num is None else num + acc * wgt[..., None]
        den = l * wgt if den is None else den + l * wgt
    return num / den[..., None]


def sink_normalise(acc, m, l, sinks):
    sk = sinks.astype(F32)
    mm = jnp.maximum(m, sk)
    scale = jnp.exp(m - mm)
    den = l * scale + jnp.exp(sk - mm)
    return acc * (scale / den)[..., None]


def gated_delta_rule(q, k, v, log_decay, beta, s0):
    Bsz, L, H, _ = q.shape
    C = GDN_CHUNK
    pad = (-L) % C
    n_chunks = (L + pad) // C

    def prep(t):
        t = jnp.pad(t, ((0, 0), (0, pad)) + ((0, 0),) * (t.ndim - 2))
        t = t.reshape((Bsz, n_chunks, C) + t.shape[2:])
        return jnp.moveaxis(t, 3, 2)

    q, k, v, g, beta = prep(q), prep(k), prep(v), prep(log_decay), prep(beta)
    g = jnp.cumsum(g, axis=-1)
    idx = jnp.arange(C)
    causal = idx[:, None] >= idx[None, :]
    strict = idx[:, None] > idx[None, :]
    decay = jnp.exp(jnp.where(causal, g[..., :, None] - g[..., None, :], -jnp.inf))
    kb = k * beta[..., None]
    lmat = jnp.where(strict, jnp.einsum('bnhid,bnhjd->bnhij', kb, k), 0.0) * decay
    rhs = jnp.concatenate([v * beta[..., None], kb * jnp.exp(g)[..., None]], axis=-1)
    eye = jnp.eye(C, dtype=F32)
    sol = lax.linalg.triangular_solve(eye + lmat, rhs, left_side=True, lower=True, unit_diagonal=True)
    u, w = sol[..., :A_DV], sol[..., A_DV:]
    qk = jnp.einsum('bnhid,bnhjd->bnhij', q, k) * decay
    q_dec = q * jnp.exp(g)[..., None]
    k_dec = k * jnp.exp(g[..., -1:] - g)[..., None]
    g_last = jnp.exp(g[..., -1])

    def step(s, xs):
        u_n, w_n, qk_n, qd_n, kd_n, gl_n = xs
        v_new = u_n - jnp.einsum('bhck,bhkv->bhcv', w_n, s)
        o = jnp.einsum('bhck,bhkv->bhcv', qd_n, s) + jnp.einsum('bhij,bhjv->bhiv', qk_n, v_new)
        s = s * gl_n[..., None, None] + jnp.einsum('bhck,bhcv->bhkv', kd_n, v_new)
        return s, o

    xs = tuple(jnp.moveaxis(t, 1, 0) for t in (u, w, qk, q_dec, k_dec, g_last))
    s_fin, o = lax.scan(step, s0, xs)
    o = jnp.swapaxes(jnp.moveaxis(o, 0, 1), 2, 3).reshape(Bsz, n_chunks * C, H, A_DV)[:, :L]
    return o, s_fin


def gdn_mixer(qkv, z, alpha_in, beta_in, conv_w, a_log, dt_bias, norm_w, conv_buf, s0):
    Bsz, L, _ = qkv.shape
    if conv_buf is None:
        conv_buf = jnp.zeros((Bsz, CONV_W - 1, A_CONV_CH), qkv.dtype)
    ext = jnp.concatenate([conv_buf.astype(qkv.dtype), qkv], axis=1)
    conv = ext[:, 0:L].astype(F32) * conv_w[0].astype(F32)
    for j in range(1, CONV_W):
        conv = conv + ext[:, j:j + L].astype(F32) * conv_w[j].astype(F32)
    new_buf = ext[:, L:]
    q, k, v = jnp.split(jax.nn.silu(conv), [A_QK, 2 * A_QK], axis=-1)
    q = l2_normalize(q.reshape(Bsz, L, A_HEADS, A_DK)) * (A_DK ** -0.5)
    k = l2_normalize(k.reshape(Bsz, L, A_HEADS, A_DK))
    v = v.reshape(Bsz, L, A_HEADS, A_DV)
    beta = jax.nn.sigmoid(beta_in.astype(F32))
    log_decay = -jnp.exp(a_log.astype(F32)) * jax.nn.softplus(alpha_in.astype(F32) + dt_bias.astype(F32))
    if s0 is None:
        s0 = jnp.zeros((Bsz, A_HEADS, A_DK, A_DV), F32)
    o, s_new = gated_delta_rule(q, k, v, log_decay, beta, s0.astype(F32))
    o = o * lax.rsqrt(jnp.mean(jnp.square(o), axis=-1, keepdims=True) + RMS_EPS) * norm_w.astype(F32)
    o = o * jax.nn.silu(z.astype(F32).reshape(Bsz, L, A_HEADS, A_DV))
    return o.reshape(Bsz, L, A_VW), new_buf, s_new


def dilated_prompt(q, k, v):
    Bsz, L = q.shape[:2]
    stats = []
    for window, dil in B_BRANCHES:
        ls = L // dil

        def to_sub(t):
            t = t.reshape((Bsz, ls, dil) + t.shape[2:])
            return jnp.swapaxes(t, 1, 2).reshape((Bsz * dil, ls) + t.shape[3:])

        def from_sub(t):
            t = t.reshape((Bsz, dil, ls) + t.shape[2:])
            return jnp.swapaxes(t, 1, 2).reshape((Bsz, L) + t.shape[3:])

        acc, m, l = banded_stats(to_sub(q)[:, :, :, None], to_sub(k), to_sub(v), window // dil)
        stats.append((from_sub(acc), from_sub(m), from_sub(l)))
    return merge_by_denominator(stats)[:, :, :, 0]


def dilated_sample(q, k, v, k_buf, v_buf):
    T = q.shape[1]
    lw = k_buf.shape[1]
    kc = jnp.concatenate([k_buf.astype(k.dtype), k], axis=1)
    vc = jnp.concatenate([v_buf.astype(v.dtype), v], axis=1)
    qq = q[:, :, None, :, None, :]
    stats = []
    for window, dil in B_BRANCHES:
        idx = lw + jnp.arange(T)[:, None] - dil * jnp.arange(window // dil + 1)[None, :]
        valid = idx >= 0
        idx = jnp.maximum(idx, 0)
        acc, m, l = attend_stats(qq, kc[:, idx], vc[:, idx], valid[:, None, None, None, :])
        stats.append((acc[:, :, 0], m[:, :, 0], l[:, :, 0]))
    return merge_by_denominator(stats)[:, :, :, 0]


def ssm_combine(e1, e2):
    a1, b1 = e1
    a2, b2 = e2
    return a1 * a2, a2 * b1 + b2


def s5_mixer(u, lam_re, lam_im, log_step, b_re, b_im, c_re, c_im, d_skip, glu_w, glu_b, h0):
    Bsz, L, _ = u.shape
    lam = lax.complex(lam_re.astype(F32), lam_im.astype(F32))
    lam_bar = jnp.exp(lam * jnp.exp(log_step.astype(F32))[:, None])
    b_bar = ((lam_bar - 1.0) / lam)[:, :, None] * lax.complex(b_re.astype(F32), b_im.astype(F32))
    c_mat = lax.complex(c_re.astype(F32), c_im.astype(F32))
    uf = u.astype(F32)
    bu = jnp.einsum('gpc,blgc->blgp', b_bar, uf.reshape(Bsz, L, C_GROUPS, C_GROUP).astype(jnp.complex64))
    if h0 is not None:
        bu = bu.at[:, 0].add(lam_bar * h0)
    a = jnp.broadcast_to(lam_bar, bu.shape)
    _, h = lax.associative_scan(ssm_combine, (a, bu), axis=1)
    y = jnp.einsum('gcp,blgp->blgc', c_mat, h).real.reshape(Bsz, L, C_WIDTH) + d_skip.astype(F32) * uf
    val, gate = jnp.split(y @ glu_w.astype(F32) + glu_b.astype(F32), 2, axis=-1)
    return val * jax.nn.sigmoid(gate), h[:, -1]


def peer_ffn(x, w_query, sub_keys, u_tab, v_tab):
    Bsz, L, D = x.shape
    t = x.reshape(Bsz * L, D)
    n = t.shape[0]
    q = (t @ w_query).reshape(n, PEER_HEADS, 2, PEER_HALF)
    s = jnp.einsum('nhcd,hckd->nhck', q.astype(F32), sub_keys.astype(F32))
    v1, i1 = lax.top_k(s[:, :, 0], PEER_TOPK)
    v2, i2 = lax.top_k(s[:, :, 1], PEER_TOPK)
    cand = (v1[..., :, None] + v2[..., None, :]).reshape(n, PEER_HEADS, PEER_TOPK * PEER_TOPK)
    sc, ci = lax.top_k(cand, PEER_TOPK)
    expert = (jnp.take_along_axis(i1, ci // PEER_TOPK, axis=-1) * N_KEYS
              + jnp.take_along_axis(i2, ci % PEER_TOPK, axis=-1))
    gate = jax.nn.softmax(sc, axis=-1)
    pad = (-n) % PEER_BLOCK
    tb = jnp.pad(t, ((0, pad), (0, 0))).reshape(-1, PEER_BLOCK, D)
    eb = jnp.pad(expert, ((0, pad), (0, 0), (0, 0))).reshape(-1, PEER_BLOCK, PEER_HEADS, PEER_TOPK)
    gb = jnp.pad(gate, ((0, pad), (0, 0), (0, 0))).reshape(-1, PEER_BLOCK, PEER_HEADS, PEER_TOPK)

    def expert_block(args):
        tt, ee, gg = args
        h = jnp.einsum('tjkd,td->tjk', u_tab[ee], tt)
        coef = gg * jax.nn.gelu(h.astype(F32), approximate=False)
        return jnp.einsum('tjk,tjkd->td', coef.astype(tt.dtype), v_tab[ee])

    out = lax.map(expert_block, (tb, eb, gb)).reshape(-1, D)[:n]
    return out.reshape(Bsz, L, D)


def layer_forward(x, w, cache):
    Bsz, L, _ = x.shape
    fresh = cache is None
    proj = jnp.einsum('bld,dc->blc', x, w['w_in'])
    a_qkv, a_z, a_alpha, a_beta, b_qkv, c_u, d_q, d_kp, d_vp = jnp.split(proj, split_offsets(IN_SIZES), axis=-1)

    o_a, a_conv_new, a_state_new = gdn_mixer(
        a_qkv, a_z, a_alpha, a_beta, w['a_conv_w'], w['a_log'], w['a_dt_bias'], w['a_norm_w'],
        None if fresh else cache['a_conv'], None if fresh else cache['a_state'])

    b_q, b_k, b_v = [t.reshape(Bsz, L, B_HEADS, HEAD_DIM) for t in jnp.split(b_qkv, 3, axis=-1)]
    if fresh:
        o_b = dilated_prompt(b_q, b_k, b_v)
        keep = min(B_CACHE, L)
        b_k_new, b_v_new = b_k[:, L - keep:], b_v[:, L - keep:]
    else:
        o_b = dilated_sample(b_q, b_k, b_v, cache['b_k'], cache['b_v'])
        b_k_new, b_v_new = b_k, b_v

    h0 = None if fresh else lax.complex(cache['c_re'].astype(F32), cache['c_im'].astype(F32))
    o_c, c_state = s5_mixer(c_u, w['c_lambda_re'], w['c_lambda_im'], w['c_log_step'], w['c_b_re'], w['c_b_im'],
                            w['c_c_re'], w['c_c_im'], w['c_d'], w['c_glu_w'], w['c_glu_b'], h0)

    d_q = d_q.reshape(Bsz, L, D_KV_HEADS, D_GROUP, HEAD_DIM)
    d_k = d_kp.reshape(Bsz, L, D_KV_HEADS, HEAD_DIM)
    d_v = d_vp.reshape(Bsz, L, D_KV_HEADS, HEAD_DIM)
    if fresh:
        acc, m, l = banded_stats(d_q, d_k, d_v, D_WINDOW - 1)
        keep = min(D_WINDOW, L)
        d_k_new, d_v_new = d_k[:, L - keep:], d_v[:, L - keep:]
    else:
        lw = cache['d_k'].shape[1]
        kc = jnp.concatenate([cache['d_k'].astype(d_k.dtype), d_k], axis=1)
        vc = jnp.concatenate([cache['d_v'].astype(d_v.dtype), d_v], axis=1)
        dist = lw + jnp.arange(L)[:, None] - jnp.arange(lw + L)[None, :]
        acc, m, l = attend_stats(d_q, kc, vc, (dist >= 0) & (dist < D_WINDOW))
        d_k_new, d_v_new = d_k, d_v
    o_d = sink_normalise(acc, m, l, w['d_sinks'])

    mix = jnp.concatenate([o_a.astype(x.dtype), o_b.reshape(Bsz, L, B_WIDTH).astype(x.dtype),
                           o_c.astype(x.dtype), o_d.reshape(Bsz, L, D_WIDTH).astype(x.dtype)], axis=-1)
    x = layer_norm(DEEPNORM_ALPHA * x + mix @ w['w_out'], w['ln1_g'], w['ln1_b'])
    x = layer_norm(DEEPNORM_ALPHA * x + peer_ffn(x, w['peer_wq'], w['peer_sub_keys'], w['peer_u'], w['peer_v']),
                   w['ln2_g'], w['ln2_b'])
    return x, (a_conv_new, a_state_new, b_k_new, b_v_new, jnp.real(c_state), jnp.imag(c_state), d_k_new, d_v_new)


def setup_inputs(seed: int = 0) -> dict:
    key = jax.random.key(seed)
    keys = iter(jax.random.split(key, 48))

    def normal(shape, scale):
        return jax.random.normal(next(keys), shape, F32) * scale

    def uniform(shape, lo, hi):
        return jax.random.uniform(next(keys), shape, F32, lo, hi)

    lb = min(B_CACHE, PAST_LEN)
    ld = min(D_WINDOW, PAST_LEN)
    beta = DEEPNORM_BETA
    dt = jnp.exp(uniform((DEPTH, A_HEADS), math.log(1e-3), math.log(1e-1)))
    return {
        'x_prompt': normal((BATCH, SEQ, D_MODEL), 1.0),
        'x_sample': normal((DEC_BATCH, DEC_SEQ, D_MODEL), 1.0),
        'cache_a_conv': normal((DEPTH, DEC_BATCH, CONV_W - 1, A_CONV_CH), 1.0),
        'state_a': normal((DEPTH, DEC_BATCH, A_HEADS, A_DK, A_DV), 0.1),
        'cache_b_k': normal((DEPTH, DEC_BATCH, lb, B_HEADS, HEAD_DIM), 1.0),
        'cache_b_v': normal((DEPTH, DEC_BATCH, lb, B_HEADS, HEAD_DIM), 1.0),
        'state_c_re': normal((DEPTH, DEC_BATCH, C_GROUPS, C_STATE), 0.3),
        'state_c_im': normal((DEPTH, DEC_BATCH, C_GROUPS, C_STATE), 0.3),
        'cache_d_k': normal((DEPTH, DEC_BATCH, ld, D_KV_HEADS, HEAD_DIM), 1.0),
        'cache_d_v': normal((DEPTH, DEC_BATCH, ld, D_KV_HEADS, HEAD_DIM), 1.0),
        'w_in': normal((DEPTH, D_MODEL, IN_COLS), D_MODEL ** -0.5),
        'a_conv_w': normal((DEPTH, CONV_W, A_CONV_CH), CONV_W ** -0.5),
        'a_log': jnp.log(uniform((DEPTH, A_HEADS), 1.0, 16.0)),
        'a_dt_bias': dt + jnp.log(-jnp.expm1(-dt)),
        'a_norm_w': 1.0 + normal((DEPTH, A_DV), 0.02),
        'c_lambda_re': -0.5 + normal((DEPTH, C_GROUPS, C_STATE), 0.01),
        'c_lambda_im': math.pi * jnp.arange(C_STATE, dtype=F32) + normal((DEPTH, C_GROUPS, C_STATE), 0.01),
        'c_log_step': uniform((DEPTH, C_GROUPS), math.log(1e-3), math.log(1e-1)),
        'c_b_re': normal((DEPTH, C_GROUPS, C_STATE, C_GROUP), (2 * C_GROUP) ** -0.5),
        'c_b_im': normal((DEPTH, C_GROUPS, C_STATE, C_GROUP), (2 * C_GROUP) ** -0.5),
        'c_c_re': normal((DEPTH, C_GROUPS, C_GROUP, C_STATE), (2 * C_STATE) ** -0.5),
        'c_c_im': normal((DEPTH, C_GROUPS, C_GROUP, C_STATE), (2 * C_STATE) ** -0.5),
        'c_d': normal((DEPTH, C_WIDTH), 1.0),
        'c_glu_w': normal((DEPTH, C_WIDTH, 2 * C_WIDTH), C_WIDTH ** -0.5),
        'c_glu_b': normal((DEPTH, 2 * C_WIDTH), 0.02),
        'd_sinks': normal((DEPTH, D_KV_HEADS, D_GROUP), 0.5),
        'w_out': normal((DEPTH, MIX_WIDTH, D_MODEL), beta * MIX_WIDTH ** -0.5),
        'ln1_g': 1.0 + normal((DEPTH, D_MODEL), 0.02),
        'ln1_b': normal((DEPTH, D_MODEL), 0.02),
        'peer_wq': normal((DEPTH, D_MODEL, PEER_HEADS * PEER_QDIM), D_MODEL ** -0.5),
        'peer_sub_keys': normal((DEPTH, PEER_HEADS, 2, N_KEYS, PEER_HALF), PEER_HALF ** -0.5),
        'peer_u': normal((DEPTH, N_EXPERTS, D_MODEL), D_MODEL ** -0.5),
        'peer_v': normal((DEPTH, N_EXPERTS, D_MODEL), beta * 0.5),
        'ln2_g': 1.0 + normal((DEPTH, D_MODEL), 0.02),
        'ln2_b': normal((DEPTH, D_MODEL), 0.02),
    }


def reference(x_prompt, x_sample, cache_a_conv, state_a, cache_b_k, cache_b_v, state_c_re, state_c_im,
              cache_d_k, cache_d_v, w_in, a_conv_w, a_log, a_dt_bias, a_norm_w, c_lambda_re, c_lambda_im,
              c_log_step, c_b_re, c_b_im, c_c_re, c_c_im, c_d, c_glu_w, c_glu_b, d_sinks, w_out, ln1_g, ln1_b,
              peer_wq, peer_sub_keys, peer_u, peer_v, ln2_g, ln2_b):
    xp = x_prompt
    xs = x_sample
    p_states = []
    s_states = []
    for l in range(DEPTH):
        w = {
            'w_in': w_in[l], 'a_conv_w': a_conv_w[l], 'a_log': a_log[l], 'a_dt_bias': a_dt_bias[l],
            'a_norm_w': a_norm_w[l], 'c_lambda_re': c_lambda_re[l], 'c_lambda_im': c_lambda_im[l],
            'c_log_step': c_log_step[l], 'c_b_re': c_b_re[l], 'c_b_im': c_b_im[l], 'c_c_re': c_c_re[l],
            'c_c_im': c_c_im[l], 'c_d': c_d[l], 'c_glu_w': c_glu_w[l], 'c_glu_b': c_glu_b[l],
            'd_sinks': d_sinks[l], 'w_out': w_out[l], 'ln1_g': ln1_g[l], 'ln1_b': ln1_b[l],
            'peer_wq': peer_wq[l], 'peer_sub_keys': peer_sub_keys[l], 'peer_u': peer_u[l], 'peer_v': peer_v[l],
            'ln2_g': ln2_g[l], 'ln2_b': ln2_b[l],
        }
        cache = {
            'a_conv': cache_a_conv[l], 'a_state': state_a[l], 'b_k': cache_b_k[l], 'b_v': cache_b_v[l],
            'c_re': state_c_re[l], 'c_im': state_c_im[l], 'd_k': cache_d_k[l], 'd_v': cache_d_v[l],
        }
        xp, st_p = layer_forward(xp, w, None)
        xs, st_s = layer_forward(xs, w, cache)
        p_states.append(st_p)
        s_states.append(st_s)
    p_a_conv, p_a_state, p_b_k, p_b_v, p_c_re, p_c_im, p_d_k, p_d_v = [
        jnp.stack([st[i] for st in p_states]) for i in range(8)]
    s_a_conv, s_a_state, s_b_k, s_b_v, s_c_re, s_c_im, s_d_k, s_d_v = [
        jnp.stack([st[i] for st in s_states]) for i in range(8)]
    return (xp, xs, p_a_conv, p_a_state, p_b_k, p_b_v, p_c_re, p_c_im, p_d_k, p_d_v,
            s_a_conv, s_a_state, s_b_k, s_b_v, s_c_re, s_c_im, s_d_k, s_d_v)
```

```python
import numpy as np
from contextlib import ExitStack
import concourse.bass as bass
import concourse.mybir as mybir
from concourse.bass_types import AP
from concourse.bass_utils import run_bass_kernel_spmd

F32 = mybir.dt.float32
BF16 = mybir.dt.bfloat16
I32 = mybir.dt.int32
U32 = mybir.dt.uint32
AF = mybir.ActivationFunctionType
ALU = mybir.AluOpType
AX = mybir.AxisListType

D = 2048
INC = 4872
NEG = -1.0e30


class Buf:
    __slots__ = ("t", "w", "r", "name")

    def __init__(self, t, name=""):
        self.t = t
        self.w = None
        self.r = []
        self.name = name

    def __getitem__(self, idx):
        return self.t[idx]


class Sub:
    __slots__ = ("t", "p", "name")

    def __init__(self, t, parent, name=""):
        self.t = t
        self.p = parent
        self.name = name

    @property
    def w(self):
        return self.p.w

    @w.setter
    def w(self, v):
        self.p.w = v

    @property
    def r(self):
        return self.p.r

    @r.setter
    def r(self, v):
        self.p.r = v

    def __getitem__(self, idx):
        return self.t[idx]


class Sched:
    NDMA = 4

    def __init__(self, nc):
        self.nc = nc
        self.eng = {"pe": nc.tensor, "dve": nc.vector, "act": nc.scalar, "pool": nc.gpsimd, "sp": nc.sync}
        self.sem = {}
        self.cnt = {}
        for k in ("pe", "dve", "act", "pool"):
            self.sem[k] = nc.alloc_semaphore("sem_" + k)
            self.cnt[k] = 0
        self.dq = {}
        self.nslot = {"sp": 4, "pool": 8, "act": 4}
        for q in ("sp", "pool", "act"):
            self.dq[q] = 0
            for s in range(self.nslot[q]):
                k = ("dma", q, s)
                self.sem[k] = nc.alloc_semaphore("sem_dma_%s_%d" % (q, s))
                self.cnt[k] = 0
        self.seen = {e: {} for e in self.eng}
        self.n_inst = 0

    def _wait(self, e, key, val):
        if val <= 0 or self.seen[e].get(key, 0) >= val:
            return
        self.eng[e].wait_ge(self.sem[key], val)
        self.seen[e][key] = val
        self.n_inst += 1

    def _deps(self, e, reads, writes, skip_same=None):
        deps = {}
        for b in reads:
            if b.w is not None:
                k, v = b.w
                deps[k] = max(deps.get(k, 0), v)
        for b in writes:
            if b.w is not None:
                k, v = b.w
                deps[k] = max(deps.get(k, 0), v)
            for (k, v) in b.r:
                deps[k] = max(deps.get(k, 0), v)
        for k, v in deps.items():
            if skip_same is not None and k == skip_same:
                continue
            self._wait(e, k, v)

    def _mark(self, key, val, reads, writes):
        for b in writes:
            b.w = (key, val)
            b.r = []
        wroots = [getattr(b, "p", b) for b in writes]
        for b in reads:
            if any(getattr(b, "p", b) is wr for wr in wroots):
                continue
            b.r = [(k, v) for (k, v) in b.r if k != key] + [(key, val)]

    def op(self, e, fn, reads=(), writes=()):
        ex = [b for b in reads if isinstance(b, Sub)]
        if ex:
            reads = [b for b in reads if not isinstance(b, Sub)]
            writes = list(writes) + ex
        self._deps(e, reads, writes, skip_same=("pe" if e == "pe" else None))
        ins = fn(self.eng[e])
        self.cnt[e] += 1
        ins.then_inc(self.sem[e], 1)
        self._mark(e, self.cnt[e], reads, writes)
        self.n_inst += 1
        return ins

    def dma(self, out_ap, in_ap, reads=(), writes=(), q="sp", fn=None, **kw):
        slot = self.dq[q] % self.nslot[q]
        self.dq[q] += 1
        key = ("dma", q, slot)
        self._wait(q, key, self.cnt[key])
        self._deps(q, reads, writes)
        if fn is not None:
            ins = fn(self.eng[q])
        else:
            ins = self.eng[q].dma_start(out=out_ap, in_=in_ap, **kw)
        self.cnt[key] += 16
        ins.then_inc(self.sem[key], 16)
        self._mark(key, self.cnt[key], reads, writes)
        self.n_inst += 1
        return ins

    def barrier(self):
        keys = [k for k in self.cnt if self.cnt[k] > 0]
        for e in self.eng:
            for k in keys:
                self._wait(e, k, self.cnt[k])

    def finish(self):
        for k in self.cnt:
            if self.cnt[k] > 0:
                self._wait("sp", k, self.cnt[k])


class Cfg:
    def __init__(self, TP=2048, L=4, stages="abcdwp", debug=False, gdn_level=9, intra_stop=99, chain_steps=4):
        self.debug = debug
        self.TP = TP
        self.L = L
        self.NTP = TP // 128
        self.NT = self.NTP + 1
        self.NTOK = self.NT * 128
        self.stages = stages
        self.gdn_level = gdn_level
        self.intra_stop = intra_stop
        self.chain_steps = chain_steps


FQKV, FZ, FBQ, FBK, FCU, FDQ, FDK, FROWS = 0, 1536, 2048, 2560, 3072, 3584, 4096, 4224
TAB, TBK, TBV, TDK, TDV, TCOLS = 0, 8, 520, 1032, 1160, 1288
GROUPS = [
    (0, 2048, FQKV, None),
    (2048, 8, None, TAB),
    (2056, 512, FBQ, None),
    (2568, 512, FBK, TBK),
    (3080, 512, None, TBV),
    (3592, 512, FCU, None),
    (4104, 512, FDQ, None),
    (4616, 128, FDK, TDK),
    (4744, 128, None, TDV),
]


class Builder:
    def __init__(self, cfg):
        self.cfg = cfg
        self.nc = bass.Bass("TRN2", target_bir_lowering=False)
        self.S = Sched(self.nc)
        self.ins = {}
        self.outs = {}
        self._evac_i = 0
        self._uid = 0

    def din(self, name, shape, dt=F32):
        t = self.nc.dram_tensor(name, list(shape), dt, kind="ExternalInput")
        b = Buf(t.ap(), name)
        self.ins[name] = b
        return b

    def dout(self, name, shape, dt=F32):
        t = self.nc.dram_tensor(name, list(shape), dt, kind="ExternalOutput")
        b = Buf(t.ap(), name)
        self.outs[name] = b
        return b

    def dscr(self, name, shape, dt=F32):
        kind = "ExternalOutput" if self.cfg.debug else "Internal"
        t = self.nc.dram_tensor(name, list(shape), dt, kind=kind)
        return Buf(t.ap(), name)

    def sb(self, st, name, shape, dt=F32):
        self._uid += 1
        name = "sb%d_%s" % (self._uid, name)
        return Buf(st.enter_context(self.nc.sbuf_tensor(name, list(shape), dt)), name)

    def ps(self, st, name, shape, dt=F32):
        self._uid += 1
        name = "ps%d_%s" % (self._uid, name)
        return Buf(st.enter_context(self.nc.psum_tensor(name, list(shape), dt)), name)

    def evac(self, out_ap, in_ap, reads, writes, eng=None):
        if eng is None:
            eng = "act" if (self._evac_i % 2 == 0) else "dve"
            self._evac_i += 1
        if eng == "act":
            return self.S.op("act", lambda e: e.activation(out=out_ap, in_=in_ap, func=AF.Copy), reads, writes)
        return self.S.op(eng, lambda e: e.tensor_copy(out=out_ap, in_=in_ap), reads, writes)

    def declare(self):
        c = self.cfg
        L, NTOK, TP = c.L, c.NTOK, c.TP
        self.xin = self.din("xin", [NTOK, D])
        self.ident = self.din("ident", [128, 128])
        self.w_in = self.din("w_in", [L, D, INC])
        self.y = self.dout("y", [NTOK, D])
        self.o_pbk = self.dout("p_b_k", [L, TP, 512])
        self.o_pbv = self.dout("p_b_v", [L, TP, 512])
        self.o_pdk = self.dout("p_d_k", [L, 128, 128])
        self.o_pdv = self.dout("p_d_v", [L, 128, 128])
        self.o_sbk = self.dout("s_b_k", [L, 4, 8, 512])
        self.o_sbv = self.dout("s_b_v", [L, 4, 8, 512])
        self.o_sdk = self.dout("s_d_k", [L, 4, 8, 128])
        self.o_sdv = self.dout("s_d_v", [L, 4, 8, 128])
        self.bmask = self.din("bmask", [16, 128, 512])
        self.bmask_s = self.din("bmask_s", [17, 128, 8])
        self.dmask = self.din("dmask", [5, 128, 512])
        self.dmask_s = self.din("dmask_s", [2, 128, 8])
        self.cache_b_kT = self.din("cache_b_kT", [L, 4, 8, 64, 2048])
        self.cache_b_v = self.din("cache_b_v", [L, 4, 2048, 512])
        self.cache_d_kT = self.din("cache_d_kT", [L, 4, 2, 64, 128])
        self.cache_d_v = self.din("cache_d_v", [L, 4, 128, 128])
        self.d_sinks = self.din("d_sinks", [L, 8])
        self.rowmask = self.din("rowmask", [128, 4])
        self.gmasks = self.din("gmasks", [3, 128, 128])
        self.valid = self.din("valid", [128, c.NT])
        self.a_cw = self.din("a_cw", [L, 128, 12, 4])
        self.a_cc = self.din("a_cc", [L, 128, 12, 4, 3])
        self.a_par = self.din("a_par", [L, 8])
        self.a_nw = self.din("a_nw", [L, 128])
        self.state_a = self.din("state_a", [L, 4, 4, 128, 128])
        self.o_paconv = self.dout("p_a_convT", [L, 1536, 3])
        self.o_saconv = self.dout("s_a_convT", [L, 4, 1536, 3])
        self.o_pastate = self.dout("p_a_state", [L, 4, 128, 128])
        self.o_sastate = self.dout("s_a_state", [L, 4, 4, 128, 128])
        self.c_lre = self.din("c_lre", [L, 128, 16])
        self.c_lim = self.din("c_lim", [L, 128, 16])
        self.c_lst = self.din("c_lst", [L, 128, 16])
        self.c_bsd_re = self.din("c_bsd_re", [L, 128, 16, 32])
        self.c_bsd_im = self.din("c_bsd_im", [L, 128, 16, 32])
        self.c_csd_re = self.din("c_csd_re", [L, 128, 16, 32])
        self.c_csd_im = self.din("c_csd_im", [L, 128, 16, 32])
        self.c_dT = self.din("c_dT", [L, 128, 4])
        self.c_glu_w = self.din("c_glu_w", [L, 512, 1024])
        self.c_glu_bT = self.din("c_glu_bT", [L, 128, 8])
        self.c_h0re = self.din("c_h0re", [L, 128, 16, 4])
        self.c_h0im = self.din("c_h0im", [L, 128, 16, 4])
        self.o_pcre = self.dout("p_c_re", [L, 128, 16])
        self.o_pcim = self.dout("p_c_im", [L, 128, 16])
        self.o_scre = self.dout("s_c_re", [L, 128, 16, 4])
        self.o_scim = self.dout("s_c_im", [L, 128, 16, 4])
        self.w_out = self.din("w_out", [L, D, D])
        self.ln1_g = self.din("ln1_g", [L, D])
        self.ln1_b = self.din("ln1_b", [L, D])
        self.ln2_g = self.din("ln2_g", [L, D])
        self.ln2_b = self.din("ln2_b", [L, D])
        self.peer_wq = self.din("peer_wq", [L, D, D])
        self.peer_keysT = self.din("peer_keysT", [L, 128, 16, 128])
        self.peer_u = [self.din("peer_u%d" % i, [16384, D]) for i in range(L)]
        self.peer_v = [self.din("peer_v%d" % i, [16384, D]) for i in range(L)]
        self.iota16 = self.din("iota16", [128, 16])
        self.sc_d = self.dscr("sc_d", [NTOK, D])
        self.mixT = self.dscr("mixT", [2048, NTOK])
        self.featT = self.dscr("featT", [FROWS, NTOK])
        self.tokM = self.dscr("tokM", [NTOK, TCOLS])
        self.xres = self.dscr("xres", [NTOK, D])

    def transpose_tile_to_xT(self, st, xt_tile, tt, ptr, xT):
        S = self.S
        for g in range(4):
            p = ptr[g % 2]
            for j in range(4):
                kt = g * 4 + j
                S.op("pe", lambda e, kt=kt, j=j, p=p: e.transpose(out=p[:, j * 128:(j + 1) * 128],
                                                                   in_=xt_tile[:, kt * 128:(kt + 1) * 128],
                                                                   identity=self.identb[:]),
                     reads=[xt_tile, self.identb], writes=[p])
            o = xT[:, g * 4:(g + 1) * 4, tt * 128:(tt + 1) * 128]
            i = p[:, :].rearrange("p (a b) -> p a b", a=4)
            self.evac(o, i, [p], [xT])

    def phase0(self):
        c = self.cfg
        S = self.S
        with ExitStack() as st:
            xt = [self.sb(st, "p0_xt%d" % i, [128, D]) for i in range(2)]
            ptr = [self.ps(st, "p0_ptr%d" % i, [128, 512]) for i in range(2)]
            zt = self.sb(st, "p0_zero", [128, 128])
            S.op("pool", lambda e: e.memset(zt[:, :], 0.0), [], [zt])
            for kt in range(16):
                S.dma(self.mixT[kt * 128:(kt + 1) * 128, c.TP:c.NTOK], zt[:, :], reads=[zt], writes=[self.mixT], q="pool")
            for tt in range(c.NT):
                b = xt[tt % 2]
                S.dma(b[:], self.xin[tt * 128:(tt + 1) * 128, :], reads=[self.xin], writes=[b])
                S.dma(self.xres[tt * 128:(tt + 1) * 128, :], b[:], reads=[b], writes=[self.xres], q="act")
                self.transpose_tile_to_xT(st, b, tt, ptr, self.xT)
        S.barrier()

    def phase_proj(self, l):
        c = self.cfg
        S = self.S
        NTOK = c.NTOK
        tblocks = []
        t0 = 0
        while t0 < NTOK:
            n = min(512, NTOK - t0)
            tblocks.append((t0, n))
            t0 += n
        with ExitStack() as st:
            wb = [self.sb(st, "pj_w%d" % i, [128, 16, 512], BF16) for i in range(2)]
            stg = [self.sb(st, "pj_stg%d" % i, [128, NTOK]) for i in range(2)]
            stm = [self.sb(st, "pj_stm%d" % i, [128, 512]) for i in range(2)]
            pp = [self.ps(st, "pj_ps%d" % i, [128, 512]) for i in range(4)]
            wi = 0
            si = 0
            mi = 0
            pi = 0
            for (c0, ncols, frow, tcol) in GROUPS:
                for b0 in range(0, ncols, 512):
                    nb = min(512, ncols - b0)
                    w = wb[wi % 2]
                    wi += 1
                    src = self.w_in[l, :, c0 + b0:c0 + b0 + nb].rearrange("(kt p) c -> p kt c", p=128)
                    S.dma(w[:, :, 0:nb], src, reads=[self.w_in], writes=[w], q="pool")
                    if frow is not None:
                        for ct in range(0, nb, 128):
                            m = min(128, nb - ct)
                            sg = stg[si % 2]
                            si += 1
                            for (tb0, tn) in tblocks:
                                p = pp[pi % 4]
                                pi += 1
                                for kt in range(16):
                                    S.op("pe", lambda e, p=p, w=w, kt=kt, ct=ct, m=m, tb0=tb0, tn=tn: e.matmul(
                                        p[0:m, 0:tn], lhsT=w[:, kt, ct:ct + m], rhs=self.xT[:, kt, tb0:tb0 + tn],
                                        start=(kt == 0), stop=(kt == 15)), reads=[w, self.xT], writes=[p])
                                self.evac(sg[0:m, tb0:tb0 + tn], p[0:m, 0:tn], [p], [sg])
                            r0 = frow + b0 + ct
                            S.dma(self.featT[r0:r0 + m, :], sg[0:m, :], reads=[sg], writes=[self.featT])
                    if tcol is not None:
                        for tt in range(c.NT):
                            p = pp[pi % 4]
                            pi += 1
                            for kt in range(16):
                                S.op("pe", lambda e, p=p, w=w, kt=kt, tt=tt, nb=nb: e.matmul(
                                    p[:, 0:nb], lhsT=self.xT[:, kt, tt * 128:(tt + 1) * 128], rhs=w[:, kt, 0:nb],
                                    start=(kt == 0), stop=(kt == 15)), reads=[w, self.xT], writes=[p])
                            sm = stm[mi % 2]
                            mi += 1
                            self.evac(sm[:, 0:nb], p[:, 0:nb], [p], [sm])
                            S.dma(self.tokM[tt * 128:(tt + 1) * 128, tcol + b0:tcol + b0 + nb], sm[:, 0:nb],
                                  reads=[sm], writes=[self.tokM], q="act")
        S.barrier()
        TP = c.TP
        S.dma(self.o_pbk[l, :, :], self.tokM[0:TP, TBK:TBK + 512], reads=[self.tokM], writes=[self.o_pbk])
        S.dma(self.o_pbv[l, :, :], self.tokM[0:TP, TBV:TBV + 512], reads=[self.tokM], writes=[self.o_pbv])
        S.dma(self.o_pdk[l, :, :], self.tokM[TP - 128:TP, TDK:TDK + 128], reads=[self.tokM], writes=[self.o_pdk])
        S.dma(self.o_pdv[l, :, :], self.tokM[TP - 128:TP, TDV:TDV + 128], reads=[self.tokM], writes=[self.o_pdv])
        for s in range(4):
            r0 = TP + 32 * s
            S.dma(self.o_sbk[l, s, :, :], self.tokM[r0:r0 + 8, TBK:TBK + 512], reads=[self.tokM], writes=[self.o_sbk])
            S.dma(self.o_sbv[l, s, :, :], self.tokM[r0:r0 + 8, TBV:TBV + 512], reads=[self.tokM], writes=[self.o_sbv])
            S.dma(self.o_sdk[l, s, :, :], self.tokM[r0:r0 + 8, TDK:TDK + 128], reads=[self.tokM], writes=[self.o_sdk])
            S.dma(self.o_sdv[l, s, :, :], self.tokM[r0:r0 + 8, TDV:TDV + 128], reads=[self.tokM], writes=[self.o_sdv])

    def attn_core(self, qT_ap, q_reads, nq, keys, num_ps, den_ps, pr0, scale, W):
        S = self.S
        n = len(keys)
        for i, (kT_ap, nk, v_ap, mask_ap, rd) in enumerate(keys):
            sp = W["sp"][self._ai % 2]
            pe_ = W["pe"][self._ai % 2]
            pm = W["pm"][self._ai % 2]
            self._ai += 1
            S.op("pe", lambda e: e.matmul(sp[0:nk, 0:nq], lhsT=kT_ap, rhs=qT_ap, start=True, stop=True),
                 reads=list(rd) + list(q_reads), writes=[sp])
            S.op("act", lambda e: e.activation(out=pe_[0:nk, 0:nq], in_=sp[0:nk, 0:nq], func=AF.Exp, scale=scale),
                 reads=[sp], writes=[pe_])
            meng = "dve" if (self._ai % 2 == 0) else "pool"
            S.op(meng, lambda e: e.tensor_tensor(out=pm[0:nk, 0:nq], in0=pe_[0:nk, 0:nq], in1=mask_ap, op=ALU.mult),
                 reads=[pe_, W["maskbuf"]], writes=[pm])
            S.op("pe", lambda e: e.matmul(num_ps[pr0:pr0 + 64, 0:nq], lhsT=v_ap, rhs=pm[0:nk, 0:nq],
                                          start=(i == 0), stop=(i == n - 1)),
                 reads=list(rd) + [pm], writes=[num_ps])
            S.op("pe", lambda e: e.matmul(den_ps[pr0:pr0 + 64, 0:nq], lhsT=W["ones"][0:nk, 0:64], rhs=pm[0:nk, 0:nq],
                                          start=(i == 0), stop=(i == n - 1)),
                 reads=[W["ones"], pm], writes=[den_ps])

    def attn_finish(self, num_ps, den_ps, pr0, nq, sink_ap, sink_reads, W, dst_ap, dst_buf):
        S = self.S
        rc = W["rc"][self._fi % 2]
        og = W["og"][self._fi % 2]
        self._fi += 1
        if sink_ap is not None:
            S.op("dve", lambda e: e.tensor_scalar(out=rc[pr0:pr0 + 64, 0:nq], in0=den_ps[pr0:pr0 + 64, 0:nq],
                                                  scalar1=sink_ap, scalar2=None, op0=ALU.add),
                 reads=[den_ps] + list(sink_reads), writes=[rc])
            S.op("dve", lambda e: e.reciprocal(out=rc[pr0:pr0 + 64, 0:nq], in_=rc[pr0:pr0 + 64, 0:nq]),
                 reads=[rc], writes=[rc])
        else:
            S.op("dve", lambda e: e.reciprocal(out=rc[pr0:pr0 + 64, 0:nq], in_=den_ps[pr0:pr0 + 64, 0:nq]),
                 reads=[den_ps], writes=[rc])
        S.op("dve", lambda e: e.tensor_tensor(out=og[pr0:pr0 + 64, 0:nq], in0=num_ps[pr0:pr0 + 64, 0:nq],
                                              in1=rc[pr0:pr0 + 64, 0:nq], op=ALU.mult),
             reads=[num_ps, rc], writes=[og])
        S.dma(dst_ap, og[pr0:pr0 + 64, 0:nq], reads=[og], writes=[dst_buf])

    def attn_work(self, st, pfx):
        W = {}
        W["sp"] = [self.ps(st, pfx + "sp%d" % i, [128, 512]) for i in range(2)]
        W["pe"] = [self.sb(st, pfx + "pe%d" % i, [128, 512], BF16) for i in range(2)]
        W["pm"] = [self.sb(st, pfx + "pm%d" % i, [128, 512], BF16) for i in range(2)]
        W["rc"] = [self.sb(st, pfx + "rc%d" % i, [128, 512]) for i in range(2)]
        W["og"] = [self.sb(st, pfx + "og%d" % i, [128, 512]) for i in range(2)]
        W["num"] = [self.ps(st, pfx + "num%d" % i, [128, 512]) for i in range(2)]
        W["den"] = [self.ps(st, pfx + "den%d" % i, [128, 512]) for i in range(2)]
        ones = self.sb(st, pfx + "ones", [128, 64], BF16)
        self.S.op("pool", lambda e: e.memset(ones[:], 1.0), writes=[ones])
        W["ones"] = ones
        return W

    def phase_attn_b(self, l):
        c = self.cfg
        S = self.S
        TP, NTP = c.TP, c.NTP
        self._ai = 0
        self._fi = 0
        with ExitStack() as st:
            W = self.attn_work(st, "ab_")
            qT = self.sb(st, "ab_qT", [64, 8, c.NTOK], BF16)
            kT = self.sb(st, "ab_kT", [64, 8, c.NTOK], BF16)
            V = self.sb(st, "ab_V", [128, c.NT, 512], BF16)
            mk = self.sb(st, "ab_mk", [128, 16, 512], BF16)
            mks = self.sb(st, "ab_mks", [128, 17, 8], BF16)
            W["maskbuf"] = mk
            for h in range(8):
                S.dma(qT[:, h, :], self.featT[FBQ + 64 * h:FBQ + 64 * h + 64, :], reads=[self.featT], writes=[qT], q="pool")
                S.dma(kT[:, h, :], self.featT[FBK + 64 * h:FBK + 64 * h + 64, :], reads=[self.featT], writes=[kT], q="pool")
            S.dma(V[:, :, :], self.tokM[:, TBV:TBV + 512].rearrange("(t p) c -> p t c", p=128),
                  reads=[self.tokM], writes=[V], q="pool")
            S.dma(mk[:, :, :], self.bmask[:, :, :].rearrange("m p q -> p m q"), reads=[self.bmask], writes=[mk], q="pool")
            S.dma(mks[:, :, :], self.bmask_s[:, :, :].rearrange("m p q -> p m q"), reads=[self.bmask_s], writes=[mks], q="pool")
            for qb in range(0, TP, 512):
                nq = min(512, TP - qb)
                for h in range(8):
                    num = W["num"][h % 2]
                    den = W["den"][h % 2]
                    keys = []
                    for kt in range(0, (qb + nq) // 128):
                        m = (qb - kt * 128) // 128
                        keys.append((kT[:, h, kt * 128:(kt + 1) * 128], 128, V[:, kt, 64 * h:64 * h + 64],
                                     mk[:, m + 3, 0:nq], [kT, V]))
                    self.attn_core(qT[:, h, qb:qb + nq], [qT], nq, keys, num, den, 0, 0.125, W)
                    r0 = 512 + 64 * h
                    self.attn_finish(num, den, 0, nq, None, [], W, self.mixT[r0:r0 + 64, qb:qb + nq], self.mixT)
            W["maskbuf"] = mks
            with ExitStack() as st2:
                Vn = self.sb(st2, "ab_Vn", [8, 4, 512], BF16)
                for s in range(4):
                    S.dma(Vn[:, s, :], self.tokM[TP + 32 * s:TP + 32 * s + 8, TBV:TBV + 512], reads=[self.tokM], writes=[Vn], q="pool")
                ckT = [self.sb(st2, "ab_ckT%d" % i, [64, 8, 2048], BF16) for i in range(2)]
                cV = [self.sb(st2, "ab_cV0", [128, 16, 512], BF16)] * 2
                for s in range(4):
                    ck = ckT[s % 2]
                    cv = cV[s % 2]
                    S.dma(ck[:, :, :], self.cache_b_kT[l, s].rearrange("h d r -> d h r"), reads=[self.cache_b_kT], writes=[ck], q="pool")
                    S.dma(cv[:, :, :], self.cache_b_v[l, s].rearrange("(t p) c -> p t c", p=128), reads=[self.cache_b_v], writes=[cv], q="pool")
                    q0 = TP + 32 * s
                    for h in range(8):
                        num = W["num"][h % 2]
                        den = W["den"][h % 2]
                        keys = []
                        for kt in range(16):
                            keys.append((ck[:, h, kt * 128:(kt + 1) * 128], 128, cv[:, kt, 64 * h:64 * h + 64],
                                         mks[:, kt, :], [ck, cv]))
                        keys.append((kT[:, h, q0:q0 + 8], 8, Vn[0:8, s, 64 * h:64 * h + 64],
                                     mks[0:8, 16, :], [kT, Vn]))
                        self.attn_core(qT[:, h, q0:q0 + 8], [qT], 8, keys, num, den, 0, 0.125, W)
                        r0 = 512 + 64 * h
                        self.attn_finish(num, den, 0, 8, None, [], W, self.mixT[r0:r0 + 64, q0:q0 + 8], self.mixT)
        S.barrier()

    def phase_attn_d(self, l):
        c = self.cfg
        S = self.S
        TP, NTP = c.TP, c.NTP
        self._ai = 0
        self._fi = 0
        with ExitStack() as st:
            W = self.attn_work(st, "ad_")
            qT = self.sb(st, "ad_qT", [64, 8, c.NTOK], BF16)
            kT = self.sb(st, "ad_kT", [64, 2, c.NTOK], BF16)
            V = self.sb(st, "ad_V", [128, c.NT, 128], BF16)
            Vn = self.sb(st, "ad_Vn", [8, 4, 128], BF16)
            mk = self.sb(st, "ad_mk", [128, 5, 512], BF16)
            mks = self.sb(st, "ad_mks", [128, 2, 8], BF16)
            ckT = self.sb(st, "ad_ckT", [64, 4, 2, 128], BF16)
            cV = self.sb(st, "ad_cV", [128, 4, 128], BF16)
            snk = self.sb(st, "ad_snk", [64, 8])
            esk = self.sb(st, "ad_esk", [64, 8])
            S.dma(snk[:, :], self.d_sinks[l:l + 1, :].to_broadcast([64, 8]), reads=[self.d_sinks], writes=[snk])
            S.op("act", lambda e: e.activation(out=esk[:, :], in_=snk[:, :], func=AF.Exp), reads=[snk], writes=[esk])
            for h in range(8):
                S.dma(qT[:, h, :], self.featT[FDQ + 64 * h:FDQ + 64 * h + 64, :], reads=[self.featT], writes=[qT], q="pool")
            for h in range(2):
                S.dma(kT[:, h, :], self.featT[FDK + 64 * h:FDK + 64 * h + 64, :], reads=[self.featT], writes=[kT], q="pool")
            S.dma(V[:, :, :], self.tokM[:, TDV:TDV + 128].rearrange("(t p) c -> p t c", p=128),
                  reads=[self.tokM], writes=[V], q="pool")
            for s in range(4):
                S.dma(Vn[:, s, :], self.tokM[TP + 32 * s:TP + 32 * s + 8, TDV:TDV + 128], reads=[self.tokM], writes=[Vn], q="pool")
                S.dma(ckT[:, s, :, :], self.cache_d_kT[l, s].rearrange("h d r -> d h r"), reads=[self.cache_d_kT], writes=[ckT], q="pool")
                S.dma(cV[:, s, :], self.cache_d_v[l, s], reads=[self.cache_d_v], writes=[cV], q="pool")
            S.dma(mk[:, :, :], self.dmask[:, :, :].rearrange("m p q -> p m q"), reads=[self.dmask], writes=[mk], q="pool")
            S.dma(mks[:, :, :], self.dmask_s[:, :, :].rearrange("m p q -> p m q"), reads=[self.dmask_s], writes=[mks], q="pool")
            W["maskbuf"] = mk
            for qb in range(0, TP, 512):
                nq = min(512, TP - qb)
                for h in range(8):
                    kv = h // 4
                    num = W["num"][h % 2]
                    den = W["den"][h % 2]
                    keys = []
                    for kt in range(max(0, qb // 128 - 1), (qb + nq) // 128):
                        m = (qb - kt * 128) // 128
                        keys.append((kT[:, kv, kt * 128:(kt + 1) * 128], 128, V[:, kt, 64 * kv:64 * kv + 64],
                                     mk[:, 1 - m, 0:nq], [kT, V]))
                    self.attn_core(qT[:, h, qb:qb + nq], [qT], nq, keys, num, den, 0, 0.125, W)
                    r0 = 1536 + 64 * h
                    self.attn_finish(num, den, 0, nq, esk[:, h:h + 1], [esk], W, self.mixT[r0:r0 + 64, qb:qb + nq], self.mixT)
            W["maskbuf"] = mks
            for s in range(4):
                q0 = TP + 32 * s
                for h in range(8):
                    kv = h // 4
                    num = W["num"][h % 2]
                    den = W["den"][h % 2]
                    keys = [(ckT[:, s, kv, :], 128, cV[:, s, 64 * kv:64 * kv + 64], mks[:, 0, :], [ckT, cV]),
                            (kT[:, kv, q0:q0 + 8], 8, Vn[0:8, s, 64 * kv:64 * kv + 64], mks[0:8, 1, :], [kT, Vn])]
                    self.attn_core(qT[:, h, q0:q0 + 8], [qT], 8, keys, num, den, 0, 0.125, W)
                    r0 = 1536 + 64 * h
                    self.attn_finish(num, den, 0, 8, esk[:, h:h + 1], [esk], W, self.mixT[r0:r0 + 64, q0:q0 + 8], self.mixT)
        S.barrier()

    def tt(self, e, out, in0, in1, op, reads, writes):
        return self.S.op(e, lambda en: en.tensor_tensor(out=out, in0=in0, in1=in1, op=op), reads, writes)

    def phase_s5(self, l):
        c = self.cfg
        S = self.S
        TP, NTOK = c.TP, c.NTOK
        LC = 512
        PI = float(np.pi)
        with ExitStack() as st:
            def sm(name, n=16):
                return self.sb(st, "c_" + name, [128, n])
            lre, lim, lst = sm("lre"), sm("lim"), sm("lst")
            S.dma(lre[:, :], self.c_lre[l], reads=[self.c_lre], writes=[lre])
            S.dma(lim[:, :], self.c_lim[l], reads=[self.c_lim], writes=[lim])
            S.dma(lst[:, :], self.c_lst[l], reads=[self.c_lst], writes=[lst])
            dl, ar, th, rr = sm("dl"), sm("ar"), sm("th"), sm("rr")
            S.op("act", lambda e: e.activation(out=dl[:, :], in_=lst[:, :], func=AF.Exp), [lst], [dl])
            self.tt("dve", ar[:, :], lre[:, :], dl[:, :], ALU.mult, [lre, dl], [ar])
            self.tt("dve", th[:, :], lim[:, :], dl[:, :], ALU.mult, [lim, dl], [th])
            S.op("act", lambda e: e.activation(out=rr[:, :], in_=ar[:, :], func=AF.Exp), [ar], [rr])
            tq, ki, kf, ph, fx = sm("tq"), self.sb(st, "c_ki", [128, 16], I32), sm("kf"), sm("ph"), sm("fx")
            S.op("dve", lambda e: e.tensor_scalar(out=tq[:, :], in0=th[:, :], scalar1=1.0 / (2 * PI), scalar2=64.5,
                                                  op0=ALU.mult, op1=ALU.add), [th], [tq])
            S.op("dve", lambda e: e.tensor_copy(out=ki[:, :], in_=tq[:, :]), [tq], [ki])
            S.op("dve", lambda e: e.tensor_copy(out=kf[:, :], in_=ki[:, :]), [ki], [kf])
            S.op("dve", lambda e: e.tensor_scalar(out=kf[:, :], in0=kf[:, :], scalar1=-64.0, scalar2=-2 * PI,
                                                  op0=ALU.add, op1=ALU.mult), [kf], [kf])
            self.tt("dve", ph[:, :], th[:, :], kf[:, :], ALU.add, [th, kf], [ph])
            S.op("dve", lambda e: e.tensor_scalar(out=fx[:, :], in0=ph[:, :], scalar1=-PI, scalar2=2 * PI,
                                                  op0=ALU.is_lt, op1=ALU.mult), [ph], [fx])
            self.tt("dve", ph[:, :], ph[:, :], fx[:, :], ALU.add, [ph, fx], [ph])
            S.op("dve", lambda e: e.tensor_scalar(out=fx[:, :], in0=ph[:, :], scalar1=PI, scalar2=-2 * PI,
                                                  op0=ALU.is_gt, op1=ALU.mult), [ph], [fx])
            self.tt("dve", ph[:, :], ph[:, :], fx[:, :], ALU.add, [ph, fx], [ph])
            S.op("dve", lambda e: e.tensor_scalar(out=ph[:, :], in0=ph[:, :], scalar1=PI, scalar2=-PI,
                                                  op0=ALU.min, op1=ALU.max), [ph], [ph])
            sn, cs, ab = sm("sn"), sm("cs"), sm("ab")
            hpi = sm("hpi", 1)
            S.op("dve", lambda e: e.memset(hpi[:, :], PI / 2), [], [hpi])
            S.op("act", lambda e: e.activation(out=sn[:, :], in_=ph[:, :], func=AF.Sin), [ph], [sn])
            S.op("act", lambda e: e.activation(out=ab[:, :], in_=ph[:, :], func=AF.Abs), [ph], [ab])
            S.op("act", lambda e: e.activation(out=cs[:, :], in_=ab[:, :], func=AF.Sin, scale=-1.0, bias=hpi[:, 0:1]),
                 [ab, hpi], [cs])
            lbr, lbi, t1, t2, den, cr, ci = sm("lbr"), sm("lbi"), sm("t1"), sm("t2"), sm("den"), sm("cr"), sm("ci")
            self.tt("dve", lbr[:, :], rr[:, :], cs[:, :], ALU.mult, [rr, cs], [lbr])
            self.tt("dve", lbi[:, :], rr[:, :], sn[:, :], ALU.mult, [rr, sn], [lbi])
            lm1 = sm("lm1")
            S.op("dve", lambda e: e.tensor_scalar(out=lm1[:, :], in0=lbr[:, :], scalar1=-1.0, scalar2=None, op0=ALU.add), [lbr], [lm1])
            self.tt("dve", t1[:, :], lre[:, :], lre[:, :], ALU.mult, [lre], [t1])
            self.tt("dve", t2[:, :], lim[:, :], lim[:, :], ALU.mult, [lim], [t2])
            self.tt("dve", den[:, :], t1[:, :], t2[:, :], ALU.add, [t1, t2], [den])
            S.op("dve", lambda e: e.reciprocal(out=den[:, :], in_=den[:, :]), [den], [den])
            self.tt("dve", t1[:, :], lm1[:, :], lre[:, :], ALU.mult, [lm1, lre], [t1])
            self.tt("dve", t2[:, :], lbi[:, :], lim[:, :], ALU.mult, [lbi, lim], [t2])
            self.tt("dve", cr[:, :], t1[:, :], t2[:, :], ALU.add, [t1, t2], [cr])
            self.tt("dve", cr[:, :], cr[:, :], den[:, :], ALU.mult, [cr, den], [cr])
            self.tt("dve", t1[:, :], lbi[:, :], lre[:, :], ALU.mult, [lbi, lre], [t1])
            self.tt("dve", t2[:, :], lm1[:, :], lim[:, :], ALU.mult, [lm1, lim], [t2])
            self.tt("dve", ci[:, :], t1[:, :], t2[:, :], ALU.subtract, [t1, t2], [ci])
            self.tt("dve", ci[:, :], ci[:, :], den[:, :], ALU.mult, [ci, den], [ci])
            bre = self.sb(st, "c_bre", [128, 16, 32])
            bim = self.sb(st, "c_bim", [128, 16, 32])
            bpr = self.sb(st, "c_bpr", [128, 16, 32])
            bpi = self.sb(st, "c_bpi", [128, 16, 32])
            btmp = self.sb(st, "c_btmp", [128, 16, 32])
            S.dma(bre[:, :, :], self.c_bsd_re[l], reads=[self.c_bsd_re], writes=[bre])
            S.dma(bim[:, :, :], self.c_bsd_im[l], reads=[self.c_bsd_im], writes=[bim])
            crb = cr[:, :].unsqueeze(2).to_broadcast([128, 16, 32])
            cib = ci[:, :].unsqueeze(2).to_broadcast([128, 16, 32])
            self.tt("dve", bpr[:, :, :], bre[:, :, :], crb, ALU.mult, [bre, cr], [bpr])
            self.tt("dve", btmp[:, :, :], bim[:, :, :], cib, ALU.mult, [bim, ci], [btmp])
            self.tt("dve", bpr[:, :, :], bpr[:, :, :], btmp[:, :, :], ALU.subtract, [bpr, btmp], [bpr])
            self.tt("dve", bpi[:, :, :], bim[:, :, :], crb, ALU.mult, [bim, cr], [bpi])
            self.tt("dve", btmp[:, :, :], bre[:, :, :], cib, ALU.mult, [bre, ci], [btmp])
            self.tt("dve", bpi[:, :, :], bpi[:, :, :], btmp[:, :, :], ALU.add, [bpi, btmp], [bpi])
            ptr = self.ps(st, "c_ptr", [128, 512])
            rmask = self.sb(st, "c_rmask", [128, 4])
            S.dma(rmask[:, :], self.rowmask[:, :], reads=[self.rowmask], writes=[rmask])
            btr = self.sb(st, "c_btr", [128, 16, 128], BF16)
            bti = self.sb(st, "c_bti", [128, 16, 128], BF16)
            for (src, dst) in ((bpr, btr), (bpi, bti)):
                for jj in range(4):
                    S.op("pe", lambda e, src=src, jj=jj: e.transpose(out=ptr[:, jj * 128:(jj + 1) * 128],
                                                                    in_=src[:, 4 * jj:4 * jj + 4, :],
                                                                    identity=self.identf[:]),
                         reads=[src, self.identf], writes=[ptr])
                for jq in range(4):
                    S.op("dve", lambda e, dst=dst, jq=jq: e.tensor_scalar(
                        out=dst[:, jq:16:4, :], in0=ptr[:, :].rearrange("p (a b) -> p a b", a=4),
                        scalar1=rmask[:, jq:jq + 1], scalar2=None, op0=ALU.mult), [ptr, rmask], [dst])
            csr0 = self.sb(st, "c_csr0", [128, 16, 32], BF16)
            csi0 = self.sb(st, "c_csi0", [128, 16, 32], BF16)
            S.dma(csr0[:, :, :], self.c_csd_re[l], reads=[self.c_csd_re], writes=[csr0], q="pool")
            S.dma(csi0[:, :, :], self.c_csd_im[l], reads=[self.c_csd_im], writes=[csi0], q="pool")
            csr = self.sb(st, "c_csr", [128, 16, 128], BF16)
            csi = self.sb(st, "c_csi", [128, 16, 128], BF16)
            for (src, dst) in ((csr0, csr), (csi0, csi)):
                S.op("pool", lambda e, dst=dst: e.memset(dst[:, :, :], 0.0), [], [dst])
                for jq in range(4):
                    S.op("dve", lambda e, src=src, dst=dst, jq=jq: e.tensor_copy(
                        out=dst[:, jq:16:4, 32 * jq:32 * jq + 32], in_=src[:, jq:16:4, :]), [src], [dst])
            Ec = self.sb(st, "c_Ec", [128, 16, LC])
            Es = self.sb(st, "c_Es", [128, 16, LC])
            cm, smm, c2, s2, tA, tB = sm("cm"), sm("smm"), sm("c2"), sm("s2"), sm("tA"), sm("tB")
            S.op("dve", lambda e: e.memset(Ec[:, :, 0:1], 1.0), [], [Ec])
            S.op("dve", lambda e: e.memset(Es[:, :, 0:1], 0.0), [], [Es])
            S.op("dve", lambda e: e.tensor_copy(out=cm[:, :], in_=cs[:, :]), [cs], [cm])
            S.op("dve", lambda e: e.tensor_scalar(out=smm[:, :], in0=sn[:, :], scalar1=-1.0, scalar2=None, op0=ALU.mult), [sn], [smm])
            with ExitStack() as stw:
                tw1 = self.sb(stw, "c_tw1", [128, 16, LC // 2])
                tw2 = self.sb(stw, "c_tw2", [128, 16, LC // 2])
                m = 1
                while m < LC:
                    cb = cm[:, :].unsqueeze(2).to_broadcast([128, 16, m])
                    sbb = smm[:, :].unsqueeze(2).to_broadcast([128, 16, m])
                    self.tt("dve", tw1[:, :, 0:m], Ec[:, :, 0:m], cb, ALU.mult, [Ec, cm], [tw1])
                    self.tt("pool", tw2[:, :, 0:m], Es[:, :, 0:m], sbb, ALU.mult, [Es, smm], [tw2])
                    self.tt("dve", Ec[:, :, m:2 * m], tw1[:, :, 0:m], tw2[:, :, 0:m], ALU.subtract, [tw1, tw2, Ec], [Ec])
                    self.tt("dve", tw1[:, :, 0:m], Ec[:, :, 0:m], sbb, ALU.mult, [Ec, smm], [tw1])
                    self.tt("pool", tw2[:, :, 0:m], Es[:, :, 0:m], cb, ALU.mult, [Es, cm], [tw2])
                    self.tt("dve", Es[:, :, m:2 * m], tw1[:, :, 0:m], tw2[:, :, 0:m], ALU.add, [tw1, tw2, Es], [Es])
                    self.tt("dve", tA[:, :], cm[:, :], cm[:, :], ALU.mult, [cm], [tA])
                    self.tt("dve", tB[:, :], smm[:, :], smm[:, :], ALU.mult, [smm], [tB])
                    self.tt("dve", c2[:, :], tA[:, :], tB[:, :], ALU.subtract, [tA, tB], [c2])
                    self.tt("dve", s2[:, :], cm[:, :], smm[:, :], ALU.mult, [cm, smm], [s2])
                    S.op("dve", lambda e: e.tensor_scalar(out=smm[:, :], in0=s2[:, :], scalar1=2.0, scalar2=None, op0=ALU.mult), [s2], [smm])
                    S.op("dve", lambda e: e.tensor_copy(out=cm[:, :], in_=c2[:, :]), [c2], [cm])
                    m *= 2
            S.barrier()
            uT = self.sb(st, "c_uT", [128, 4, NTOK])
            uTb = self.sb(st, "c_uTb", [128, 4, NTOK], BF16)
            for jj in range(4):
                S.dma(uT[:, jj, :], self.featT[FCU + 128 * jj:FCU + 128 * jj + 128, :], reads=[self.featT], writes=[uT])
                S.dma(uTb[:, jj, :], self.featT[FCU + 128 * jj:FCU + 128 * jj + 128, :], reads=[self.featT], writes=[uTb], q="pool")
            yT = self.sb(st, "c_yT", [128, 4, NTOK], BF16)
            dcol = self.sb(st, "c_dcol", [128, 4])
            S.dma(dcol[:, :], self.c_dT[l], reads=[self.c_dT], writes=[dcol])
            hpr = self.sb(st, "c_hpr", [128, 16])
            hpi_ = self.sb(st, "c_hpi", [128, 16])
            S.op("dve", lambda e: e.memset(hpr[:, :], 0.0), [], [hpr])
            S.op("dve", lambda e: e.memset(hpi_[:, :], 0.0), [], [hpi_])
            h0r = self.sb(st, "c_h0r", [128, 16, 4])
            h0i = self.sb(st, "c_h0i", [128, 16, 4])
            S.dma(h0r[:, :, :], self.c_h0re[l], reads=[self.c_h0re], writes=[h0r])
            S.dma(h0i[:, :, :], self.c_h0im[l], reads=[self.c_h0im], writes=[h0i])
            hsr = self.sb(st, "c_hsr", [128, 16, 4])
            hsi = self.sb(st, "c_hsi", [128, 16, 4])
            inr = self.sb(st, "c_inr", [128, 16, 4])
            ini = self.sb(st, "c_ini", [128, 16, 4])
            ia = self.sb(st, "c_ia", [128, 16, 4])
            ib = self.sb(st, "c_ib", [128, 16, 4])
            W = {}
            stw2 = ExitStack()
            for nm in ("zr", "zi", "ta", "tb", "gr", "gi", "hr", "hi"):
                W[nm] = [self.sb(stw2, "c_w%s%d" % (nm, i), [128, LC]) for i in range(2)]
            W["hrb"] = [self.sb(stw2, "c_whrb%d" % i, [128, LC], BF16) for i in range(2)]
            W["hib"] = [self.sb(stw2, "c_whib%d" % i, [128, LC], BF16) for i in range(2)]
            xr = [self.ps(st, "c_xr%d" % i, [128, LC]) for i in range(2)]
            xi = [self.ps(st, "c_xi%d" % i, [128, LC]) for i in range(2)]
            yp = [self.ps(st, "c_yp%d" % i, [128, LC]) for i in range(2)]
            it = [0]

            def init_from(prev_r, prev_i, n):
                csb = cs[:, :].unsqueeze(2).to_broadcast([128, 16, n])
                snb = sn[:, :].unsqueeze(2).to_broadcast([128, 16, n])
                self.tt("dve", ia[:, :, 0:n], prev_r, csb, ALU.mult, [cs, hpr, h0r], [ia])
                self.tt("dve", ib[:, :, 0:n], prev_i, snb, ALU.mult, [sn, hpi_, h0i], [ib])
                self.tt("dve", inr[:, :, 0:n], ia[:, :, 0:n], ib[:, :, 0:n], ALU.subtract, [ia, ib], [inr])
                self.tt("dve", ia[:, :, 0:n], prev_r, snb, ALU.mult, [sn, hpr, h0r], [ia])
                self.tt("dve", ib[:, :, 0:n], prev_i, csb, ALU.mult, [cs, hpi_, h0i], [ib])
                self.tt("dve", ini[:, :, 0:n], ia[:, :, 0:n], ib[:, :, 0:n], ALU.add, [ia, ib], [ini])

            def block(col0, nseg, seglen, last_r, last_i):
                n = nseg * seglen
                ncols = n if nseg == 1 else 32 * nseg
                for j in range(16):
                    k = it[0] % 2
                    it[0] += 1
                    jj, jq = j // 4, j % 4
                    pr = slice(0, 128)
                    if nseg == 1:
                        ucols = uTb[pr, jj, col0:col0 + n]
                        def v(t):
                            return t[:, 0:n]
                        def tab(T):
                            return T[:, j, 0:n]
                    else:
                        ucols = uTb[pr, jj, col0:col0 + 32 * nseg].rearrange("p (s t) -> p s t", s=nseg)[:, :, 0:seglen]
                        def v(t):
                            return t[:, 0:n].rearrange("p (s t) -> p s t", s=nseg)
                        def tab(T):
                            return T[:, j, 0:seglen].unsqueeze(1).to_broadcast([128, nseg, seglen])
                    S.op("pe", lambda e: e.matmul(v(xr[k]), lhsT=btr[:, j, :], rhs=ucols, start=True, stop=True),
                         reads=[btr, uTb], writes=[xr[k]])
                    S.op("pe", lambda e: e.matmul(v(xi[k]), lhsT=bti[:, j, :], rhs=ucols, start=True, stop=True),
                         reads=[bti, uTb], writes=[xi[k]])
                    zr, zi, ta, tb, gr, gi, hr, hi = [W[nm][k] for nm in ("zr", "zi", "ta", "tb", "gr", "gi", "hr", "hi")]
                    self.tt("dve", v(ta), v(xr[k]), tab(Ec), ALU.mult, [xr[k], Ec], [ta])
                    self.tt("dve", v(tb), v(xi[k]), tab(Es), ALU.mult, [xi[k], Es], [tb])
                    self.tt("pool", v(zr), v(ta), v(tb), ALU.subtract, [ta, tb], [zr])
                    self.tt("dve", v(ta), v(xi[k]), tab(Ec), ALU.mult, [xi[k], Ec], [ta])
                    self.tt("dve", v(tb), v(xr[k]), tab(Es), ALU.mult, [xr[k], Es], [tb])
                    self.tt("pool", v(zi), v(ta), v(tb), ALU.add, [ta, tb], [zi])
                    rb = rr[:, j:j + 1].to_broadcast([128, seglen])
                    for sg in range(nseg):
                        cs_ = slice(sg * seglen, (sg + 1) * seglen)
                        S.op("dve", lambda e, sg=sg, cs_=cs_: e.tensor_tensor_scan(
                            out=gr[:, cs_], data0=rb, data1=zr[:, cs_], initial=inr[:, j, sg:sg + 1],
                            op0=ALU.mult, op1=ALU.add), [zr, rr, inr], [gr])
                        S.op("dve", lambda e, sg=sg, cs_=cs_: e.tensor_tensor_scan(
                            out=gi[:, cs_], data0=rb, data1=zi[:, cs_], initial=ini[:, j, sg:sg + 1],
                            op0=ALU.mult, op1=ALU.add), [zi, rr, ini], [gi])
                    self.tt("pool", v(ta), v(gr), tab(Ec), ALU.mult, [gr, Ec], [ta])
                    self.tt("pool", v(tb), v(gi), tab(Es), ALU.mult, [gi, Es], [tb])
                    self.tt("dve", v(hr), v(ta), v(tb), ALU.add, [ta, tb], [hr])
                    self.tt("pool", v(ta), v(gr), tab(Es), ALU.mult, [gr, Es], [ta])
                    self.tt("pool", v(tb), v(gi), tab(Ec), ALU.mult, [gi, Ec], [tb])
                    self.tt("dve", v(hi), v(ta), v(tb), ALU.subtract, [ta, tb], [hi])
                    hrb, hib = W["hrb"][k], W["hib"][k]
                    S.op("act", lambda e: e.activation(out=hrb[:, 0:n], in_=hr[:, 0:n], func=AF.Copy), [hr], [hrb])
                    S.op("act", lambda e: e.activation(out=hib[:, 0:n], in_=hi[:, 0:n], func=AF.Copy), [hi], [hib])
                    if nseg == 1:
                        S.op("act", lambda e: e.activation(out=last_r[:, j:j + 1], in_=hr[:, n - 1:n], func=AF.Copy), [hr], [hpr])
                        S.op("act", lambda e: e.activation(out=last_i[:, j:j + 1], in_=hi[:, n - 1:n], func=AF.Copy, scale=-1.0), [hi], [hpi_])
                    else:
                        lv = slice(seglen - 1, n, seglen)
                        S.op("act", lambda e: e.activation(out=last_r[:, j, :], in_=hr[:, lv], func=AF.Copy), [hr], [hsr])
                        S.op("act", lambda e: e.activation(out=last_i[:, j, :], in_=hi[:, lv], func=AF.Copy, scale=-1.0), [hi], [hsi])
                    ypk = yp[(it[0] // 8) % 2] if False else yp[0]
                    if nseg == 1:
                        yv = ypk[pr, 0:n]
                        hbv_r, hbv_i = hrb[:, 0:n], hib[:, 0:n]
                    else:
                        yv = ypk[pr, 0:n]
                        hbv_r, hbv_i = hrb[:, 0:n], hib[:, 0:n]
                    S.op("pe", lambda e: e.matmul(yv, lhsT=csr[:, j, :], rhs=hbv_r, start=(jq == 0), stop=False),
                         reads=[csr, hrb], writes=[ypk])
                    S.op("pe", lambda e: e.matmul(yv, lhsT=csi[:, j, :], rhs=hbv_i, start=False, stop=(jq == 3)),
                         reads=[csi, hib], writes=[ypk])
                    if jq == 3:
                        if nseg == 1:
                            uin = uT[:, jj, col0:col0 + n]
                            yout = yT[:, jj, col0:col0 + n]
                            yin = ypk[:, 0:n]
                        else:
                            uin = uT[:, jj, col0:col0 + 32 * nseg].rearrange("p (s t) -> p s t", s=nseg)[:, :, 0:seglen]
                            yout = yT[:, jj, col0:col0 + 32 * nseg].rearrange("p (s t) -> p s t", s=nseg)[:, :, 0:seglen]
                            yin = ypk[:, 0:n].rearrange("p (s t) -> p s t", s=nseg)
                        S.op("dve", lambda e: e.scalar_tensor_tensor(out=yout, in0=uin, scalar=dcol[:, jj:jj + 1], in1=yin,
                                                                     op0=ALU.mult, op1=ALU.add),
                             [uT, dcol, ypk], [yT])

            S.op("pool", lambda e: e.memset(yT[:, :, :], 0.0), [], [yT])
            for ch0 in range(0, TP, LC):
                n = min(LC, TP - ch0)
                init_from(hpr[:, :].unsqueeze(2), hpi_[:, :].unsqueeze(2), 1)
                block(ch0, 1, n, hpr, hpi_)
            S.dma(self.o_pcre[l], hpr[:, :], reads=[hpr], writes=[self.o_pcre])
            S.dma(self.o_pcim[l], hpi_[:, :], reads=[hpi_], writes=[self.o_pcim])
            init_from(h0r[:, :, :], h0i[:, :, :], 4)
            block(TP, 4, 8, hsr, hsi)
            S.dma(self.o_scre[l], hsr[:, :, :], reads=[hsr], writes=[self.o_scre])
            S.dma(self.o_scim[l], hsi[:, :, :], reads=[hsi], writes=[self.o_scim])
            S.barrier()
            stw2.close()
            gw = self.sb(st, "c_gw", [128, 4, 1024], BF16)
            gb = self.sb(st, "c_gb", [128, 8])
            S.dma(gw[:, :, :], self.c_glu_w[l].rearrange("(k p) c -> p k c", p=128), reads=[self.c_glu_w], writes=[gw], q="pool")
            S.dma(gb[:, :], self.c_glu_bT[l], reads=[self.c_glu_bT], writes=[gb])
            sg_ = [self.sb(st, "c_sig%d" % i, [128, 512]) for i in range(2)]
            og = [self.sb(st, "c_og%d" % i, [128, 512]) for i in range(2)]
            gi_ = 0
            for tb0 in range(0, NTOK, 512):
                tn = min(512, NTOK - tb0)
                for i in range(4):
                    pv, pg = xr[gi_ % 2], xi[gi_ % 2]
                    for kk in range(4):
                        S.op("pe", lambda e, kk=kk: e.matmul(pv[:, 0:tn], lhsT=gw[:, kk, 128 * i:128 * i + 128],
                                                             rhs=yT[:, kk, tb0:tb0 + tn], start=(kk == 0), stop=(kk == 3)),
                             reads=[gw, yT], writes=[pv])
                    for kk in range(4):
                        S.op("pe", lambda e, kk=kk: e.matmul(pg[:, 0:tn], lhsT=gw[:, kk, 512 + 128 * i:512 + 128 * i + 128],
                                                             rhs=yT[:, kk, tb0:tb0 + tn], start=(kk == 0), stop=(kk == 3)),
                             reads=[gw, yT], writes=[pg])
                    sgb, ogb = sg_[gi_ % 2], og[gi_ % 2]
                    gi_ += 1
                    S.op("act", lambda e: e.activation(out=sgb[:, 0:tn], in_=pg[:, 0:tn], func=AF.Sigmoid, bias=gb[:, 4 + i:5 + i]),
                         [pg, gb], [sgb])
                    S.op("dve", lambda e: e.scalar_tensor_tensor(out=ogb[:, 0:tn], in0=pv[:, 0:tn], scalar=gb[:, i:i + 1],
                                                                 in1=sgb[:, 0:tn], op0=ALU.add, op1=ALU.mult),
                         [pv, gb, sgb], [ogb])
                    S.dma(self.mixT[1024 + 128 * i:1024 + 128 * i + 128, tb0:tb0 + tn], ogb[:, 0:tn], reads=[ogb], writes=[self.mixT])
        S.barrier()

    def phase_gdn(self, l):
        c = self.cfg
        S = self.S
        TP, NTOK, NT, NTP = c.TP, c.NTOK, c.NT, c.NTP
        with ExitStack() as st:
            qT = self.sb(st, "a_qT", [128, 4, NTOK], BF16)
            kT = self.sb(st, "a_kT", [128, 4, NTOK], BF16)
            vT = self.sb(st, "a_vT", [128, 4, NTOK], BF16)
            onesf = self.sb(st, "a_onesf", [128, 128])
            S.op("pool", lambda e: e.memset(onesf[:, :], 1.0), [], [onesf])
            with ExitStack() as st1:
                cw = self.sb(st1, "a_cw", [128, 12, 4])
                S.dma(cw[:, :, :], self.a_cw[l], reads=[self.a_cw], writes=[cw])
                EW = TP + 3
                Eb = [self.sb(st1, "a_E%d" % i, [128, EW]) for i in range(2)]
                Esb = [self.sb(st1, "a_Es%d" % i, [128, 4, 11]) for i in range(2)]
                Ob = [self.sb(st1, "a_O%d" % i, [128, NTOK]) for i in range(2)]
                Sq = [self.sb(st1, "a_Sq%d" % i, [128, 512]) for i in range(2)]
                Rs = [self.sb(st1, "a_Rs%d" % i, [128, 512]) for i in range(2)]
                pss = [self.ps(st1, "a_pss%d" % i, [128, 512]) for i in range(2)]
                epsc = self.sb(st1, "a_eps", [128, 1])
                S.op("dve", lambda e: e.memset(epsc[:, :], 1e-6), [], [epsc])
                for i in range(2):
                    S.op("dve", lambda e, i=i: e.memset(Eb[i][:, 0:3], 0.0), [], [Eb[i]])
                for ct in range(12):
                    E, Es_, O = Eb[ct % 2], Esb[ct % 2], Ob[ct % 2]
                    r0 = ct * 128
                    S.dma(E[:, 3:3 + TP], self.featT[r0:r0 + 128, 0:TP], reads=[self.featT], writes=[E])
                    S.dma(Es_[:, :, 3:11], self.featT[r0:r0 + 128, TP:TP + 128].rearrange("p (s t) -> p s t", s=4)[:, :, 0:8],
                          reads=[self.featT], writes=[Es_], q="act")
                    S.dma(Es_[:, :, 0:3], self.a_cc[l, :, ct, :, :], reads=[self.a_cc], writes=[Es_], q="act")
                    S.op("pool", lambda e: e.memset(O[:, TP:NTOK], 0.0), [], [O])
                    Osv = O[:, TP:NTOK].rearrange("p (s t) -> p s t", s=4)[:, :, 0:8]
                    for (ov, ev) in ((O[:, 0:TP], lambda j: E[:, j:j + TP]), (Osv, lambda j: Es_[:, :, j:j + 8])):
                        S.op("dve", lambda e: e.tensor_scalar(out=ov, in0=ev(0), scalar1=cw[:, ct, 0:1], scalar2=None, op0=ALU.mult),
                             [E, Es_, cw], [O])
                        for j in range(1, 4):
                            S.op("dve", lambda e, j=j: e.scalar_tensor_tensor(out=ov, in0=ev(j), scalar=cw[:, ct, j:j + 1], in1=ov,
                                                                             op0=ALU.mult, op1=ALU.add), [E, Es_, cw, O], [O])
                    S.op("act", lambda e: e.activation(out=O[:, :], in_=O[:, :], func=AF.Silu), [O], [O])
                    grp, h = ct // 4, ct % 4
                    dst = (qT, kT, vT)[grp]
                    if grp == 2:
                        S.op("act", lambda e: e.activation(out=dst[:, h, :], in_=O[:, :], func=AF.Copy), [O], [dst])
                        continue
                    for tb0 in range(0, NTOK, 512):
                        tn = min(512, NTOK - tb0)
                        sq, rs, pq = Sq[(tb0 // 512) % 2], Rs[(tb0 // 512) % 2], pss[(tb0 // 512) % 2]
                        S.op("act", lambda e: e.activation(out=sq[:, 0:tn], in_=O[:, tb0:tb0 + tn], func=AF.Square), [O], [sq])
                        S.op("pe", lambda e: e.matmul(pq[:, 0:tn], lhsT=onesf[:, :], rhs=sq[:, 0:tn], start=True, stop=True),
                             [onesf, sq], [pq])
                        S.op("act", lambda e: e.activation(out=rs[:, 0:tn], in_=pq[:, 0:tn], func=AF.Sqrt, bias=epsc[:, 0:1]), [pq, epsc], [rs])
                        S.op("dve", lambda e: e.reciprocal(out=rs[:, 0:tn], in_=rs[:, 0:tn]), [rs], [rs])
                        if grp == 0:
                            S.op("dve", lambda e: e.scalar_tensor_tensor(out=dst[:, h, tb0:tb0 + tn], in0=O[:, tb0:tb0 + tn],
                                                                         scalar=float(128 ** -0.5), in1=rs[:, 0:tn],
                                                                         op0=ALU.mult, op1=ALU.mult), [O, rs], [dst])
                        else:
                            self.tt("dve", dst[:, h, tb0:tb0 + tn], O[:, tb0:tb0 + tn], rs[:, 0:tn], ALU.mult, [O, rs], [dst])
            S.barrier()
            S.dma(self.o_paconv[l], self.featT[0:1536, TP - 3:TP], reads=[self.featT], writes=[self.o_paconv])
            for s_ in range(4):
                S.dma(self.o_saconv[l, s_], self.featT[0:1536, TP + 32 * s_ + 5:TP + 32 * s_ + 8], reads=[self.featT], writes=[self.o_saconv])
            ab = self.sb(st, "a_ab", [128, NT, 8])
            S.dma(ab[:, :, :], self.tokM[:, TAB:TAB + 8].rearrange("(t p) c -> p t c", p=128), reads=[self.tokM], writes=[ab])
            par = self.sb(st, "a_par", [128, 8])
            S.dma(par[:, :], self.a_par[l:l + 1, :].to_broadcast([128, 8]), reads=[self.a_par], writes=[par])
            vld = self.sb(st, "a_vld", [128, NT])
            S.dma(vld[:, :], self.valid[:, :], reads=[self.valid], writes=[vld])
            negA = self.sb(st, "a_negA", [128, 4])
            S.op("act", lambda e: e.activation(out=negA[:, :], in_=par[:, 0:4], func=AF.Exp), [par], [negA])
            S.op("dve", lambda e: e.tensor_scalar(out=negA[:, :], in0=negA[:, :], scalar1=-1.0, scalar2=None, op0=ALU.mult), [negA], [negA])
            gg = self.sb(st, "a_gg", [128, NT, 4])
            bb = self.sb(st, "a_bb", [128, NT, 4])
            nbb = self.sb(st, "a_nbb", [128, NT, 4])
            vb4 = vld[:, :].unsqueeze(2).to_broadcast([128, NT, 4])
            self.tt("dve", gg[:, :, :], ab[:, :, 0:4], par[:, 4:8].unsqueeze(1).to_broadcast([128, NT, 4]), ALU.add, [ab, par], [gg])
            S.op("act", lambda e: e.activation(out=gg[:, :, :], in_=gg[:, :, :], func=AF.Exp), [gg], [gg])
            S.op("act", lambda e: e.activation(out=gg[:, :, :], in_=gg[:, :, :], func=AF.Ln, bias=1.0), [gg], [gg])
            self.tt("dve", gg[:, :, :], gg[:, :, :], negA[:, :].unsqueeze(1).to_broadcast([128, NT, 4]), ALU.mult, [gg, negA], [gg])
            self.tt("dve", gg[:, :, :], gg[:, :, :], vb4, ALU.mult, [gg, vld], [gg])
            S.op("act", lambda e: e.activation(out=bb[:, :, :], in_=ab[:, :, 4:8], func=AF.Sigmoid), [ab], [bb])
            self.tt("dve", bb[:, :, :], bb[:, :, :], vb4, ALU.mult, [bb, vld], [bb])
            S.op("dve", lambda e: e.tensor_scalar(out=nbb[:, :, :], in0=bb[:, :, :], scalar1=-1.0, scalar2=None, op0=ALU.mult), [bb], [nbb])
            gm = self.sb(st, "a_gm", [128, 3, 128])
            S.dma(gm[:, :, :], self.gmasks[:, :, :].rearrange("m p q -> p m q"), reads=[self.gmasks], writes=[gm])
            bsel = self.sb(st, "a_bsel", [128, 4])
            S.dma(bsel[:, :], self.rowmask[:, :], reads=[self.rowmask], writes=[bsel])
            bselb = self.sb(st, "a_bselb", [128, 4], BF16)
            S.op("dve", lambda e: e.tensor_copy(out=bselb[:, :], in_=bsel[:, :]), [bsel], [bselb])
            identb = self.sb(st, "a_identb", [128, 128], BF16)
            S.op("dve", lambda e: e.tensor_copy(out=identb[:, :], in_=self.identf[:, :]), [self.identf], [identb])
            nw = self.sb(st, "a_nw", [128, 128])
            S.dma(nw[:, :], self.a_nw[l:l + 1, :].to_broadcast([128, 128]), reads=[self.a_nw], writes=[nw])
            Sf = [self.sb(st, "a_Sf%d" % h, [128, 128]) for h in range(4)]
            Sb = [self.sb(st, "a_Sb%d" % h, [128, 128], BF16) for h in range(4)]
            for h in range(4):
                S.op("pool", lambda e, h=h: e.memset(Sf[h][:, :], 0.0), [], [Sf[h]])
                S.op("pool", lambda e, h=h: e.memset(Sb[h][:, :], 0.0), [], [Sb[h]])
            banks = [self.ps(st, "a_bank%d" % i, [128, 512]) for i in range(8)]
            def quarter(bk, qi):
                return Sub(banks[bk][:, qi * 128:(qi + 1) * 128], banks[bk], "a_q%d_%d" % (bk, qi))
            scr = [quarter(bk, qi) for qi in range(4) for bk in (0, 1, 2, 3)]
            scr_i = [0]
            def pscr():
                b = scr[scr_i[0] % len(scr)]
                scr_i[0] += 1
                return b
            hb = {h: [quarter(4 + h, qi) for qi in range(4)] for h in range(4)}
            def wk(name, shape, dt=F32):
                return [[self.sb(st, "a_%s_%d_%d" % (name, h, i), shape, dt) for i in range(2)] for h in range(4)]
            u_b = wk("u", [128, 128])
            wTm_b = wk("wTm", [128, 4, 128], BF16)
            qdTm_b = wk("qdTm", [128, 4, 128], BF16)
            kdm_b = wk("kdm", [128, 4, 128], BF16)
            qkm_b = wk("qkm", [128, 4, 128], BF16)
            gl_b = wk("gl", [128, 4])
            oacc_b = wk("oacc", [128, 128])
            for h in range(4):
                for i in range(2):
                    S.op("pool", lambda e, h=h, i=i: e.memset(wTm_b[h][i][:, :, :], 0.0), [], [wTm_b[h][i]])
                    S.op("pool", lambda e, h=h, i=i: e.memset(qdTm_b[h][i][:, :, :], 0.0), [], [qdTm_b[h][i]])
            def t128(name, dt=F32, n=128):
                return self.sb(st, "a_t_" + name, [128, n], dt)
            gbc, sml, colv = t128("gbc"), t128("sml", F32, 8), t128("colv", F32, 8)
            d1, d2, Nm, NmT, PT, Mx, MxT, tmpA = (t128("d1"), t128("d2"), t128("N"), t128("NT"), t128("PT"),
                                                   t128("M"), t128("MT"), t128("tmpA"))
            qkf = t128("qkf", BF16)
            vbt, kbg, kdec, qdtm = t128("vb"), t128("kbg"), t128("kdec", BF16), t128("qdtm", BF16)
            vnb = [t128("vnb%d" % h, BF16) for h in range(4)]
            t1o = [t128("t1o%d" % h) for h in range(4)]
            zt = [self.sb(st, "a_zt%d" % i, [128, 128]) for i in range(2)]
            og = [self.sb(st, "a_og%d" % i, [128, 128]) for i in range(2)]
            onrm = t128("onrm")
            ssq = t128("ssq", F32, 2)

            def diag_ap(buf):
                a = buf[:, :, :]
                return AP(tensor=a.tensor, offset=a.offset, ap=[list(a.ap[0]), [128 + 32, 4], [1, 32]])

            def intra(h, tt_):
                par_ = tt_ % 2
                cols = slice(tt_ * 128, (tt_ + 1) * 128)
                gcol = gg[:, tt_, h:h + 1]
                bcol = bb[:, tt_, h:h + 1]
                nbcol = nbb[:, tt_, h:h + 1]
                S.op("dve", lambda e: e.tensor_scalar(out=gbc[:, :], in0=onesf[:, :], scalar1=gcol, scalar2=None, op0=ALU.mult),
                     [onesf, gg], [gbc])
                pG, pS = pscr(), pscr()
                S.op("pe", lambda e: e.matmul(pG[:, :], lhsT=gbc[:, :], rhs=gm[:, 0, :], start=True, stop=True), [gbc, gm], [pG])
                S.op("pe", lambda e: e.matmul(pS[:, 0:1], lhsT=gm[:, 0, :], rhs=gcol, start=True, stop=True), [gm, gg], [pS])
                S.op("pe", lambda e: e.matmul(pS[:, 1:2], lhsT=gm[:, 1, :], rhs=gcol, start=True, stop=True), [gm, gg], [pS])
                S.op("pe", lambda e: e.matmul(pS[:, 2:6], lhsT=gbc[:, :], rhs=bsel[:, :], start=True, stop=True), [gbc, bsel], [pS])
                S.op("act", lambda e: e.activation(out=sml[:, 0:6], in_=pS[:, 0:6], func=AF.Copy), [pS], [sml])
                gl = gl_b[h][par_]
                S.op("act", lambda e: e.activation(out=gl[:, :], in_=sml[:, 2:6], func=AF.Exp), [sml], [gl])
                S.op("act", lambda e: e.activation(out=colv[:, 0:1], in_=sml[:, 0:1], func=AF.Exp), [sml], [colv])
                S.op("dve", lambda e: e.tensor_tensor(out=colv[:, 3:4], in0=sml[:, 1:2], in1=sml[:, 0:1], op=ALU.subtract), [sml, colv], [colv])
                S.op("act", lambda e: e.activation(out=colv[:, 1:2], in_=colv[:, 3:4], func=AF.Exp), [colv], [colv])
                S.op("dve", lambda e: e.tensor_tensor(out=colv[:, 2:3], in0=colv[:, 0:1], in1=bcol, op=ALU.mult), [colv, bb], [colv])
                if c.intra_stop <= 1:
                    return
                S.op("dve", lambda e: e.tensor_scalar(out=d1[:, :], in0=pG[:, :], scalar1=sml[:, 0:1], scalar2=0.0,
                                                      op0=ALU.subtract, op1=ALU.max), [pG, sml], [d1])
                S.op("act", lambda e: e.activation(out=d1[:, :], in_=d1[:, :], func=AF.Exp, scale=-1.0), [d1], [d1])
                self.tt("pool", d1[:, :], d1[:, :], gm[:, 2, :], ALU.mult, [d1, gm], [d1])
                S.op("dve", lambda e: e.tensor_scalar(out=d2[:, :], in0=pG[:, :], scalar1=sml[:, 0:1], scalar2=0.0,
                                                      op0=ALU.subtract, op1=ALU.min), [pG, sml], [d2])
                S.op("act", lambda e: e.activation(out=d2[:, :], in_=d2[:, :], func=AF.Exp), [d2], [d2])
                self.tt("pool", d2[:, :], d2[:, :], gm[:, 0, :], ALU.mult, [d2, gm], [d2])
                if c.intra_stop <= 2:
                    return
                pK, pQ = pscr(), pscr()
                S.op("pe", lambda e: e.matmul(pK[:, :], lhsT=kT[:, h, cols], rhs=kT[:, h, cols], start=True, stop=True), [kT], [pK])
                S.op("pe", lambda e: e.matmul(pQ[:, :], lhsT=kT[:, h, cols], rhs=qT[:, h, cols], start=True, stop=True), [kT, qT], [pQ])
                S.op("dve", lambda e: e.scalar_tensor_tensor(out=Nm[:, :], in0=pK[:, :], scalar=nbcol, in1=d1[:, :],
                                                             op0=ALU.mult, op1=ALU.mult), [pK, nbb, d1], [Nm])
                self.tt("dve", qkf[:, :], pQ[:, :], d2[:, :], ALU.mult, [pQ, d2], [qkf])
                qkm = qkm_b[h][par_]
                self.tt("pool", qkm[:, :, :], qkf[:, :].unsqueeze(1).to_broadcast([128, 4, 128]),
                        bselb[:, :].unsqueeze(2).to_broadcast([128, 4, 128]), ALU.mult, [qkf, bselb], [qkm])
                if c.intra_stop <= 3:
                    return
                pT = pscr()
                S.op("pe", lambda e: e.transpose(out=pT[:, :], in_=Nm[:, :], identity=self.identf[:, :]), [Nm, self.identf], [pT])
                S.op("act", lambda e: e.activation(out=NmT[:, :], in_=pT[:, :], func=AF.Copy), [pT], [NmT])
                self.tt("dve", PT[:, :], pT[:, :], self.identf[:, :], ALU.add, [pT, self.identf], [PT])
                cur, curT = Nm, NmT
                nxt = [(Mx, MxT), (Nm, NmT)]
                for step in range(c.chain_steps):
                    M_, MT_ = nxt[step % 2]
                    pM = pscr()
                    S.op("pe", lambda e, cur=cur, curT=curT, pM=pM: e.matmul(pM[:, :], lhsT=curT[:, :], rhs=cur[:, :], start=True, stop=True),
                         [cur, curT], [pM])
                    if step < 3:
                        pMT = pscr()
                        S.op("pe", lambda e, cur=cur, curT=curT, pMT=pMT: e.matmul(pMT[:, :], lhsT=cur[:, :], rhs=curT[:, :], start=True, stop=True),
                             [cur, curT], [pMT])
                    S.op("act", lambda e, M_=M_, pM=pM: e.activation(out=M_[:, :], in_=pM[:, :], func=AF.Copy), [pM], [M_])
                    if step < 3:
                        S.op("dve", lambda e, MT_=MT_, pMT=pMT: e.tensor_copy(out=MT_[:, :], in_=pMT[:, :]), [pMT], [MT_])
                    pP = pscr()
                    S.op("pe", lambda e, M_=M_, pP=pP: e.matmul(pP[:, :], lhsT=M_[:, :], rhs=PT[:, :], start=True, stop=True), [M_, PT], [pP])
                    self.tt("dve", PT[:, :], PT[:, :], pP[:, :], ALU.add, [PT, pP], [PT])
                    cur, curT = M_, MT_
                if c.intra_stop <= 4:
                    return
                pkt, pvt = pscr(), pscr()
                pkt_b = Buf(pkt[:, :].bitcast(BF16)[:, 0:128], pkt.name)
                pkt_b.w, pkt_b.r = pkt.w, pkt.r
                S.op("pe", lambda e: e.transpose(out=pkt_b[:, :], in_=kT[:, h, cols], identity=identb[:, :]), [kT, identb], [pkt])
                pvt_b = Buf(pvt[:, :].bitcast(BF16)[:, 0:128], pvt.name)
                S.op("pe", lambda e: e.transpose(out=pvt_b[:, :], in_=vT[:, h, cols], identity=identb[:, :]), [vT, identb], [pvt])
                S.op("dve", lambda e: e.tensor_scalar(out=kbg[:, :], in0=pkt_b[:, :], scalar1=colv[:, 2:3], scalar2=None, op0=ALU.mult),
                     [pkt, colv], [kbg])
                S.op("act", lambda e: e.activation(out=kdec[:, :], in_=pkt_b[:, :], func=AF.Copy, scale=colv[:, 1:2]), [pkt, colv], [kdec])
                kdm = kdm_b[h][par_]
                self.tt("pool", kdm[:, :, :], kdec[:, :].unsqueeze(1).to_broadcast([128, 4, 128]),
                        bselb[:, :].unsqueeze(2).to_broadcast([128, 4, 128]), ALU.mult, [kdec, bselb], [kdm])
                S.op("dve", lambda e: e.tensor_scalar(out=vbt[:, :], in0=pvt_b[:, :], scalar1=bcol, scalar2=None, op0=ALU.mult),
                     [pvt, bb], [vbt])
                if c.intra_stop <= 5:
                    return
                pqt = pscr()
                pqt_b = Buf(pqt[:, :].bitcast(BF16)[:, 0:128], pqt.name)
                S.op("pe", lambda e: e.transpose(out=pqt_b[:, :], in_=qT[:, h, cols], identity=identb[:, :]), [qT, identb], [pqt])
                S.op("act", lambda e: e.activation(out=qdtm[:, :], in_=pqt_b[:, :], func=AF.Copy, scale=colv[:, 0:1]), [pqt, colv], [qdtm])
                pq2 = pscr()
                pq2_b = Buf(pq2[:, :].bitcast(BF16)[:, 0:128], pq2.name)
                S.op("pe", lambda e: e.transpose(out=pq2_b[:, :], in_=qdtm[:, :], identity=identb[:, :]), [qdtm, identb], [pq2])
                qdTm = qdTm_b[h][par_]
                S.op("dve", lambda e: e.tensor_copy(out=diag_ap(qdTm), in_=pq2_b[:, :].rearrange("p (c t) -> p c t", c=4)), [pq2], [qdTm])
                if c.intra_stop <= 6:
                    return
                pu, pw = pscr(), pscr()
                S.op("pe", lambda e: e.matmul(pu[:, :], lhsT=PT[:, :], rhs=vbt[:, :], start=True, stop=True), [PT, vbt], [pu])
                S.op("pe", lambda e: e.matmul(pw[:, :], lhsT=kbg[:, :], rhs=PT[:, :], start=True, stop=True), [kbg, PT], [pw])
                u_ = u_b[h][par_]
                S.op("act", lambda e: e.activation(out=u_[:, :], in_=pu[:, :], func=AF.Copy), [pu], [u_])
                wTm = wTm_b[h][par_]
                S.op("dve", lambda e: e.tensor_copy(out=diag_ap(wTm), in_=pw[:, :].rearrange("p (c t) -> p c t", c=4)), [pw], [wTm])

            def inter(h, tt_, cb):
                par_ = tt_ % 2
                pa, po, pS_, _ = hb[h]
                wTm, qdTm, kdm, qkm, u_, gl, oacc = (wTm_b[h][par_], qdTm_b[h][par_], kdm_b[h][par_], qkm_b[h][par_],
                                                     u_b[h][par_], gl_b[h][par_], oacc_b[h][par_])
                sample = (tt_ == NTP)
                if sample:
                    S.dma(Sf[h][:, :], self.state_a[l, cb, h], reads=[self.state_a], writes=[Sf[h]])
                    S.op("act", lambda e: e.activation(out=Sb[h][:, :], in_=Sf[h][:, :], func=AF.Copy), [Sf[h]], [Sb[h]])
                S.op("pe", lambda e: e.matmul(pa[:, :], lhsT=wTm[:, cb, :], rhs=Sb[h][:, :], start=True, stop=True), [wTm, Sb[h]], [pa])
                self.tt("dve", vnb[h][:, :], u_[:, :], pa[:, :], ALU.subtract, [u_, pa], [vnb[h]])
                S.op("pe", lambda e: e.matmul(po[:, :], lhsT=qdTm[:, cb, :], rhs=Sb[h][:, :], start=True, stop=False), [qdTm, Sb[h]], [po])
                S.op("pe", lambda e: e.matmul(po[:, :], lhsT=qkm[:, cb, :], rhs=vnb[h][:, :], start=False, stop=True), [qkm, vnb[h]], [po])
                S.op("pe", lambda e: e.matmul(pS_[:, :], lhsT=kdm[:, cb, :], rhs=vnb[h][:, :], start=True, stop=True), [kdm, vnb[h]], [pS_])
                if cb == 0:
                    S.op("act", lambda e: e.activation(out=oacc[:, :], in_=po[:, :], func=AF.Copy), [po], [oacc])
                else:
                    self.tt("pool" if False else "dve", oacc[:, :], oacc[:, :], po[:, :], ALU.add, [oacc, po], [oacc])
                S.op("dve", lambda e: e.scalar_tensor_tensor(out=Sf[h][:, :], in0=Sf[h][:, :], scalar=gl[:, cb:cb + 1], in1=pS_[:, :],
                                                             op0=ALU.mult, op1=ALU.add), [Sf[h], gl, pS_], [Sf[h]])
                if sample:
                    S.dma(self.o_sastate[l, cb, h], Sf[h][:, :], reads=[Sf[h]], writes=[self.o_sastate])
                else:
                    S.op("act", lambda e: e.activation(out=Sb[h][:, :], in_=Sf[h][:, :], func=AF.Copy), [Sf[h]], [Sb[h]])

            def outp(h, tt_):
                par_ = tt_ % 2
                oacc = oacc_b[h][par_]
                cols = slice(tt_ * 128, (tt_ + 1) * 128)
                k = (h + tt_) % 2
                z = zt[k]
                S.dma(z[:, :], self.featT[FZ + 128 * h:FZ + 128 * h + 128, cols], reads=[self.featT], writes=[z])
                S.op("act", lambda e: e.activation(out=z[:, :], in_=z[:, :], func=AF.Silu), [z], [z])
                S.op("act", lambda e: e.activation(out=onrm[:, :], in_=oacc[:, :], func=AF.Square, accum_out=ssq[:, 0:1]), [oacc], [onrm, ssq])
                S.op("act", lambda e: e.activation(out=ssq[:, 1:2], in_=ssq[:, 0:1], func=AF.Sqrt, scale=1.0 / 128.0, bias=epsc2[:, 0:1]), [ssq, epsc2], [ssq])
                S.op("dve", lambda e: e.reciprocal(out=ssq[:, 1:2], in_=ssq[:, 1:2]), [ssq], [ssq])
                S.op("dve", lambda e: e.scalar_tensor_tensor(out=onrm[:, :], in0=oacc[:, :], scalar=ssq[:, 1:2], in1=nw[:, :],
                                                             op0=ALU.mult, op1=ALU.mult), [oacc, ssq, nw], [onrm])
                pO = pscr()
                S.op("pe", lambda e: e.transpose(out=pO[:, :], in_=onrm[:, :], identity=self.identf[:, :]), [onrm, self.identf], [pO])
                o_ = og[k]
                self.tt("dve", o_[:, :], pO[:, :], z[:, :], ALU.mult, [pO, z], [o_])
                S.dma(self.mixT[128 * h:128 * h + 128, cols], o_[:, :], reads=[o_], writes=[self.mixT], q="act")

            epsc2 = self.sb(st, "a_eps2", [128, 1])
            S.op("dve", lambda e: e.memset(epsc2[:, :], 1e-6), [], [epsc2])
            GL = c.gdn_level
            for tt_ in range(NT if GL >= 1 else 0):
                for h in range(4):
                    intra(h, tt_)
                for cb in range(4 if GL >= 2 else 0):
                    for h in range(4):
                        inter(h, tt_, cb)
                for h in range(4 if GL >= 3 else 0):
                    outp(h, tt_)
                if tt_ == NTP - 1:
                    for h in range(4):
                        S.dma(self.o_pastate[l, h], Sf[h][:, :], reads=[Sf[h]], writes=[self.o_pastate])
        S.barrier()

    def layernorm(self, r, g, b, out, junk, stt, epsc):
        S = self.S
        S.op("act", lambda e: e.activation(out=junk[:, :], in_=r[:, :], func=AF.Copy, accum_out=stt[:, 0:1]), [r], [junk, stt])
        S.op("dve", lambda e: e.tensor_scalar(out=stt[:, 1:2], in0=stt[:, 0:1], scalar1=-1.0 / D, scalar2=None, op0=ALU.mult), [stt], [stt])
        S.op("act", lambda e: e.activation(out=junk[:, :], in_=r[:, :], func=AF.Square, bias=stt[:, 1:2], accum_out=stt[:, 2:3]),
             [r, stt], [junk, stt])
        S.op("act", lambda e: e.activation(out=stt[:, 3:4], in_=stt[:, 2:3], func=AF.Sqrt, scale=1.0 / D, bias=epsc[:, 0:1]), [stt, epsc], [stt])
        S.op("dve", lambda e: e.reciprocal(out=stt[:, 4:5], in_=stt[:, 3:4]), [stt], [stt])
        S.op("dve", lambda e: e.tensor_scalar(out=out[:, :], in0=r[:, :], scalar1=stt[:, 1:2], scalar2=stt[:, 4:5],
                                              op0=ALU.add, op1=ALU.mult), [r, stt], [out])
        self.tt("pool", out[:, :], out[:, :], g[:, :], ALU.mult, [out, g], [out])
        self.tt("dve", out[:, :], out[:, :], b[:, :], ALU.add, [out, b], [out])

    def phase_post1(self, l):
        c = self.cfg
        S = self.S
        ALPHA = float(8 ** 0.25)
        with ExitStack() as st:
            wo = self.sb(st, "w1_wo", [128, 16, D], BF16)
            for cb in range(4):
                S.dma(wo[:, :, cb * 512:(cb + 1) * 512],
                      self.w_out[l, :, cb * 512:(cb + 1) * 512].rearrange("(kt p) c -> p kt c", p=128),
                      reads=[self.w_out], writes=[wo], q="pool")
            g1 = self.sb(st, "w1_g", [128, D])
            b1 = self.sb(st, "w1_b", [128, D])
            S.dma(g1[:, :], self.ln1_g[l:l + 1, :].to_broadcast([128, D]), reads=[self.ln1_g], writes=[g1])
            S.dma(b1[:, :], self.ln1_b[l:l + 1, :].to_broadcast([128, D]), reads=[self.ln1_b], writes=[b1])
            epsc = self.sb(st, "w1_eps", [128, 1])
            S.op("dve", lambda e: e.memset(epsc[:, :], 1e-5), [], [epsc])
            mt = [self.sb(st, "w1_mt%d" % i, [128, 16, 128], BF16) for i in range(2)]
            xr = [self.sb(st, "w1_xr%d" % i, [128, D]) for i in range(2)]
            rb = [self.sb(st, "w1_r0", [128, D])] * 2
            xo = [self.sb(st, "w1_xo%d" % i, [128, D]) for i in range(2)]
            junk = self.sb(st, "w1_junk", [128, D], BF16)
            stt = [self.sb(st, "w1_st%d" % i, [128, 8]) for i in range(2)]
            pb = [self.ps(st, "w1_pb%d" % i, [128, 512]) for i in range(4)]
            ptr = [self.ps(st, "w1_ptr%d" % i, [128, 512]) for i in range(2)]
            for tt in range(c.NT):
                k = tt % 2
                cols = slice(tt * 128, (tt + 1) * 128)
                S.dma(mt[k][:, :, :], self.mixT[:, cols].rearrange("(kt p) t -> p kt t", p=128), reads=[self.mixT], writes=[mt[k]], q="pool")
                S.dma(xr[k][:, :], self.xres[cols, :], reads=[self.xres], writes=[xr[k]])
                for cb in range(4):
                    for kt in range(16):
                        S.op("pe", lambda e, cb=cb, kt=kt: e.matmul(pb[cb][:, :], lhsT=mt[k][:, kt, :], rhs=wo[:, kt, cb * 512:(cb + 1) * 512],
                                                                    start=(kt == 0), stop=(kt == 15)), reads=[mt[k], wo], writes=[pb[cb]])
                    S.op("dve", lambda e, cb=cb: e.scalar_tensor_tensor(out=rb[k][:, cb * 512:(cb + 1) * 512], in0=xr[k][:, cb * 512:(cb + 1) * 512],
                                                                        scalar=ALPHA, in1=pb[cb][:, :], op0=ALU.mult, op1=ALU.add),
                         [xr[k], pb[cb]], [rb[k]])
                self.layernorm(rb[k], g1, b1, xo[k], junk, stt[k], epsc)
                S.dma(self.xres[cols, :], xo[k][:, :], reads=[xo[k]], writes=[self.xres], q="act")
                self.transpose_tile_to_xT(st, xo[k], tt, ptr, self.xT)
        S.barrier()

    def phase_peer(self, l):
        c = self.cfg
        S = self.S
        NT, NTOK = c.NT, c.NTOK
        ALPHA = float(8 ** 0.25)
        last = (l == c.L - 1)
        with ExitStack() as st:
            kT = self.sb(st, "p1_kT", [128, 16, 128], BF16)
            S.dma(kT[:, :, :], self.peer_keysT[l], reads=[self.peer_keysT], writes=[kT], q="pool")
            wq = [self.sb(st, "p1_wq%d" % i, [128, 16, 128], BF16) for i in range(2)]
            qT = [self.sb(st, "p1_qT%d" % i, [128, 512], BF16) for i in range(2)]
            ssb = [self.sb(st, "p1_s%d" % i, [128, 4, 128]) for i in range(2)]
            pq = [self.ps(st, "p1_pq%d" % i, [128, 512]) for i in range(2)]
            psc = [self.ps(st, "p1_ps%d" % i, [128, 512]) for i in range(2)]
            it = 0
            for hc in range(16):
                w = wq[hc % 2]
                S.dma(w[:, :, :], self.peer_wq[l, :, hc * 128:(hc + 1) * 128].rearrange("(kt p) c -> p kt c", p=128),
                      reads=[self.peer_wq], writes=[w], q="pool")
                for tb0 in range(0, NTOK, 512):
                    tn = min(512, NTOK - tb0)
                    k = it % 2
                    it += 1
                    for kt in range(16):
                        S.op("pe", lambda e, kt=kt: e.matmul(pq[k][:, 0:tn], lhsT=w[:, kt, :], rhs=self.xT[:, kt, tb0:tb0 + tn],
                                                             start=(kt == 0), stop=(kt == 15)), reads=[w, self.xT], writes=[pq[k]])
                    self.evac(qT[k][:, 0:tn], pq[k][:, 0:tn], [pq[k]], [qT[k]])
                    nti = tn // 128
                    for ti in range(nti):
                        S.op("pe", lambda e, ti=ti: e.matmul(psc[k][:, ti * 128:(ti + 1) * 128], lhsT=qT[k][:, ti * 128:(ti + 1) * 128],
                                                             rhs=kT[:, hc, :], start=True, stop=True), reads=[qT[k], kT], writes=[psc[k]])
                    self.evac(ssb[k][:, 0:nti, :], psc[k][:, 0:tn].rearrange("p (a b) -> p a b", b=128), [psc[k]], [ssb[k]])
                    S.dma(self.sc_d[tb0:tb0 + tn, hc * 128:(hc + 1) * 128].rearrange("(a p) k -> p a k", p=128), ssb[k][:, 0:nti, :],
                          reads=[ssb[k]], writes=[self.sc_d], q="act")
        S.barrier()
        with ExitStack() as st:
            g2 = self.sb(st, "p2_g", [128, D])
            b2 = self.sb(st, "p2_b", [128, D])
            S.dma(g2[:, :], self.ln2_g[l:l + 1, :].to_broadcast([128, D]), reads=[self.ln2_g], writes=[g2])
            S.dma(b2[:, :], self.ln2_b[l:l + 1, :].to_broadcast([128, D]), reads=[self.ln2_b], writes=[b2])
            epsc = self.sb(st, "p2_eps", [128, 1])
            S.op("dve", lambda e: e.memset(epsc[:, :], 1e-5), [], [epsc])
            io16 = self.sb(st, "p2_io16", [128, 16])
            S.dma(io16[:, :], self.iota16[:, :], reads=[self.iota16], writes=[io16])
            ssb = self.sb(st, "p2_s", [128, D])
            x1 = self.sb(st, "p2_x1", [128, D])
            ub = [self.sb(st, "p2_ub%d" % i, [128, D]) for i in range(4)]
            acc = self.sb(st, "p2_acc", [128, D])
            junk = self.sb(st, "p2_junk", [128, D])
            junkb = self.sb(st, "p2_junkb", [128, D], BF16)
            xo = self.sb(st, "p2_xo", [128, D])
            stt = self.sb(st, "p2_st", [128, 8])
            v = self.sb(st, "p2_v", [128, 16, 16])
            iu = self.sb(st, "p2_iu", [128, 16, 16], U32)
            i12 = self.sb(st, "p2_i12", [128, 16, 16])
            tmp = self.sb(st, "p2_tmp", [128, 256])
            cand = self.sb(st, "p2_cand", [128, 8, 256])
            scv = self.sb(st, "p2_scv", [128, 8, 16])
            cu = self.sb(st, "p2_cu", [128, 8, 16], U32)
            ca = self.sb(st, "p2_ca", [128, 8, 16], U32)
            cb_ = self.sb(st, "p2_cb", [128, 8, 16], U32)
            caf = self.sb(st, "p2_caf", [128, 8, 16])
            cbf = self.sb(st, "p2_cbf", [128, 8, 16])
            oh = self.sb(st, "p2_oh", [128, 16, 16])
            isel = self.sb(st, "p2_isel", [128, 2, 8, 16])
            eidf = self.sb(st, "p2_eidf", [128, 128])
            eid = self.sb(st, "p2_eid", [128, 128], I32)
            gate = self.sb(st, "p2_gate", [128, 8, 16])
            zs = self.sb(st, "p2_zs", [128, 8])
            hh = self.sb(st, "p2_hh", [128, 128])
            coef = self.sb(st, "p2_coef", [128, 128])
            ptr = [self.ps(st, "p2_ptr%d" % i, [128, 512]) for i in range(2)]
            ui = 0
            for tt in range(NT):
                rows = slice(tt * 128, (tt + 1) * 128)
                S.dma(ssb[:, :], self.sc_d[rows, :], reads=[self.sc_d], writes=[ssb])
                S.dma(x1[:, :], self.xres[rows, :], reads=[self.xres], writes=[x1], q="act")
                for hc in range(16):
                    sv = ssb[:, hc * 128:(hc + 1) * 128]
                    S.op("dve", lambda e: e.max(out=v[:, hc, 0:8], in_=sv), [ssb], [v])
                    S.op("dve", lambda e: e.max_index(out=iu[:, hc, 0:8], in_max=v[:, hc, 0:8], in_values=sv), [ssb, v], [iu])
                    S.op("dve", lambda e: e.match_replace(out=tmp[:, 0:128], in_to_replace=v[:, hc, 0:8], in_values=sv, imm_value=NEG),
                         [ssb, v], [tmp])
                    S.op("dve", lambda e: e.max(out=v[:, hc, 8:16], in_=tmp[:, 0:128]), [tmp], [v])
                    S.op("dve", lambda e: e.max_index(out=iu[:, hc, 8:16], in_max=v[:, hc, 8:16], in_values=tmp[:, 0:128]), [tmp, v], [iu])
                S.op("dve", lambda e: e.tensor_copy(out=i12[:, :, :], in_=iu[:, :, :]), [iu], [i12])
                for h in range(8):
                    cv = cand[:, h, :]
                    S.op("dve", lambda e: e.tensor_tensor(out=cv.rearrange("p (a b) -> p a b", a=16),
                                                          in0=v[:, 2 * h, :].unsqueeze(2).to_broadcast([128, 16, 16]),
                                                          in1=v[:, 2 * h + 1, :].unsqueeze(1).to_broadcast([128, 16, 16]), op=ALU.add),
                         [v], [cand])
                    S.op("dve", lambda e: e.max(out=scv[:, h, 0:8], in_=cv), [cand], [scv])
                    S.op("dve", lambda e: e.max_index(out=cu[:, h, 0:8], in_max=scv[:, h, 0:8], in_values=cv), [cand, scv], [cu])
                    S.op("dve", lambda e: e.match_replace(out=tmp[:, :], in_to_replace=scv[:, h, 0:8], in_values=cv, imm_value=NEG),
                         [cand, scv], [tmp])
                    S.op("dve", lambda e: e.max(out=scv[:, h, 8:16], in_=tmp[:, :]), [tmp], [scv])
                    S.op("dve", lambda e: e.max_index(out=cu[:, h, 8:16], in_max=scv[:, h, 8:16], in_values=tmp[:, :]), [tmp, scv], [cu])
                self.tt("dve", gate[:, :, :], scv[:, :, :], scv[:, :, 0:1].to_broadcast([128, 8, 16]), ALU.subtract, [scv], [gate])
                S.op("act", lambda e: e.activation(out=gate[:, :, :], in_=gate[:, :, :], func=AF.Exp), [gate], [gate])
                S.op("dve", lambda e: e.tensor_reduce(out=zs[:, :], in_=gate[:, :, :], axis=AX.X, op=ALU.add), [gate], [zs])
                S.op("dve", lambda e: e.reciprocal(out=zs[:, :], in_=zs[:, :]), [zs], [zs])
                self.tt("dve", gate[:, :, :], gate[:, :, :], zs[:, :].unsqueeze(2).to_broadcast([128, 8, 16]), ALU.mult, [gate, zs], [gate])
                S.op("dve", lambda e: e.tensor_single_scalar(out=ca[:, :, :], in_=cu[:, :, :], scalar=4, op=ALU.logical_shift_right), [cu], [ca])
                S.op("dve", lambda e: e.tensor_single_scalar(out=cb_[:, :, :], in_=cu[:, :, :], scalar=15, op=ALU.bitwise_and), [cu], [cb_])
                S.op("dve", lambda e: e.tensor_copy(out=caf[:, :, :], in_=ca[:, :, :]), [ca], [caf])
                S.op("dve", lambda e: e.tensor_copy(out=cbf[:, :, :], in_=cb_[:, :, :]), [cb_], [cbf])
                for h in range(8):
                    for half, cf in ((0, caf), (1, cbf)):
                        self.tt("dve", oh[:, :, :], cf[:, h, :].unsqueeze(2).to_broadcast([128, 16, 16]),
                                io16[:, :].unsqueeze(1).to_broadcast([128, 16, 16]), ALU.is_equal, [cf, io16], [oh])
                        self.tt("dve", oh[:, :, :], oh[:, :, :], i12[:, 2 * h + half, :].unsqueeze(1).to_broadcast([128, 16, 16]),
                                ALU.mult, [oh, i12], [oh])
                        S.op("dve", lambda e, half=half: e.tensor_reduce(out=isel[:, half, h, :], in_=oh[:, :, :], axis=AX.X, op=ALU.add),
                             [oh], [isel])
                S.op("dve", lambda e: e.scalar_tensor_tensor(out=eidf[:, :], in0=isel[:, 0, :, :].rearrange("p a b -> p (a b)"), scalar=128.0,
                                                             in1=isel[:, 1, :, :].rearrange("p a b -> p (a b)"), op0=ALU.mult, op1=ALU.add),
                     [isel], [eidf])
                S.op("dve", lambda e: e.tensor_copy(out=eid[:, :], in_=eidf[:, :]), [eidf], [eid])
                for sl in range(128):
                    u_ = ub[ui % 4]
                    ui += 1
                    S.dma(None, None, reads=[eid, self.peer_u[l]], writes=[u_], q="pool",
                          fn=lambda g, u_=u_, sl=sl: g.indirect_dma_start(
                              out=u_[:, :], out_offset=None, in_=self.peer_u[l][:, :],
                              in_offset=bass.IndirectOffsetOnAxis(ap=eid[:, sl:sl + 1], axis=0),
                              ))
                    S.op("dve", lambda e, u_=u_, sl=sl: e.scalar_tensor_tensor(out=junk[:, :], in0=u_[:, :], scalar=1.0, in1=x1[:, :],
                                                                               op0=ALU.mult, op1=ALU.mult, accum_out=hh[:, sl:sl + 1]),
                         [u_, x1], [junk, hh])
                S.op("act", lambda e: e.activation(out=coef[:, :], in_=hh[:, :], func=AF.Gelu), [hh], [coef])
                self.tt("dve", coef[:, :], coef[:, :], gate[:, :, :].rearrange("p a b -> p (a b)"), ALU.mult, [coef, gate], [coef])
                for sl in range(128):
                    u_ = ub[ui % 4]
                    ui += 1
                    S.dma(None, None, reads=[eid, self.peer_v[l]], writes=[u_], q="pool",
                          fn=lambda g, u_=u_, sl=sl: g.indirect_dma_start(
                              out=u_[:, :], out_offset=None, in_=self.peer_v[l][:, :],
                              in_offset=bass.IndirectOffsetOnAxis(ap=eid[:, sl:sl + 1], axis=0),
                              ))
                    if sl == 0:
                        S.op("dve", lambda e, u_=u_: e.tensor_scalar(out=acc[:, :], in0=u_[:, :], scalar1=coef[:, 0:1], scalar2=None, op0=ALU.mult),
                             [u_, coef], [acc])
                    else:
                        S.op("dve", lambda e, u_=u_, sl=sl: e.scalar_tensor_tensor(out=acc[:, :], in0=u_[:, :], scalar=coef[:, sl:sl + 1], in1=acc[:, :],
                                                                                   op0=ALU.mult, op1=ALU.add), [u_, coef, acc], [acc])
                S.op("dve", lambda e: e.scalar_tensor_tensor(out=acc[:, :], in0=x1[:, :], scalar=ALPHA, in1=acc[:, :], op0=ALU.mult, op1=ALU.add),
                     [x1, acc], [acc])
                self.layernorm(acc, g2, b2, xo, junkb, stt, epsc)
                S.dma(self.xres[rows, :], xo[:, :], reads=[xo], writes=[self.xres], q="act")
                if last:
                    S.dma(self.y[rows, :], xo[:, :], reads=[xo], writes=[self.y], q="act")
                else:
                    self.transpose_tile_to_xT(st, xo, tt, ptr, self.xT)
        S.barrier()

    def build(self):
        c = self.cfg
        S = self.S
        self.declare()
        with ExitStack() as gst:
            self.identf = self.sb(gst, "identf", [128, 128])
            self.identb = self.identf
            S.dma(self.identf[:], self.ident[:, :], reads=[self.ident], writes=[self.identf])
            xst = ExitStack()
            self.xT = self.sb(xst, "xT", [128, 16, c.NTOK], BF16)
            self.phase0()
            self.phase_proj(0)
            xst.close()
            S.barrier()
            sg = c.stages
            for l in range(c.L):
                if "b" in sg:
                    self.phase_attn_b(l)
                if "d" in sg:
                    self.phase_attn_d(l)
                if "c" in sg:
                    self.phase_s5(l)
                if "a" in sg:
                    self.phase_gdn(l)
                xst = ExitStack()
                self.xT = self.sb(xst, "xT", [128, 16, c.NTOK], BF16)
                if "w" in sg:
                    self.phase_post1(l)
                if "p" in sg:
                    self.phase_peer(l)
                if l < c.L - 1:
                    self.phase_proj(l + 1)
                xst.close()
                S.barrier()
            S.finish()
        return self.nc


def _bw(d):
    d = np.asarray(d)
    w = ((d >= 0) & (d <= 128)).astype(np.float32)
    w += ((d >= 0) & (d <= 512) & (d % 4 == 0))
    w += ((d >= 0) & (d <= 2048) & (d % 16 == 0))
    return w.astype(np.float32)


def host_consts():
    kl = np.arange(128)[:, None]
    ql = np.arange(512)[None, :]
    bmask = np.stack([_bw(128 * m + ql - kl) for m in range(-3, 13)])
    t8 = np.arange(8)[None, :]
    bs = [_bw(2048 + t8 - kt * 128 - kl) for kt in range(16)]
    bs.append(_bw(t8 - kl) * (kl < 8))
    bmask_s = np.stack(bs)
    dm = []
    for m in (1, 0, -1, -2, -3):
        d = 128 * m + ql - kl
        dm.append(((d >= 0) & (d <= 127)).astype(np.float32))
    dmask = np.stack(dm)
    d0 = 128 + t8 - kl
    d1 = t8 - kl
    dmask_s = np.stack([((d0 >= 0) & (d0 <= 127)).astype(np.float32),
                        ((d1 >= 0) & (d1 <= 127) & (kl < 8)).astype(np.float32)])
    return {"bmask": bmask, "bmask_s": bmask_s, "dmask": dmask, "dmask_s": dmask_s,
            "ident": np.eye(128, dtype=np.float32),
            "rowmask": np.ascontiguousarray((np.arange(128)[:, None] // 32 == np.arange(4)[None, :]).astype(np.float32))}


def host_inputs(cfg, core, inp):
    c = cfg
    pc = core % 4
    xin = np.zeros((c.NTOK, D), np.float32)
    xin[0:c.TP] = inp["x_prompt"][pc]
    for s in range(4):
        xin[c.TP + 32 * s:c.TP + 32 * s + 8] = inp["x_sample"][4 * core + s]
    m = {"xin": xin, "w_in": inp["w_in"][:c.L]}
    m.update(host_consts())
    sl = slice(4 * core, 4 * core + 4)
    L = c.L
    m["cache_b_kT"] = np.ascontiguousarray(inp["cache_b_k"][:L, sl].transpose(0, 1, 3, 4, 2))
    m["cache_b_v"] = np.ascontiguousarray(inp["cache_b_v"][:L, sl].reshape(L, 4, -1, 512))
    m["cache_d_kT"] = np.ascontiguousarray(inp["cache_d_k"][:L, sl].transpose(0, 1, 3, 4, 2))
    m["cache_d_v"] = np.ascontiguousarray(inp["cache_d_v"][:L, sl].reshape(L, 4, 128, 128))
    m["d_sinks"] = np.ascontiguousarray(inp["d_sinks"][:L].reshape(L, 8))
    blk = np.arange(128) // 32
    same = blk[:, None] == blk[None, :]
    ii = np.arange(128)
    m["gmasks"] = np.stack([(same & (ii[:, None] <= ii[None, :])), same, (same & (ii[:, None] > ii[None, :]))]).astype(np.float32)
    valid = np.ones((128, c.NT), np.float32)
    valid[:, c.NT - 1] = (np.arange(128) % 32 < 8)
    m["valid"] = valid
    m["a_cw"] = np.ascontiguousarray(inp["a_conv_w"][:L].reshape(L, 4, 12, 128).transpose(0, 3, 2, 1))
    m["a_cc"] = np.ascontiguousarray(inp["cache_a_conv"][:L, sl].reshape(L, 4, 3, 12, 128).transpose(0, 4, 3, 1, 2))
    m["a_par"] = np.ascontiguousarray(np.concatenate([inp["a_log"][:L], inp["a_dt_bias"][:L]], axis=1))
    m["a_nw"] = inp["a_norm_w"][:L]
    m["state_a"] = np.ascontiguousarray(inp["state_a"][:L, sl])
    def st_layout(a):
        sh = a.shape
        a = a.reshape((sh[0], 16, 2, 64) + sh[3:])
        a = np.moveaxis(a, 1, 3)
        return np.ascontiguousarray(a.reshape((sh[0], 128, 16) + sh[3:]))
    m["c_lre"] = st_layout(inp["c_lambda_re"][:L])
    m["c_lim"] = st_layout(inp["c_lambda_im"][:L])
    m["c_lst"] = st_layout(np.repeat(inp["c_log_step"][:L, :, None], 64, axis=2))
    def blockdiag(a):
        o = np.zeros((L, 2, 64, 16, 2, 16), np.float32)
        ar = a.reshape(L, 16, 2, 64, 16)
        for gl in range(2):
            o[:, gl, :, :, gl, :] = np.moveaxis(ar[:, :, gl], 1, 2)
        return np.ascontiguousarray(o.reshape(L, 128, 16, 32))
    m["c_bsd_re"] = blockdiag(inp["c_b_re"][:L])
    m["c_bsd_im"] = blockdiag(inp["c_b_im"][:L])
    m["c_csd_re"] = blockdiag(np.swapaxes(inp["c_c_re"][:L], 2, 3))
    m["c_csd_im"] = blockdiag(np.swapaxes(inp["c_c_im"][:L], 2, 3))
    m["c_dT"] = np.ascontiguousarray(inp["c_d"][:L].reshape(L, 4, 128).transpose(0, 2, 1))
    m["c_glu_w"] = inp["c_glu_w"][:L]
    m["c_glu_bT"] = np.ascontiguousarray(inp["c_glu_b"][:L].reshape(L, 8, 128).transpose(0, 2, 1))
    m["w_out"] = inp["w_out"][:L]
    for k in ("ln1_g", "ln1_b", "ln2_g", "ln2_b", "peer_wq"):
        m[k] = inp[k][:L]
    for i in range(L):
        m["peer_u%d" % i] = inp["peer_u"][i]
        m["peer_v%d" % i] = inp["peer_v"][i]
    m["peer_keysT"] = np.ascontiguousarray(inp["peer_sub_keys"][:L].reshape(L, 16, 128, 128).transpose(0, 3, 1, 2))
    m["iota16"] = np.ascontiguousarray(np.broadcast_to(np.arange(16, dtype=np.float32)[None, :], (128, 16)))
    m["c_h0re"] = np.ascontiguousarray(np.moveaxis(st_layout(np.moveaxis(inp["state_c_re"][:L, sl], 1, 3)), 3, 3))
    m["c_h0im"] = np.ascontiguousarray(np.moveaxis(st_layout(np.moveaxis(inp["state_c_im"][:L, sl], 1, 3)), 3, 3))
    return m


def unstate(a):
    sh = a.shape
    a = a.reshape((2, 64, 16) + sh[2:])
    a = np.moveaxis(a, 2, 0)
    return np.ascontiguousarray(a.reshape((32, 64) + sh[2:]))


def kernel(**inp):
    cfg = Cfg()
    b = Builder(cfg)
    nc = b.build()
    inp = {k: np.ascontiguousarray(np.asarray(v)) for k, v in inp.items()}
    in_maps = [host_inputs(cfg, core, inp) for core in range(8)]
    res = run_bass_kernel_spmd(nc, in_maps, core_ids=list(range(8))).results
    L, TP = cfg.L, cfg.TP
    f = np.float32
    y_p = np.zeros((4, TP, D), f)
    y_s = np.zeros((32, 8, D), f)
    p_a_conv = np.zeros((L, 4, 3, 1536), f)
    p_a_state = np.zeros((L, 4, 4, 128, 128), f)
    p_b_k = np.zeros((L, 4, TP, 8, 64), f)
    p_b_v = np.zeros((L, 4, TP, 8, 64), f)
    p_c_re = np.zeros((L, 4, 32, 64), f)
    p_c_im = np.zeros((L, 4, 32, 64), f)
    p_d_k = np.zeros((L, 4, 128, 2, 64), f)
    p_d_v = np.zeros((L, 4, 128, 2, 64), f)
    s_a_conv = np.zeros((L, 32, 3, 1536), f)
    s_a_state = np.zeros((L, 32, 4, 128, 128), f)
    s_b_k = np.zeros((L, 32, 8, 8, 64), f)
    s_b_v = np.zeros((L, 32, 8, 8, 64), f)
    s_c_re = np.zeros((L, 32, 32, 64), f)
    s_c_im = np.zeros((L, 32, 32, 64), f)
    s_d_k = np.zeros((L, 32, 8, 2, 64), f)
    s_d_v = np.zeros((L, 32, 8, 2, 64), f)
    for core in range(8):
        r = {k: np.asarray(v) for k, v in res[core].items()}
        sl = slice(4 * core, 4 * core + 4)
        for s_ in range(4):
            y_s[4 * core + s_] = r["y"][TP + 32 * s_:TP + 32 * s_ + 8]
        s_a_conv[:, sl] = np.swapaxes(r["s_a_convT"], 2, 3)
        s_a_state[:, sl] = r["s_a_state"]
        s_b_k[:, sl] = r["s_b_k"].reshape(L, 4, 8, 8, 64)
        s_b_v[:, sl] = r["s_b_v"].reshape(L, 4, 8, 8, 64)
        s_d_k[:, sl] = r["s_d_k"].reshape(L, 4, 8, 2, 64)
        s_d_v[:, sl] = r["s_d_v"].reshape(L, 4, 8, 2, 64)
        for l in range(L):
            s_c_re[l, sl] = np.moveaxis(unstate(r["s_c_re"][l]), 2, 0)
            s_c_im[l, sl] = np.moveaxis(unstate(r["s_c_im"][l]), 2, 0)
        if core < 4:
            pc = core
            y_p[pc] = r["y"][0:TP]
            p_a_conv[:, pc] = np.swapaxes(r["p_a_convT"], 1, 2)
            p_a_state[:, pc] = r["p_a_state"]
            p_b_k[:, pc] = r["p_b_k"].reshape(L, TP, 8, 64)
            p_b_v[:, pc] = r["p_b_v"].reshape(L, TP, 8, 64)
            p_d_k[:, pc] = r["p_d_k"].reshape(L, 128, 2, 64)
            p_d_v[:, pc] = r["p_d_v"].reshape(L, 128, 2, 64)
            for l in range(L):
                p_c_re[l, pc] = unstate(r["p_c_re"][l])
                p_c_im[l, pc] = unstate(r["p_c_im"][l])
    return (y_p, y_s, p_a_conv, p_a_state, p_b_k, p_b_v, p_c_re, p_c_im, p_d_k, p_d_v,
            s_a_conv, s_a_state, s_b_k, s_b_v, s_c_re, s_c_im, s_d_k, s_d_v)
```

```python
import numpy as np
from contextlib import ExitStack
import concourse.bass as bass
import concourse.mybir as mybir
from concourse.bass_types import AP
from concourse.bass_utils import run_bass_kernel_spmd

F32 = mybir.dt.float32
BF16 = mybir.dt.bfloat16
I32 = mybir.dt.int32
U32 = mybir.dt.uint32
AF = mybir.ActivationFunctionType
ALU = mybir.AluOpType
AX = mybir.AxisListType

D = 2048
INC = 4872
NEG = -1.0e30


class Buf:
    __slots__ = ("t", "w", "r", "name")

    ALL = []

    def __init__(self, t, name=""):
        self.t = t
        self.w = None
        self.r = []
        self.name = name
        Buf.ALL.append(self)

    def __getitem__(self, idx):
        return self.t[idx]


class Sub:
    __slots__ = ("t", "p", "name")

    def __init__(self, t, parent, name=""):
        self.t = t
        self.p = parent
        self.name = name

    @property
    def w(self):
        return self.p.w

    @w.setter
    def w(self, v):
        self.p.w = v

    @property
    def r(self):
        return self.p.r

    @r.setter
    def r(self, v):
        self.p.r = v

    def __getitem__(self, idx):
        return self.t[idx]


class Sched:
    NDMA = 4

    def __init__(self, nc):
        self.nc = nc
        self.eng = {"pe": nc.tensor, "dve": nc.vector, "act": nc.scalar, "pool": nc.gpsimd, "sp": nc.sync}
        self.sem = {}
        self.cnt = {}
        for k in ("pe", "dve", "act", "pool"):
            self.sem[k] = nc.alloc_semaphore("sem_" + k)
            self.cnt[k] = 0
        self.dq = {}
        self.nslot = {"sp": 4, "pool": 8, "act": 4}
        for q in ("sp", "pool", "act"):
            self.dq[q] = 0
            for s in range(self.nslot[q]):
                k = ("dma", q, s)
                self.sem[k] = nc.alloc_semaphore("sem_dma_%s_%d" % (q, s))
                self.cnt[k] = 0
        self.seen = {e: {} for e in self.eng}
        self.n_inst = 0

    def _wait(self, e, key, val):
        if val <= 0 or self.seen[e].get(key, 0) >= val:
            return
        self.eng[e].wait_ge(self.sem[key], val)
        self.seen[e][key] = val
        self.n_inst += 1

    def _deps(self, e, reads, writes, skip_same=None):
        deps = {}
        for b in reads:
            if b.w is not None:
                k, v = b.w
                deps[k] = max(deps.get(k, 0), v)
        for b in writes:
            if b.w is not None:
                k, v = b.w
                deps[k] = max(deps.get(k, 0), v)
            for (k, v) in b.r:
                deps[k] = max(deps.get(k, 0), v)
        for k, v in deps.items():
            if skip_same is not None and k == skip_same:
                continue
            self._wait(e, k, v)

    def _mark(self, key, val, reads, writes):
        for b in writes:
            b.w = (key, val)
            b.r = []
        wroots = [getattr(b, "p", b) for b in writes]
        for b in reads:
            if any(getattr(b, "p", b) is wr for wr in wroots):
                continue
            b.r = [(k, v) for (k, v) in b.r if k != key] + [(key, val)]

    def op(self, e, fn, reads=(), writes=()):
        ex = [b for b in reads if isinstance(b, Sub)]
        if ex:
            reads = [b for b in reads if not isinstance(b, Sub)]
            writes = list(writes) + ex
        self._deps(e, reads, writes, skip_same=("pe" if e == "pe" else None))
        ins = fn(self.eng[e])
        self.cnt[e] += 1
        ins.then_inc(self.sem[e], 1)
        self._mark(e, self.cnt[e], reads, writes)
        self.n_inst += 1
        return ins

    def dma(self, out_ap, in_ap, reads=(), writes=(), q="sp", fn=None, **kw):
        slot = self.dq[q] % self.nslot[q]
        self.dq[q] += 1
        key = ("dma", q, slot)
        self._wait(q, key, self.cnt[key])
        self._deps(q, reads, writes)
        if fn is not None:
            ins = fn(self.eng[q])
        else:
            ins = self.eng[q].dma_start(out=out_ap, in_=in_ap, **kw)
        self.cnt[key] += 16
        ins.then_inc(self.sem[key], 16)
        self._mark(key, self.cnt[key], reads, writes)
        self.n_inst += 1
        return ins

    def barrier(self):
        keys = [k for k in self.cnt if self.cnt[k] > 0]
        for e in self.eng:
            for k in keys:
                self._wait(e, k, self.cnt[k])

    def new_epoch(self):
        self.ep = getattr(self, "ep", 0) + 1
        for k in list(self.sem.keys()):
            nm = k if isinstance(k, str) else "dma_%s_%d" % (k[1], k[2])
            self.sem[k] = self.nc.alloc_semaphore("sem_%s_e%d" % (nm, self.ep))
            self.cnt[k] = 0
        self.seen = {e: {} for e in self.eng}
        for b in Buf.ALL:
            b.w = None
            b.r = []

    def finish(self):
        for k in self.cnt:
            if self.cnt[k] > 0:
                self._wait("sp", k, self.cnt[k])


class Cfg:
    def __init__(self, TP=2048, L=4, stages="abcdwp", debug=False, gdn_level=9, intra_stop=99, chain_steps=4, epoch_at=(2,)):
        self.debug = debug
        self.TP = TP
        self.L = L
        self.NTP = TP // 128
        self.NT = self.NTP + 1
        self.NTOK = self.NT * 128
        self.stages = stages
        self.gdn_level = gdn_level
        self.intra_stop = intra_stop
        self.chain_steps = chain_steps
        self.epoch_at = epoch_at


FQKV, FZ, FBQ, FBK, FCU, FDQ, FDK, FROWS = 0, 1536, 2048, 2560, 3072, 3584, 4096, 4224
TAB, TBK, TBV, TDK, TDV, TCOLS = 0, 8, 520, 1032, 1160, 1288
GROUPS = [
    (0, 2048, FQKV, None),
    (2048, 8, None, TAB),
    (2056, 512, FBQ, None),
    (2568, 512, FBK, TBK),
    (3080, 512, None, TBV),
    (3592, 512, FCU, None),
    (4104, 512, FDQ, None),
    (4616, 128, FDK, TDK),
    (4744, 128, None, TDV),
]


class Builder:
    def __init__(self, cfg):
        self.cfg = cfg
        self.nc = bass.Bass("TRN2", target_bir_lowering=False)
        Buf.ALL = []
        self.S = Sched(self.nc)
        self.ins = {}
        self.outs = {}
        self._evac_i = 0
        self._uid = 0

    def din(self, name, shape, dt=F32):
        t = self.nc.dram_tensor(name, list(shape), dt, kind="ExternalInput")
        b = Buf(t.ap(), name)
        self.ins[name] = b
        return b

    def dout(self, name, shape, dt=F32):
        t = self.nc.dram_tensor(name, list(shape), dt, kind="ExternalOutput")
        b = Buf(t.ap(), name)
        self.outs[name] = b
        return b

    def dscr(self, name, shape, dt=F32):
        kind = "ExternalOutput" if self.cfg.debug else "Internal"
        t = self.nc.dram_tensor(name, list(shape), dt, kind=kind)
        return Buf(t.ap(), name)

    def sb(self, st, name, shape, dt=F32):
        self._uid += 1
        name = "sb%d_%s" % (self._uid, name)
        return Buf(st.enter_context(self.nc.sbuf_tensor(name, list(shape), dt)), name)

    def ps(self, st, name, shape, dt=F32):
        self._uid += 1
        name = "ps%d_%s" % (self._uid, name)
        return Buf(st.enter_context(self.nc.psum_tensor(name, list(shape), dt)), name)

    def evac(self, out_ap, in_ap, reads, writes, eng=None):
        if eng is None:
            eng = "act" if (self._evac_i % 2 == 0) else "dve"
            self._evac_i += 1
        if eng == "act":
            return self.S.op("act", lambda e: e.activation(out=out_ap, in_=in_ap, func=AF.Copy), reads, writes)
        return self.S.op(eng, lambda e: e.tensor_copy(out=out_ap, in_=in_ap), reads, writes)

    def declare(self):
        c = self.cfg
        L, NTOK, TP = c.L, c.NTOK, c.TP
        self.xin = self.din("xin", [NTOK, D])
        self.ident = self.din("ident", [128, 128])
        self.w_in = self.din("w_in", [L, D, INC])
        self.y = self.dout("y", [NTOK, D])
        self.o_pbk = self.dout("p_b_k", [L, TP, 512])
        self.o_pbv = self.dout("p_b_v", [L, TP, 512])
        self.o_pdk = self.dout("p_d_k", [L, 128, 128])
        self.o_pdv = self.dout("p_d_v", [L, 128, 128])
        self.o_sbk = self.dout("s_b_k", [L, 4, 8, 512])
        self.o_sbv = self.dout("s_b_v", [L, 4, 8, 512])
        self.o_sdk = self.dout("s_d_k", [L, 4, 8, 128])
        self.o_sdv = self.dout("s_d_v", [L, 4, 8, 128])
        self.bmask = self.din("bmask", [16, 128, 512])
        self.bmask_s = self.din("bmask_s", [17, 128, 8])
        self.dmask = self.din("dmask", [5, 128, 512])
        self.dmask_s = self.din("dmask_s", [2, 128, 8])
        self.cache_b_kT = self.din("cache_b_kT", [L, 4, 8, 64, 2048])
        self.cache_b_v = self.din("cache_b_v", [L, 4, 2048, 512])
        self.cache_d_kT = self.din("cache_d_kT", [L, 4, 2, 64, 128])
        self.cache_d_v = self.din("cache_d_v", [L, 4, 128, 128])
        self.d_sinks = self.din("d_sinks", [L, 8])
        self.rowmask = self.din("rowmask", [128, 4])
        self.gmasks = self.din("gmasks", [3, 128, 128])
        self.valid = self.din("valid", [128, c.NT])
        self.a_cw = self.din("a_cw", [L, 128, 12, 4])
        self.a_cc = self.din("a_cc", [L, 128, 12, 4, 3])
        self.a_par = self.din("a_par", [L, 8])
        self.a_nw = self.din("a_nw", [L, 128])
        self.state_a = self.din("state_a", [L, 4, 4, 128, 128])
        self.o_paconv = self.dout("p_a_convT", [L, 1536, 3])
        self.o_saconv = self.dout("s_a_convT", [L, 4, 1536, 3])
        self.o_pastate = self.dout("p_a_state", [L, 4, 128, 128])
        self.o_sastate = self.dout("s_a_state", [L, 4, 4, 128, 128])
        self.c_lre = self.din("c_lre", [L, 128, 16])
        self.c_lim = self.din("c_lim", [L, 128, 16])
        self.c_lst = self.din("c_lst", [L, 128, 16])
        self.c_bsd_re = self.din("c_bsd_re", [L, 128, 16, 32])
        self.c_bsd_im = self.din("c_bsd_im", [L, 128, 16, 32])
        self.c_csd_re = self.din("c_csd_re", [L, 128, 16, 32])
        self.c_csd_im = self.din("c_csd_im", [L, 128, 16, 32])
        self.c_dT = self.din("c_dT", [L, 128, 4])
        self.c_glu_w = self.din("c_glu_w", [L, 512, 1024])
        self.c_glu_bT = self.din("c_glu_bT", [L, 128, 8])
        self.c_h0re = self.din("c_h0re", [L, 128, 16, 4])
        self.c_h0im = self.din("c_h0im", [L, 128, 16, 4])
        self.o_pcre = self.dout("p_c_re", [L, 128, 16])
        self.o_pcim = self.dout("p_c_im", [L, 128, 16])
        self.o_scre = self.dout("s_c_re", [L, 128, 16, 4])
        self.o_scim = self.dout("s_c_im", [L, 128, 16, 4])
        self.w_out = self.din("w_out", [L, D, D])
        self.ln1_g = self.din("ln1_g", [L, D])
        self.ln1_b = self.din("ln1_b", [L, D])
        self.ln2_g = self.din("ln2_g", [L, D])
        self.ln2_b = self.din("ln2_b", [L, D])
        self.peer_wq = self.din("peer_wq", [L, D, D])
        self.peer_keysT = self.din("peer_keysT", [L, 128, 16, 128])
        self.peer_u = [self.din("peer_u%d" % i, [16384, D]) for i in range(L)]
        self.peer_v = [self.din("peer_v%d" % i, [16384, D]) for i in range(L)]
        self.iota16 = self.din("iota16", [128, 16])
        self.sc_d = self.dscr("sc_d", [NTOK, D])
        self.uv16 = self.dscr("uv16", [16384, 2 * D], BF16)
        self.mixT = self.dscr("mixT", [2048, NTOK])
        self.featT = self.dscr("featT", [FROWS, NTOK])
        self.tokM = self.dscr("tokM", [NTOK, TCOLS])
        self.xres = self.dscr("xres", [NTOK, D])

    def transpose_tile_to_xT(self, st, xt_tile, tt, ptr, xT):
        S = self.S
        for g in range(4):
            p = ptr[g % 2]
            for j in range(4):
                kt = g * 4 + j
                S.op("pe", lambda e, kt=kt, j=j, p=p: e.transpose(out=p[:, j * 128:(j + 1) * 128],
                                                                   in_=xt_tile[:, kt * 128:(kt + 1) * 128],
                                                                   identity=self.identb[:]),
                     reads=[xt_tile, self.identb], writes=[p])
            o = xT[:, g * 4:(g + 1) * 4, tt * 128:(tt + 1) * 128]
            i = p[:, :].rearrange("p (a b) -> p a b", a=4)
            self.evac(o, i, [p], [xT])

    def phase0(self):
        c = self.cfg
        S = self.S
        with ExitStack() as st:
            xt = [self.sb(st, "p0_xt%d" % i, [128, D]) for i in range(2)]
            ptr = [self.ps(st, "p0_ptr%d" % i, [128, 512]) for i in range(2)]
            zt = self.sb(st, "p0_zero", [128, 128])
            S.op("pool", lambda e: e.memset(zt[:, :], 0.0), [], [zt])
            for kt in range(16):
                S.dma(self.mixT[kt * 128:(kt + 1) * 128, c.TP:c.NTOK], zt[:, :], reads=[zt], writes=[self.mixT], q="pool")
            for tt in range(c.NT):
                b = xt[tt % 2]
                S.dma(b[:], self.xin[tt * 128:(tt + 1) * 128, :], reads=[self.xin], writes=[b])
                S.dma(self.xres[tt * 128:(tt + 1) * 128, :], b[:], reads=[b], writes=[self.xres], q="act")
                self.transpose_tile_to_xT(st, b, tt, ptr, self.xT)
        S.barrier()

    def phase_proj(self, l):
        c = self.cfg
        S = self.S
        NTOK = c.NTOK
        tblocks = []
        t0 = 0
        while t0 < NTOK:
            n = min(512, NTOK - t0)
            tblocks.append((t0, n))
            t0 += n
        with ExitStack() as st:
            wb = [self.sb(st, "pj_w%d" % i, [128, 16, 512], BF16) for i in range(2)]
            stg = [self.sb(st, "pj_stg%d" % i, [128, NTOK]) for i in range(2)]
            stm = [self.sb(st, "pj_stm%d" % i, [128, 512]) for i in range(2)]
            pp = [self.ps(st, "pj_ps%d" % i, [128, 512]) for i in range(4)]
            wi = 0
            si = 0
            mi = 0
            pi = 0
            for (c0, ncols, frow, tcol) in GROUPS:
                for b0 in range(0, ncols, 512):
                    nb = min(512, ncols - b0)
                    w = wb[wi % 2]
                    wi += 1
                    src = self.w_in[l, :, c0 + b0:c0 + b0 + nb].rearrange("(kt p) c -> p kt c", p=128)
                    S.dma(w[:, :, 0:nb], src, reads=[self.w_in], writes=[w], q="pool")
                    if frow is not None:
                        for ct in range(0, nb, 128):
                            m = min(128, nb - ct)
                            sg = stg[si % 2]
                            si += 1
                            for (tb0, tn) in tblocks:
                                p = pp[pi % 4]
                                pi += 1
                                for kt in range(16):
                                    S.op("pe", lambda e, p=p, w=w, kt=kt, ct=ct, m=m, tb0=tb0, tn=tn: e.matmul(
                                        p[0:m, 0:tn], lhsT=w[:, kt, ct:ct + m], rhs=self.xT[:, kt, tb0:tb0 + tn],
                                        start=(kt == 0), stop=(kt == 15)), reads=[w, self.xT], writes=[p])
                                self.evac(sg[0:m, tb0:tb0 + tn], p[0:m, 0:tn], [p], [sg])
                            r0 = frow + b0 + ct
                            S.dma(self.featT[r0:r0 + m, :], sg[0:m, :], reads=[sg], writes=[self.featT])
                    if tcol is not None:
                        for tt in range(c.NT):
                            p = pp[pi % 4]
                            pi += 1
                            for kt in range(16):
                                S.op("pe", lambda e, p=p, w=w, kt=kt, tt=tt, nb=nb: e.matmul(
                                    p[:, 0:nb], lhsT=self.xT[:, kt, tt * 128:(tt + 1) * 128], rhs=w[:, kt, 0:nb],
                                    start=(kt == 0), stop=(kt == 15)), reads=[w, self.xT], writes=[p])
                            sm = stm[mi % 2]
                            mi += 1
                            self.evac(sm[:, 0:nb], p[:, 0:nb], [p], [sm])
                            S.dma(self.tokM[tt * 128:(tt + 1) * 128, tcol + b0:tcol + b0 + nb], sm[:, 0:nb],
                                  reads=[sm], writes=[self.tokM], q="act")
        S.barrier()
        TP = c.TP
        S.dma(self.o_pbk[l, :, :], self.tokM[0:TP, TBK:TBK + 512], reads=[self.tokM], writes=[self.o_pbk])
        S.dma(self.o_pbv[l, :, :], self.tokM[0:TP, TBV:TBV + 512], reads=[self.tokM], writes=[self.o_pbv])
        S.dma(self.o_pdk[l, :, :], self.tokM[TP - 128:TP, TDK:TDK + 128], reads=[self.tokM], writes=[self.o_pdk])
        S.dma(self.o_pdv[l, :, :], self.tokM[TP - 128:TP, TDV:TDV + 128], reads=[self.tokM], writes=[self.o_pdv])
        for s in range(4):
            r0 = TP + 32 * s
            S.dma(self.o_sbk[l, s, :, :], self.tokM[r0:r0 + 8, TBK:TBK + 512], reads=[self.tokM], writes=[self.o_sbk])
            S.dma(self.o_sbv[l, s, :, :], self.tokM[r0:r0 + 8, TBV:TBV + 512], reads=[self.tokM], writes=[self.o_sbv])
            S.dma(self.o_sdk[l, s, :, :], self.tokM[r0:r0 + 8, TDK:TDK + 128], reads=[self.tokM], writes=[self.o_sdk])
            S.dma(self.o_sdv[l, s, :, :], self.tokM[r0:r0 + 8, TDV:TDV + 128], reads=[self.tokM], writes=[self.o_sdv])

    def attn_core(self, qT_ap, q_reads, nq, keys, num_ps, den_ps, pr0, scale, W):
        S = self.S
        n = len(keys)
        for i, (kT_ap, nk, v_ap, mask_ap, rd) in enumerate(keys):
            sp = W["sp"][self._ai % 2]
            pe_ = W["pe"][self._ai % 2]
            pm = W["pm"][self._ai % 2]
            self._ai += 1
            S.op("pe", lambda e: e.matmul(sp[0:nk, 0:nq], lhsT=kT_ap, rhs=qT_ap, start=True, stop=True),
                 reads=list(rd) + list(q_reads), writes=[sp])
            S.op("act", lambda e: e.activation(out=pe_[0:nk, 0:nq], in_=sp[0:nk, 0:nq], func=AF.Exp, scale=scale),
                 reads=[sp], writes=[pe_])
            meng = "dve" if (self._ai % 2 == 0) else "pool"
            S.op(meng, lambda e: e.tensor_tensor(out=pm[0:nk, 0:nq], in0=pe_[0:nk, 0:nq], in1=mask_ap, op=ALU.mult),
                 reads=[pe_, W["maskbuf"]], writes=[pm])
            S.op("pe", lambda e: e.matmul(num_ps[pr0:pr0 + 64, 0:nq], lhsT=v_ap, rhs=pm[0:nk, 0:nq],
                                          start=(i == 0), stop=(i == n - 1)),
                 reads=list(rd) + [pm], writes=[num_ps])
            S.op("pe", lambda e: e.matmul(den_ps[pr0:pr0 + 64, 0:nq], lhsT=W["ones"][0:nk, 0:64], rhs=pm[0:nk, 0:nq],
                                          start=(i == 0), stop=(i == n - 1)),
                 reads=[W["ones"], pm], writes=[den_ps])

    def attn_finish(self, num_ps, den_ps, pr0, nq, sink_ap, sink_reads, W, dst_ap, dst_buf):
        S = self.S
        rc = W["rc"][self._fi % 2]
        og = W["og"][self._fi % 2]
        self._fi += 1
        if sink_ap is not None:
            S.op("dve", lambda e: e.tensor_scalar(out=rc[pr0:pr0 + 64, 0:nq], in0=den_ps[pr0:pr0 + 64, 0:nq],
                                                  scalar1=sink_ap, scalar2=None, op0=ALU.add),
                 reads=[den_ps] + list(sink_reads), writes=[rc])
            S.op("dve", lambda e: e.reciprocal(out=rc[pr0:pr0 + 64, 0:nq], in_=rc[pr0:pr0 + 64, 0:nq]),
                 reads=[rc], writes=[rc])
        else:
            S.op("dve", lambda e: e.reciprocal(out=rc[pr0:pr0 + 64, 0:nq], in_=den_ps[pr0:pr0 + 64, 0:nq]),
                 reads=[den_ps], writes=[rc])
        S.op("dve", lambda e: e.tensor_tensor(out=og[pr0:pr0 + 64, 0:nq], in0=num_ps[pr0:pr0 + 64, 0:nq],
                                              in1=rc[pr0:pr0 + 64, 0:nq], op=ALU.mult),
             reads=[num_ps, rc], writes=[og])
        S.dma(dst_ap, og[pr0:pr0 + 64, 0:nq], reads=[og], writes=[dst_buf])

    def attn_work(self, st, pfx):
        W = {}
        W["sp"] = [self.ps(st, pfx + "sp%d" % i, [128, 512]) for i in range(2)]
        W["pe"] = [self.sb(st, pfx + "pe%d" % i, [128, 512], BF16) for i in range(2)]
        W["pm"] = [self.sb(st, pfx + "pm%d" % i, [128, 512], BF16) for i in range(2)]
        W["rc"] = [self.sb(st, pfx + "rc%d" % i, [128, 512]) for i in range(2)]
        W["og"] = [self.sb(st, pfx + "og%d" % i, [128, 512]) for i in range(2)]
        W["num"] = [self.ps(st, pfx + "num%d" % i, [128, 512]) for i in range(2)]
        W["den"] = [self.ps(st, pfx + "den%d" % i, [128, 512]) for i in range(2)]
        ones = self.sb(st, pfx + "ones", [128, 64], BF16)
        self.S.op("pool", lambda e: e.memset(ones[:], 1.0), writes=[ones])
        W["ones"] = ones
        return W

    def phase_attn_b(self, l):
        c = self.cfg
        S = self.S
        TP, NTP = c.TP, c.NTP
        self._ai = 0
        self._fi = 0
        with ExitStack() as st:
            W = self.attn_work(st, "ab_")
            qT = self.sb(st, "ab_qT", [64, 8, c.NTOK], BF16)
            kT = self.sb(st, "ab_kT", [64, 8, c.NTOK], BF16)
            V = self.sb(st, "ab_V", [128, c.NT, 512], BF16)
            mk = self.sb(st, "ab_mk", [128, 16, 512], BF16)
            mks = self.sb(st, "ab_mks", [128, 17, 8], BF16)
            W["maskbuf"] = mk
            for h in range(8):
                S.dma(qT[:, h, :], self.featT[FBQ + 64 * h:FBQ + 64 * h + 64, :], reads=[self.featT], writes=[qT], q="pool")
                S.dma(kT[:, h, :], self.featT[FBK + 64 * h:FBK + 64 * h + 64, :], reads=[self.featT], writes=[kT], q="pool")
            S.dma(V[:, :, :], self.tokM[:, TBV:TBV + 512].rearrange("(t p) c -> p t c", p=128),
                  reads=[self.tokM], writes=[V], q="pool")
            S.dma(mk[:, :, :], self.bmask[:, :, :].rearrange("m p q -> p m q"), reads=[self.bmask], writes=[mk], q="pool")
            S.dma(mks[:, :, :], self.bmask_s[:, :, :].rearrange("m p q -> p m q"), reads=[self.bmask_s], writes=[mks], q="pool")
            for qb in range(0, TP, 512):
                nq = min(512, TP - qb)
                for h in range(8):
                    num = W["num"][h % 2]
                    den = W["den"][h % 2]
                    keys = []
                    for kt in range(0, (qb + nq) // 128):
                        m = (qb - kt * 128) // 128
                        keys.append((kT[:, h, kt * 128:(kt + 1) * 128], 128, V[:, kt, 64 * h:64 * h + 64],
                                     mk[:, m + 3, 0:nq], [kT, V]))
                    self.attn_core(qT[:, h, qb:qb + nq], [qT], nq, keys, num, den, 0, 0.125, W)
                    r0 = 512 + 64 * h
                    self.attn_finish(num, den, 0, nq, None, [], W, self.mixT[r0:r0 + 64, qb:qb + nq], self.mixT)
            W["maskbuf"] = mks
            with ExitStack() as st2:
                Vn = self.sb(st2, "ab_Vn", [8, 4, 512], BF16)
                for s in range(4):
                    S.dma(Vn[:, s, :], self.tokM[TP + 32 * s:TP + 32 * s + 8, TBV:TBV + 512], reads=[self.tokM], writes=[Vn], q="pool")
                ckT = [self.sb(st2, "ab_ckT%d" % i, [64, 8, 2048], BF16) for i in range(2)]
                cV = [self.sb(st2, "ab_cV0", [128, 16, 512], BF16)] * 2
                for s in range(4):
                    ck = ckT[s % 2]
                    cv = cV[s % 2]
                    S.dma(ck[:, :, :], self.cache_b_kT[l, s].rearrange("h d r -> d h r"), reads=[self.cache_b_kT], writes=[ck], q="pool")
                    S.dma(cv[:, :, :], self.cache_b_v[l, s].rearrange("(t p) c -> p t c", p=128), reads=[self.cache_b_v], writes=[cv], q="pool")
                    q0 = TP + 32 * s
                    for h in range(8):
                        num = W["num"][h % 2]
                        den = W["den"][h % 2]
                        keys = []
                        for kt in range(16):
                            keys.append((ck[:, h, kt * 128:(kt + 1) * 128], 128, cv[:, kt, 64 * h:64 * h + 64],
                                         mks[:, kt, :], [ck, cv]))
                        keys.append((kT[:, h, q0:q0 + 8], 8, Vn[0:8, s, 64 * h:64 * h + 64],
                                     mks[0:8, 16, :], [kT, Vn]))
                        self.attn_core(qT[:, h, q0:q0 + 8], [qT], 8, keys, num, den, 0, 0.125, W)
                        r0 = 512 + 64 * h
                        self.attn_finish(num, den, 0, 8, None, [], W, self.mixT[r0:r0 + 64, q0:q0 + 8], self.mixT)
        S.barrier()

    def phase_attn_d(self, l):
        c = self.cfg
        S = self.S
        TP, NTP = c.TP, c.NTP
        self._ai = 0
        self._fi = 0
        with ExitStack() as st:
            W = self.attn_work(st, "ad_")
            qT = self.sb(st, "ad_qT", [64, 8, c.NTOK], BF16)
            kT = self.sb(st, "ad_kT", [64, 2, c.NTOK], BF16)
            V = self.sb(st, "ad_V", [128, c.NT, 128], BF16)
            Vn = self.sb(st, "ad_Vn", [8, 4, 128], BF16)
            mk = self.sb(st, "ad_mk", [128, 5, 512], BF16)
            mks = self.sb(st, "ad_mks", [128, 2, 8], BF16)
            ckT = self.sb(st, "ad_ckT", [64, 4, 2, 128], BF16)
            cV = self.sb(st, "ad_cV", [128, 4, 128], BF16)
            snk = self.sb(st, "ad_snk", [64, 8])
            esk = self.sb(st, "ad_esk", [64, 8])
            S.dma(snk[:, :], self.d_sinks[l:l + 1, :].to_broadcast([64, 8]), reads=[self.d_sinks], writes=[snk])
            S.op("act", lambda e: e.activation(out=esk[:, :], in_=snk[:, :], func=AF.Exp), reads=[snk], writes=[esk])
            for h in range(8):
                S.dma(qT[:, h, :], self.featT[FDQ + 64 * h:FDQ + 64 * h + 64, :], reads=[self.featT], writes=[qT], q="pool")
            for h in range(2):
                S.dma(kT[:, h, :], self.featT[FDK + 64 * h:FDK + 64 * h + 64, :], reads=[self.featT], writes=[kT], q="pool")
            S.dma(V[:, :, :], self.tokM[:, TDV:TDV + 128].rearrange("(t p) c -> p t c", p=128),
                  reads=[self.tokM], writes=[V], q="pool")
            for s in range(4):
                S.dma(Vn[:, s, :], self.tokM[TP + 32 * s:TP + 32 * s + 8, TDV:TDV + 128], reads=[self.tokM], writes=[Vn], q="pool")
                S.dma(ckT[:, s, :, :], self.cache_d_kT[l, s].rearrange("h d r -> d h r"), reads=[self.cache_d_kT], writes=[ckT], q="pool")
                S.dma(cV[:, s, :], self.cache_d_v[l, s], reads=[self.cache_d_v], writes=[cV], q="pool")
            S.dma(mk[:, :, :], self.dmask[:, :, :].rearrange("m p q -> p m q"), reads=[self.dmask], writes=[mk], q="pool")
            S.dma(mks[:, :, :], self.dmask_s[:, :, :].rearrange("m p q -> p m q"), reads=[self.dmask_s], writes=[mks], q="pool")
            W["maskbuf"] = mk
            for qb in range(0, TP, 512):
                nq = min(512, TP - qb)
                for h in range(8):
                    kv = h // 4
                    num = W["num"][h % 2]
                    den = W["den"][h % 2]
                    keys = []
                    for kt in range(max(0, qb // 128 - 1), (qb + nq) // 128):
                        m = (qb - kt * 128) // 128
                        keys.append((kT[:, kv, kt * 128:(kt + 1) * 128], 128, V[:, kt, 64 * kv:64 * kv + 64],
                                     mk[:, 1 - m, 0:nq], [kT, V]))
                    self.attn_core(qT[:, h, qb:qb + nq], [qT], nq, keys, num, den, 0, 0.125, W)
                    r0 = 1536 + 64 * h
                    self.attn_finish(num, den, 0, nq, esk[:, h:h + 1], [esk], W, self.mixT[r0:r0 + 64, qb:qb + nq], self.mixT)
            W["maskbuf"] = mks
            for s in range(4):
                q0 = TP + 32 * s
                for h in range(8):
                    kv = h // 4
                    num = W["num"][h % 2]
                    den = W["den"][h % 2]
                    keys = [(ckT[:, s, kv, :], 128, cV[:, s, 64 * kv:64 * kv + 64], mks[:, 0, :], [ckT, cV]),
                            (kT[:, kv, q0:q0 + 8], 8, Vn[0:8, s, 64 * kv:64 * kv + 64], mks[0:8, 1, :], [kT, Vn])]
                    self.attn_core(qT[:, h, q0:q0 + 8], [qT], 8, keys, num, den, 0, 0.125, W)
                    r0 = 1536 + 64 * h
                    self.attn_finish(num, den, 0, 8, esk[:, h:h + 1], [esk], W, self.mixT[r0:r0 + 64, q0:q0 + 8], self.mixT)
        S.barrier()

    def tt(self, e, out, in0, in1, op, reads, writes):
        return self.S.op(e, lambda en: en.tensor_tensor(out=out, in0=in0, in1=in1, op=op), reads, writes)

    def phase_s5(self, l):
        c = self.cfg
        S = self.S
        TP, NTOK = c.TP, c.NTOK
        LC = 512
        PI = float(np.pi)
        with ExitStack() as st:
            def sm(name, n=16):
                return self.sb(st, "c_" + name, [128, n])
            lre, lim, lst = sm("lre"), sm("lim"), sm("lst")
            S.dma(lre[:, :], self.c_lre[l], reads=[self.c_lre], writes=[lre])
            S.dma(lim[:, :], self.c_lim[l], reads=[self.c_lim], writes=[lim])
            S.dma(lst[:, :], self.c_lst[l], reads=[self.c_lst], writes=[lst])
            dl, ar, th, rr = sm("dl"), sm("ar"), sm("th"), sm("rr")
            S.op("act", lambda e: e.activation(out=dl[:, :], in_=lst[:, :], func=AF.Exp), [lst], [dl])
            self.tt("dve", ar[:, :], lre[:, :], dl[:, :], ALU.mult, [lre, dl], [ar])
            self.tt("dve", th[:, :], lim[:, :], dl[:, :], ALU.mult, [lim, dl], [th])
            S.op("act", lambda e: e.activation(out=rr[:, :], in_=ar[:, :], func=AF.Exp), [ar], [rr])
            tq, ki, kf, ph, fx = sm("tq"), self.sb(st, "c_ki", [128, 16], I32), sm("kf"), sm("ph"), sm("fx")
            S.op("dve", lambda e: e.tensor_scalar(out=tq[:, :], in0=th[:, :], scalar1=1.0 / (2 * PI), scalar2=64.5,
                                                  op0=ALU.mult, op1=ALU.add), [th], [tq])
            S.op("dve", lambda e: e.tensor_copy(out=ki[:, :], in_=tq[:, :]), [tq], [ki])
            S.op("dve", lambda e: e.tensor_copy(out=kf[:, :], in_=ki[:, :]), [ki], [kf])
            S.op("dve", lambda e: e.tensor_scalar(out=kf[:, :], in0=kf[:, :], scalar1=-64.0, scalar2=-2 * PI,
                                                  op0=ALU.add, op1=ALU.mult), [kf], [kf])
            self.tt("dve", ph[:, :], th[:, :], kf[:, :], ALU.add, [th, kf], [ph])
            S.op("dve", lambda e: e.tensor_scalar(out=fx[:, :], in0=ph[:, :], scalar1=-PI, scalar2=2 * PI,
                                                  op0=ALU.is_lt, op1=ALU.mult), [ph], [fx])
            self.tt("dve", ph[:, :], ph[:, :], fx[:, :], ALU.add, [ph, fx], [ph])
            S.op("dve", lambda e: e.tensor_scalar(out=fx[:, :], in0=ph[:, :], scalar1=PI, scalar2=-2 * PI,
                                                  op0=ALU.is_gt, op1=ALU.mult), [ph], [fx])
            self.tt("dve", ph[:, :], ph[:, :], fx[:, :], ALU.add, [ph, fx], [ph])
            S.op("dve", lambda e: e.tensor_scalar(out=ph[:, :], in0=ph[:, :], scalar1=PI, scalar2=-PI,
                                                  op0=ALU.min, op1=ALU.max), [ph], [ph])
            sn, cs, ab = sm("sn"), sm("cs"), sm("ab")
            hpi = sm("hpi", 1)
            S.op("dve", lambda e: e.memset(hpi[:, :], PI / 2), [], [hpi])
            S.op("act", lambda e: e.activation(out=sn[:, :], in_=ph[:, :], func=AF.Sin), [ph], [sn])
            S.op("act", lambda e: e.activation(out=ab[:, :], in_=ph[:, :], func=AF.Abs), [ph], [ab])
            S.op("act", lambda e: e.activation(out=cs[:, :], in_=ab[:, :], func=AF.Sin, scale=-1.0, bias=hpi[:, 0:1]),
                 [ab, hpi], [cs])
            lbr, lbi, t1, t2, den, cr, ci = sm("lbr"), sm("lbi"), sm("t1"), sm("t2"), sm("den"), sm("cr"), sm("ci")
            self.tt("dve", lbr[:, :], rr[:, :], cs[:, :], ALU.mult, [rr, cs], [lbr])
            self.tt("dve", lbi[:, :], rr[:, :], sn[:, :], ALU.mult, [rr, sn], [lbi])
            lm1 = sm("lm1")
            S.op("dve", lambda e: e.tensor_scalar(out=lm1[:, :], in0=lbr[:, :], scalar1=-1.0, scalar2=None, op0=ALU.add), [lbr], [lm1])
            self.tt("dve", t1[:, :], lre[:, :], lre[:, :], ALU.mult, [lre], [t1])
            self.tt("dve", t2[:, :], lim[:, :], lim[:, :], ALU.mult, [lim], [t2])
            self.tt("dve", den[:, :], t1[:, :], t2[:, :], ALU.add, [t1, t2], [den])
            S.op("dve", lambda e: e.reciprocal(out=den[:, :], in_=den[:, :]), [den], [den])
            self.tt("dve", t1[:, :], lm1[:, :], lre[:, :], ALU.mult, [lm1, lre], [t1])
            self.tt("dve", t2[:, :], lbi[:, :], lim[:, :], ALU.mult, [lbi, lim], [t2])
            self.tt("dve", cr[:, :], t1[:, :], t2[:, :], ALU.add, [t1, t2], [cr])
            self.tt("dve", cr[:, :], cr[:, :], den[:, :], ALU.mult, [cr, den], [cr])
            self.tt("dve", t1[:, :], lbi[:, :], lre[:, :], ALU.mult, [lbi, lre], [t1])
            self.tt("dve", t2[:, :], lm1[:, :], lim[:, :], ALU.mult, [lm1, lim], [t2])
            self.tt("dve", ci[:, :], t1[:, :], t2[:, :], ALU.subtract, [t1, t2], [ci])
            self.tt("dve", ci[:, :], ci[:, :], den[:, :], ALU.mult, [ci, den], [ci])
            bre = self.sb(st, "c_bre", [128, 16, 32])
            bim = self.sb(st, "c_bim", [128, 16, 32])
            bpr = self.sb(st, "c_bpr", [128, 16, 32])
            bpi = self.sb(st, "c_bpi", [128, 16, 32])
            btmp = self.sb(st, "c_btmp", [128, 16, 32])
            S.dma(bre[:, :, :], self.c_bsd_re[l], reads=[self.c_bsd_re], writes=[bre])
            S.dma(bim[:, :, :], self.c_bsd_im[l], reads=[self.c_bsd_im], writes=[bim])
            crb = cr[:, :].unsqueeze(2).to_broadcast([128, 16, 32])
            cib = ci[:, :].unsqueeze(2).to_broadcast([128, 16, 32])
            self.tt("dve", bpr[:, :, :], bre[:, :, :], crb, ALU.mult, [bre, cr], [bpr])
            self.tt("dve", btmp[:, :, :], bim[:, :, :], cib, ALU.mult, [bim, ci], [btmp])
            self.tt("dve", bpr[:, :, :], bpr[:, :, :], btmp[:, :, :], ALU.subtract, [bpr, btmp], [bpr])
            self.tt("dve", bpi[:, :, :], bim[:, :, :], crb, ALU.mult, [bim, cr], [bpi])
            self.tt("dve", btmp[:, :, :], bre[:, :, :], cib, ALU.mult, [bre, ci], [btmp])
            self.tt("dve", bpi[:, :, :], bpi[:, :, :], btmp[:, :, :], ALU.add, [bpi, btmp], [bpi])
            ptr = self.ps(st, "c_ptr", [128, 512])
            rmask = self.sb(st, "c_rmask", [128, 4])
            S.dma(rmask[:, :], self.rowmask[:, :], reads=[self.rowmask], writes=[rmask])
            btr = self.sb(st, "c_btr", [128, 16, 128], BF16)
            bti = self.sb(st, "c_bti", [128, 16, 128], BF16)
            for (src, dst) in ((bpr, btr), (bpi, bti)):
                for jj in range(4):
                    S.op("pe", lambda e, src=src, jj=jj: e.transpose(out=ptr[:, jj * 128:(jj + 1) * 128],
                                                                    in_=src[:, 4 * jj:4 * jj + 4, :],
                                                                    identity=self.identf[:]),
                         reads=[src, self.identf], writes=[ptr])
                for jq in range(4):
                    S.op("dve", lambda e, dst=dst, jq=jq: e.tensor_scalar(
                        out=dst[:, jq:16:4, :], in0=ptr[:, :].rearrange("p (a b) -> p a b", a=4),
                        scalar1=rmask[:, jq:jq + 1], scalar2=None, op0=ALU.mult), [ptr, rmask], [dst])
            csr0 = self.sb(st, "c_csr0", [128, 16, 32], BF16)
            csi0 = self.sb(st, "c_csi0", [128, 16, 32], BF16)
            S.dma(csr0[:, :, :], self.c_csd_re[l], reads=[self.c_csd_re], writes=[csr0], q="pool")
            S.dma(csi0[:, :, :], self.c_csd_im[l], reads=[self.c_csd_im], writes=[csi0], q="pool")
            csr = self.sb(st, "c_csr", [128, 16, 128], BF16)
            csi = self.sb(st, "c_csi", [128, 16, 128], BF16)
            for (src, dst) in ((csr0, csr), (csi0, csi)):
                S.op("pool", lambda e, dst=dst: e.memset(dst[:, :, :], 0.0), [], [dst])
                for jq in range(4):
                    S.op("dve", lambda e, src=src, dst=dst, jq=jq: e.tensor_copy(
                        out=dst[:, jq:16:4, 32 * jq:32 * jq + 32], in_=src[:, jq:16:4, :]), [src], [dst])
            Ec = self.sb(st, "c_Ec", [128, 16, LC])
            Es = self.sb(st, "c_Es", [128, 16, LC])
            cm, smm, c2, s2, tA, tB = sm("cm"), sm("smm"), sm("c2"), sm("s2"), sm("tA"), sm("tB")
            S.op("dve", lambda e: e.memset(Ec[:, :, 0:1], 1.0), [], [Ec])
            S.op("dve", lambda e: e.memset(Es[:, :, 0:1], 0.0), [], [Es])
            S.op("dve", lambda e: e.tensor_copy(out=cm[:, :], in_=cs[:, :]), [cs], [cm])
            S.op("dve", lambda e: e.tensor_scalar(out=smm[:, :], in0=sn[:, :], scalar1=-1.0, scalar2=None, op0=ALU.mult), [sn], [smm])
            with ExitStack() as stw:
                tw1 = self.sb(stw, "c_tw1", [128, 16, LC // 2])
                tw2 = self.sb(stw, "c_tw2", [128, 16, LC // 2])
                m = 1
                while m < LC:
                    cb = cm[:, :].unsqueeze(2).to_broadcast([128, 16, m])
                    sbb = smm[:, :].unsqueeze(2).to_broadcast([128, 16, m])
                    self.tt("dve", tw1[:, :, 0:m], Ec[:, :, 0:m], cb, ALU.mult, [Ec, cm], [tw1])
                    self.tt("pool", tw2[:, :, 0:m], Es[:, :, 0:m], sbb, ALU.mult, [Es, smm], [tw2])
                    self.tt("dve", Ec[:, :, m:2 * m], tw1[:, :, 0:m], tw2[:, :, 0:m], ALU.subtract, [tw1, tw2, Ec], [Ec])
                    self.tt("dve", tw1[:, :, 0:m], Ec[:, :, 0:m], sbb, ALU.mult, [Ec, smm], [tw1])
                    self.tt("pool", tw2[:, :, 0:m], Es[:, :, 0:m], cb, ALU.mult, [Es, cm], [tw2])
                    self.tt("dve", Es[:, :, m:2 * m], tw1[:, :, 0:m], tw2[:, :, 0:m], ALU.add, [tw1, tw2, Es], [Es])
                    self.tt("dve", tA[:, :], cm[:, :], cm[:, :], ALU.mult, [cm], [tA])
                    self.tt("dve", tB[:, :], smm[:, :], smm[:, :], ALU.mult, [smm], [tB])
                    self.tt("dve", c2[:, :], tA[:, :], tB[:, :], ALU.subtract, [tA, tB], [c2])
                    self.tt("dve", s2[:, :], cm[:, :], smm[:, :], ALU.mult, [cm, smm], [s2])
                    S.op("dve", lambda e: e.tensor_scalar(out=smm[:, :], in0=s2[:, :], scalar1=2.0, scalar2=None, op0=ALU.mult), [s2], [smm])
                    S.op("dve", lambda e: e.tensor_copy(out=cm[:, :], in_=c2[:, :]), [c2], [cm])
                    m *= 2
            S.barrier()
            uT = self.sb(st, "c_uT", [128, 4, NTOK])
            uTb = self.sb(st, "c_uTb", [128, 4, NTOK], BF16)
            for jj in range(4):
                S.dma(uT[:, jj, :], self.featT[FCU + 128 * jj:FCU + 128 * jj + 128, :], reads=[self.featT], writes=[uT])
                S.dma(uTb[:, jj, :], self.featT[FCU + 128 * jj:FCU + 128 * jj + 128, :], reads=[self.featT], writes=[uTb], q="pool")
            yT = self.sb(st, "c_yT", [128, 4, NTOK], BF16)
            dcol = self.sb(st, "c_dcol", [128, 4])
            S.dma(dcol[:, :], self.c_dT[l], reads=[self.c_dT], writes=[dcol])
            hpr = self.sb(st, "c_hpr", [128, 16])
            hpi_ = self.sb(st, "c_hpi", [128, 16])
            S.op("dve", lambda e: e.memset(hpr[:, :], 0.0), [], [hpr])
            S.op("dve", lambda e: e.memset(hpi_[:, :], 0.0), [], [hpi_])
            h0r = self.sb(st, "c_h0r", [128, 16, 4])
            h0i = self.sb(st, "c_h0i", [128, 16, 4])
            S.dma(h0r[:, :, :], self.c_h0re[l], reads=[self.c_h0re], writes=[h0r])
            S.dma(h0i[:, :, :], self.c_h0im[l], reads=[self.c_h0im], writes=[h0i])
            hsr = self.sb(st, "c_hsr", [128, 16, 4])
            hsi = self.sb(st, "c_hsi", [128, 16, 4])
            inr = self.sb(st, "c_inr", [128, 16, 4])
            ini = self.sb(st, "c_ini", [128, 16, 4])
            ia = self.sb(st, "c_ia", [128, 16, 4])
            ib = self.sb(st, "c_ib", [128, 16, 4])
            W = {}
            stw2 = ExitStack()
            for nm in ("zr", "zi", "ta", "tb", "gr", "gi", "hr", "hi"):
                W[nm] = [self.sb(stw2, "c_w%s%d" % (nm, i), [128, LC]) for i in range(2)]
            W["hrb"] = [self.sb(stw2, "c_whrb%d" % i, [128, LC], BF16) for i in range(2)]
            W["hib"] = [self.sb(stw2, "c_whib%d" % i, [128, LC], BF16) for i in range(2)]
            xr = [self.ps(st, "c_xr%d" % i, [128, LC]) for i in range(2)]
            xi = [self.ps(st, "c_xi%d" % i, [128, LC]) for i in range(2)]
            yp = [self.ps(st, "c_yp%d" % i, [128, LC]) for i in range(2)]
            it = [0]

            def init_from(prev_r, prev_i, n):
                csb = cs[:, :].unsqueeze(2).to_broadcast([128, 16, n])
                snb = sn[:, :].unsqueeze(2).to_broadcast([128, 16, n])
                self.tt("dve", ia[:, :, 0:n], prev_r, csb, ALU.mult, [cs, hpr, h0r], [ia])
                self.tt("dve", ib[:, :, 0:n], prev_i, snb, ALU.mult, [sn, hpi_, h0i], [ib])
                self.tt("dve", inr[:, :, 0:n], ia[:, :, 0:n], ib[:, :, 0:n], ALU.subtract, [ia, ib], [inr])
                self.tt("dve", ia[:, :, 0:n], prev_r, snb, ALU.mult, [sn, hpr, h0r], [ia])
                self.tt("dve", ib[:, :, 0:n], prev_i, csb, ALU.mult, [cs, hpi_, h0i], [ib])
                self.tt("dve", ini[:, :, 0:n], ia[:, :, 0:n], ib[:, :, 0:n], ALU.add, [ia, ib], [ini])

            def block(col0, nseg, seglen, last_r, last_i):
                n = nseg * seglen
                ncols = n if nseg == 1 else 32 * nseg
                for j in range(16):
                    k = it[0] % 2
                    it[0] += 1
                    jj, jq = j // 4, j % 4
                    pr = slice(0, 128)
                    if nseg == 1:
                        ucols = uTb[pr, jj, col0:col0 + n]
                        def v(t):
                            return t[:, 0:n]
                        def tab(T):
                            return T[:, j, 0:n]
                    else:
                        ucols = uTb[pr, jj, col0:col0 + 32 * nseg].rearrange("p (s t) -> p s t", s=nseg)[:, :, 0:seglen]
                        def v(t):
                            return t[:, 0:n].rearrange("p (s t) -> p s t", s=nseg)
                        def tab(T):
                            return T[:, j, 0:seglen].unsqueeze(1).to_broadcast([128, nseg, seglen])
                    S.op("pe", lambda e: e.matmul(v(xr[k]), lhsT=btr[:, j, :], rhs=ucols, start=True, stop=True),
                         reads=[btr, uTb], writes=[xr[k]])
                    S.op("pe", lambda e: e.matmul(v(xi[k]), lhsT=bti[:, j, :], rhs=ucols, start=True, stop=True),
                         reads=[bti, uTb], writes=[xi[k]])
                    zr, zi, ta, tb, gr, gi, hr, hi = [W[nm][k] for nm in ("zr", "zi", "ta", "tb", "gr", "gi", "hr", "hi")]
                    self.tt("dve", v(ta), v(xr[k]), tab(Ec), ALU.mult, [xr[k], Ec], [ta])
                    self.tt("dve", v(tb), v(xi[k]), tab(Es), ALU.mult, [xi[k], Es], [tb])
                    self.tt("pool", v(zr), v(ta), v(tb), ALU.subtract, [ta, tb], [zr])
                    self.tt("dve", v(ta), v(xi[k]), tab(Ec), ALU.mult, [xi[k], Ec], [ta])
                    self.tt("dve", v(tb), v(xr[k]), tab(Es), ALU.mult, [xr[k], Es], [tb])
                    self.tt("pool", v(zi), v(ta), v(tb), ALU.add, [ta, tb], [zi])
                    rb = rr[:, j:j + 1].to_broadcast([128, seglen])
                    for sg in range(nseg):
                        cs_ = slice(sg * seglen, (sg + 1) * seglen)
                        S.op("dve", lambda e, sg=sg, cs_=cs_: e.tensor_tensor_scan(
                            out=gr[:, cs_], data0=rb, data1=zr[:, cs_], initial=inr[:, j, sg:sg + 1],
                            op0=ALU.mult, op1=ALU.add), [zr, rr, inr], [gr])
                        S.op("dve", lambda e, sg=sg, cs_=cs_: e.tensor_tensor_scan(
                            out=gi[:, cs_], data0=rb, data1=zi[:, cs_], initial=ini[:, j, sg:sg + 1],
                            op0=ALU.mult, op1=ALU.add), [zi, rr, ini], [gi])
                    self.tt("pool", v(ta), v(gr), tab(Ec), ALU.mult, [gr, Ec], [ta])
                    self.tt("pool", v(tb), v(gi), tab(Es), ALU.mult, [gi, Es], [tb])
                    self.tt("dve", v(hr), v(ta), v(tb), ALU.add, [ta, tb], [hr])
                    self.tt("pool", v(ta), v(gr), tab(Es), ALU.mult, [gr, Es], [ta])
                    self.tt("pool", v(tb), v(gi), tab(Ec), ALU.mult, [gi, Ec], [tb])
                    self.tt("dve", v(hi), v(ta), v(tb), ALU.subtract, [ta, tb], [hi])
                    hrb, hib = W["hrb"][k], W["hib"][k]
                    S.op("act", lambda e: e.activation(out=hrb[:, 0:n], in_=hr[:, 0:n], func=AF.Copy), [hr], [hrb])
                    S.op("act", lambda e: e.activation(out=hib[:, 0:n], in_=hi[:, 0:n], func=AF.Copy), [hi], [hib])
                    if nseg == 1:
                        S.op("act", lambda e: e.activation(out=last_r[:, j:j + 1], in_=hr[:, n - 1:n], func=AF.Copy), [hr], [hpr])
                        S.op("act", lambda e: e.activation(out=last_i[:, j:j + 1], in_=hi[:, n - 1:n], func=AF.Copy, scale=-1.0), [hi], [hpi_])
                    else:
                        lv = slice(seglen - 1, n, seglen)
                        S.op("act", lambda e: e.activation(out=last_r[:, j, :], in_=hr[:, lv], func=AF.Copy), [hr], [hsr])
                        S.op("act", lambda e: e.activation(out=last_i[:, j, :], in_=hi[:, lv], func=AF.Copy, scale=-1.0), [hi], [hsi])
                    ypk = yp[(it[0] // 8) % 2] if False else yp[0]
                    if nseg == 1:
                        yv = ypk[pr, 0:n]
                        hbv_r, hbv_i = hrb[:, 0:n], hib[:, 0:n]
                    else:
                        yv = ypk[pr, 0:n]
                        hbv_r, hbv_i = hrb[:, 0:n], hib[:, 0:n]
                    S.op("pe", lambda e: e.matmul(yv, lhsT=csr[:, j, :], rhs=hbv_r, start=(jq == 0), stop=False),
                         reads=[csr, hrb], writes=[ypk])
                    S.op("pe", lambda e: e.matmul(yv, lhsT=csi[:, j, :], rhs=hbv_i, start=False, stop=(jq == 3)),
                         reads=[csi, hib], writes=[ypk])
                    if jq == 3:
                        if nseg == 1:
                            uin = uT[:, jj, col0:col0 + n]
                            yout = yT[:, jj, col0:col0 + n]
                            yin = ypk[:, 0:n]
                        else:
                            uin = uT[:, jj, col0:col0 + 32 * nseg].rearrange("p (s t) -> p s t", s=nseg)[:, :, 0:seglen]
                            yout = yT[:, jj, col0:col0 + 32 * nseg].rearrange("p (s t) -> p s t", s=nseg)[:, :, 0:seglen]
                            yin = ypk[:, 0:n].rearrange("p (s t) -> p s t", s=nseg)
                        S.op("dve", lambda e: e.scalar_tensor_tensor(out=yout, in0=uin, scalar=dcol[:, jj:jj + 1], in1=yin,
                                                                     op0=ALU.mult, op1=ALU.add),
                             [uT, dcol, ypk], [yT])

            S.op("pool", lambda e: e.memset(yT[:, :, :], 0.0), [], [yT])
            for ch0 in range(0, TP, LC):
                n = min(LC, TP - ch0)
                init_from(hpr[:, :].unsqueeze(2), hpi_[:, :].unsqueeze(2), 1)
                block(ch0, 1, n, hpr, hpi_)
            S.dma(self.o_pcre[l], hpr[:, :], reads=[hpr], writes=[self.o_pcre])
            S.dma(self.o_pcim[l], hpi_[:, :], reads=[hpi_], writes=[self.o_pcim])
            init_from(h0r[:, :, :], h0i[:, :, :], 4)
            block(TP, 4, 8, hsr, hsi)
            S.dma(self.o_scre[l], hsr[:, :, :], reads=[hsr], writes=[self.o_scre])
            S.dma(self.o_scim[l], hsi[:, :, :], reads=[hsi], writes=[self.o_scim])
            S.barrier()
            stw2.close()
            gw = self.sb(st, "c_gw", [128, 4, 1024], BF16)
            gb = self.sb(st, "c_gb", [128, 8])
            S.dma(gw[:, :, :], self.c_glu_w[l].rearrange("(k p) c -> p k c", p=128), reads=[self.c_glu_w], writes=[gw], q="pool")
            S.dma(gb[:, :], self.c_glu_bT[l], reads=[self.c_glu_bT], writes=[gb])
            sg_ = [self.sb(st, "c_sig%d" % i, [128, 512]) for i in range(2)]
            og = [self.sb(st, "c_og%d" % i, [128, 512]) for i in range(2)]
            gi_ = 0
            for tb0 in range(0, NTOK, 512):
                tn = min(512, NTOK - tb0)
                for i in range(4):
                    pv, pg = xr[gi_ % 2], xi[gi_ % 2]
                    for kk in range(4):
                        S.op("pe", lambda e, kk=kk: e.matmul(pv[:, 0:tn], lhsT=gw[:, kk, 128 * i:128 * i + 128],
                                                             rhs=yT[:, kk, tb0:tb0 + tn], start=(kk == 0), stop=(kk == 3)),
                             reads=[gw, yT], writes=[pv])
                    for kk in range(4):
                        S.op("pe", lambda e, kk=kk: e.matmul(pg[:, 0:tn], lhsT=gw[:, kk, 512 + 128 * i:512 + 128 * i + 128],
                                                             rhs=yT[:, kk, tb0:tb0 + tn], start=(kk == 0), stop=(kk == 3)),
                             reads=[gw, yT], writes=[pg])
                    sgb, ogb = sg_[gi_ % 2], og[gi_ % 2]
                    gi_ += 1
                    S.op("act", lambda e: e.activation(out=sgb[:, 0:tn], in_=pg[:, 0:tn], func=AF.Sigmoid, bias=gb[:, 4 + i:5 + i]),
                         [pg, gb], [sgb])
                    S.op("dve", lambda e: e.scalar_tensor_tensor(out=ogb[:, 0:tn], in0=pv[:, 0:tn], scalar=gb[:, i:i + 1],
                                                                 in1=sgb[:, 0:tn], op0=ALU.add, op1=ALU.mult),
                         [pv, gb, sgb], [ogb])
                    S.dma(self.mixT[1024 + 128 * i:1024 + 128 * i + 128, tb0:tb0 + tn], ogb[:, 0:tn], reads=[ogb], writes=[self.mixT])
        S.barrier()

    def phase_gdn(self, l):
        c = self.cfg
        S = self.S
        TP, NTOK, NT, NTP = c.TP, c.NTOK, c.NT, c.NTP
        with ExitStack() as st:
            qT = self.sb(st, "a_qT", [128, 4, NTOK], BF16)
            kT = self.sb(st, "a_kT", [128, 4, NTOK], BF16)
            vT = self.sb(st, "a_vT", [128, 4, NTOK], BF16)
            onesf = self.sb(st, "a_onesf", [128, 128])
            S.op("pool", lambda e: e.memset(onesf[:, :], 1.0), [], [onesf])
            with ExitStack() as st1:
                cw = self.sb(st1, "a_cw", [128, 12, 4])
                S.dma(cw[:, :, :], self.a_cw[l], reads=[self.a_cw], writes=[cw])
                EW = TP + 3
                Eb = [self.sb(st1, "a_E%d" % i, [128, EW]) for i in range(2)]
                Esb = [self.sb(st1, "a_Es%d" % i, [128, 4, 11]) for i in range(2)]
                Ob = [self.sb(st1, "a_O%d" % i, [128, NTOK]) for i in range(2)]
                Sq = [self.sb(st1, "a_Sq%d" % i, [128, 512]) for i in range(2)]
                Rs = [self.sb(st1, "a_Rs%d" % i, [128, 512]) for i in range(2)]
                pss = [self.ps(st1, "a_pss%d" % i, [128, 512]) for i in range(2)]
                epsc = self.sb(st1, "a_eps", [128, 1])
                S.op("dve", lambda e: e.memset(epsc[:, :], 1e-6), [], [epsc])
                for i in range(2):
                    S.op("dve", lambda e, i=i: e.memset(Eb[i][:, 0:3], 0.0), [], [Eb[i]])
                for ct in range(12):
                    E, Es_, O = Eb[ct % 2], Esb[ct % 2], Ob[ct % 2]
                    r0 = ct * 128
                    S.dma(E[:, 3:3 + TP], self.featT[r0:r0 + 128, 0:TP], reads=[self.featT], writes=[E])
                    S.dma(Es_[:, :, 3:11], self.featT[r0:r0 + 128, TP:TP + 128].rearrange("p (s t) -> p s t", s=4)[:, :, 0:8],
                          reads=[self.featT], writes=[Es_], q="act")
                    S.dma(Es_[:, :, 0:3], self.a_cc[l, :, ct, :, :], reads=[self.a_cc], writes=[Es_], q="act")
                    S.op("pool", lambda e: e.memset(O[:, TP:NTOK], 0.0), [], [O])
                    Osv = O[:, TP:NTOK].rearrange("p (s t) -> p s t", s=4)[:, :, 0:8]
                    for (ov, ev) in ((O[:, 0:TP], lambda j: E[:, j:j + TP]), (Osv, lambda j: Es_[:, :, j:j + 8])):
                        S.op("dve", lambda e: e.tensor_scalar(out=ov, in0=ev(0), scalar1=cw[:, ct, 0:1], scalar2=None, op0=ALU.mult),
                             [E, Es_, cw], [O])
                        for j in range(1, 4):
                            S.op("dve", lambda e, j=j: e.scalar_tensor_tensor(out=ov, in0=ev(j), scalar=cw[:, ct, j:j + 1], in1=ov,
                                                                             op0=ALU.mult, op1=ALU.add), [E, Es_, cw, O], [O])
                    S.op("act", lambda e: e.activation(out=O[:, :], in_=O[:, :], func=AF.Silu), [O], [O])
                    grp, h = ct // 4, ct % 4
                    dst = (qT, kT, vT)[grp]
                    if grp == 2:
                        S.op("act", lambda e: e.activation(out=dst[:, h, :], in_=O[:, :], func=AF.Copy), [O], [dst])
                        continue
                    for tb0 in range(0, NTOK, 512):
                        tn = min(512, NTOK - tb0)
                        sq, rs, pq = Sq[(tb0 // 512) % 2], Rs[(tb0 // 512) % 2], pss[(tb0 // 512) % 2]
                        S.op("act", lambda e: e.activation(out=sq[:, 0:tn], in_=O[:, tb0:tb0 + tn], func=AF.Square), [O], [sq])
                        S.op("pe", lambda e: e.matmul(pq[:, 0:tn], lhsT=onesf[:, :], rhs=sq[:, 0:tn], start=True, stop=True),
                             [onesf, sq], [pq])
                        S.op("act", lambda e: e.activation(out=rs[:, 0:tn], in_=pq[:, 0:tn], func=AF.Sqrt, bias=epsc[:, 0:1]), [pq, epsc], [rs])
                        S.op("dve", lambda e: e.reciprocal(out=rs[:, 0:tn], in_=rs[:, 0:tn]), [rs], [rs])
                        if grp == 0:
                            S.op("dve", lambda e: e.scalar_tensor_tensor(out=dst[:, h, tb0:tb0 + tn], in0=O[:, tb0:tb0 + tn],
                                                                         scalar=float(128 ** -0.5), in1=rs[:, 0:tn],
                                                                         op0=ALU.mult, op1=ALU.mult), [O, rs], [dst])
                        else:
                            self.tt("dve", dst[:, h, tb0:tb0 + tn], O[:, tb0:tb0 + tn], rs[:, 0:tn], ALU.mult, [O, rs], [dst])
            S.barrier()
            S.dma(self.o_paconv[l], self.featT[0:1536, TP - 3:TP], reads=[self.featT], writes=[self.o_paconv])
            for s_ in range(4):
                S.dma(self.o_saconv[l, s_], self.featT[0:1536, TP + 32 * s_ + 5:TP + 32 * s_ + 8], reads=[self.featT], writes=[self.o_saconv])
            ab = self.sb(st, "a_ab", [128, NT, 8])
            S.dma(ab[:, :, :], self.tokM[:, TAB:TAB + 8].rearrange("(t p) c -> p t c", p=128), reads=[self.tokM], writes=[ab])
            par = self.sb(st, "a_par", [128, 8])
            S.dma(par[:, :], self.a_par[l:l + 1, :].to_broadcast([128, 8]), reads=[self.a_par], writes=[par])
            vld = self.sb(st, "a_vld", [128, NT])
            S.dma(vld[:, :], self.valid[:, :], reads=[self.valid], writes=[vld])
            negA = self.sb(st, "a_negA", [128, 4])
            S.op("act", lambda e: e.activation(out=negA[:, :], in_=par[:, 0:4], func=AF.Exp), [par], [negA])
            S.op("dve", lambda e: e.tensor_scalar(out=negA[:, :], in0=negA[:, :], scalar1=-1.0, scalar2=None, op0=ALU.mult), [negA], [negA])
            gg = self.sb(st, "a_gg", [128, NT, 4])
            bb = self.sb(st, "a_bb", [128, NT, 4])
            nbb = self.sb(st, "a_nbb", [128, NT, 4])
            vb4 = vld[:, :].unsqueeze(2).to_broadcast([128, NT, 4])
            self.tt("dve", gg[:, :, :], ab[:, :, 0:4], par[:, 4:8].unsqueeze(1).to_broadcast([128, NT, 4]), ALU.add, [ab, par], [gg])
            S.op("act", lambda e: e.activation(out=gg[:, :, :], in_=gg[:, :, :], func=AF.Exp), [gg], [gg])
            S.op("act", lambda e: e.activation(out=gg[:, :, :], in_=gg[:, :, :], func=AF.Ln, bias=1.0), [gg], [gg])
            self.tt("dve", gg[:, :, :], gg[:, :, :], negA[:, :].unsqueeze(1).to_broadcast([128, NT, 4]), ALU.mult, [gg, negA], [gg])
            self.tt("dve", gg[:, :, :], gg[:, :, :], vb4, ALU.mult, [gg, vld], [gg])
            S.op("act", lambda e: e.activation(out=bb[:, :, :], in_=ab[:, :, 4:8], func=AF.Sigmoid), [ab], [bb])
            self.tt("dve", bb[:, :, :], bb[:, :, :], vb4, ALU.mult, [bb, vld], [bb])
            S.op("dve", lambda e: e.tensor_scalar(out=nbb[:, :, :], in0=bb[:, :, :], scalar1=-1.0, scalar2=None, op0=ALU.mult), [bb], [nbb])
            gm = self.sb(st, "a_gm", [128, 3, 128])
            S.dma(gm[:, :, :], self.gmasks[:, :, :].rearrange("m p q -> p m q"), reads=[self.gmasks], writes=[gm])
            bsel = self.sb(st, "a_bsel", [128, 4])
            S.dma(bsel[:, :], self.rowmask[:, :], reads=[self.rowmask], writes=[bsel])
            bselb = self.sb(st, "a_bselb", [128, 4], BF16)
            S.op("dve", lambda e: e.tensor_copy(out=bselb[:, :], in_=bsel[:, :]), [bsel], [bselb])
            identb = self.sb(st, "a_identb", [128, 128], BF16)
            S.op("dve", lambda e: e.tensor_copy(out=identb[:, :], in_=self.identf[:, :]), [self.identf], [identb])
            nw = self.sb(st, "a_nw", [128, 128])
            S.dma(nw[:, :], self.a_nw[l:l + 1, :].to_broadcast([128, 128]), reads=[self.a_nw], writes=[nw])
            Sf = [self.sb(st, "a_Sf%d" % h, [128, 128]) for h in range(4)]
            Sb = [self.sb(st, "a_Sb%d" % h, [128, 128], BF16) for h in range(4)]
            for h in range(4):
                S.op("pool", lambda e, h=h: e.memset(Sf[h][:, :], 0.0), [], [Sf[h]])
                S.op("pool", lambda e, h=h: e.memset(Sb[h][:, :], 0.0), [], [Sb[h]])
            banks = [self.ps(st, "a_bank%d" % i, [128, 512]) for i in range(8)]
            def quarter(bk, qi):
                return Sub(banks[bk][:, qi * 128:(qi + 1) * 128], banks[bk], "a_q%d_%d" % (bk, qi))
            scr = [quarter(bk, qi) for qi in range(4) for bk in (0, 1, 2, 3)]
            scr_i = [0]
            def pscr():
                b = scr[scr_i[0] % len(scr)]
                scr_i[0] += 1
                return b
            hb = {h: [quarter(4 + h, qi) for qi in range(4)] for h in range(4)}
            def wk(name, shape, dt=F32):
                return [[self.sb(st, "a_%s_%d_%d" % (name, h, i), shape, dt) for i in range(2)] for h in range(4)]
            u_b = wk("u", [128, 128])
            wTm_b = wk("wTm", [128, 4, 128], BF16)
            qdTm_b = wk("qdTm", [128, 4, 128], BF16)
            kdm_b = wk("kdm", [128, 4, 128], BF16)
            qkm_b = wk("qkm", [128, 4, 128], BF16)
            gl_b = wk("gl", [128, 4])
            oacc_b = wk("oacc", [128, 128])
            for h in range(4):
                for i in range(2):
                    S.op("pool", lambda e, h=h, i=i: e.memset(wTm_b[h][i][:, :, :], 0.0), [], [wTm_b[h][i]])
                    S.op("pool", lambda e, h=h, i=i: e.memset(qdTm_b[h][i][:, :, :], 0.0), [], [qdTm_b[h][i]])
            def t128(name, dt=F32, n=128):
                return self.sb(st, "a_t_" + name, [128, n], dt)
            gbc, sml, colv = t128("gbc"), t128("sml", F32, 8), t128("colv", F32, 8)
            d1, d2, Nm, NmT, PT, Mx, MxT, tmpA = (t128("d1"), t128("d2"), t128("N"), t128("NT"), t128("PT"),
                                                   t128("M"), t128("MT"), t128("tmpA"))
            qkf = t128("qkf", BF16)
            vbt, kbg, kdec, qdtm = t128("vb"), t128("kbg"), t128("kdec", BF16), t128("qdtm", BF16)
            vnb = [t128("vnb%d" % h, BF16) for h in range(4)]
            t1o = [t128("t1o%d" % h) for h in range(4)]
            zt = [self.sb(st, "a_zt%d" % i, [128, 128]) for i in range(2)]
            og = [self.sb(st, "a_og%d" % i, [128, 128]) for i in range(2)]
            onrm = t128("onrm")
            ssq = t128("ssq", F32, 2)

            def diag_ap(buf):
                a = buf[:, :, :]
                return AP(tensor=a.tensor, offset=a.offset, ap=[list(a.ap[0]), [128 + 32, 4], [1, 32]])

            def intra(h, tt_):
                par_ = tt_ % 2
                cols = slice(tt_ * 128, (tt_ + 1) * 128)
                gcol = gg[:, tt_, h:h + 1]
                bcol = bb[:, tt_, h:h + 1]
                nbcol = nbb[:, tt_, h:h + 1]
                S.op("dve", lambda e: e.tensor_scalar(out=gbc[:, :], in0=onesf[:, :], scalar1=gcol, scalar2=None, op0=ALU.mult),
                     [onesf, gg], [gbc])
                pG, pS = pscr(), pscr()
                S.op("pe", lambda e: e.matmul(pG[:, :], lhsT=gbc[:, :], rhs=gm[:, 0, :], start=True, stop=True), [gbc, gm], [pG])
                S.op("pe", lambda e: e.matmul(pS[:, 0:1], lhsT=gm[:, 0, :], rhs=gcol, start=True, stop=True), [gm, gg], [pS])
                S.op("pe", lambda e: e.matmul(pS[:, 1:2], lhsT=gm[:, 1, :], rhs=gcol, start=True, stop=True), [gm, gg], [pS])
                S.op("pe", lambda e: e.matmul(pS[:, 2:6], lhsT=gbc[:, :], rhs=bsel[:, :], start=True, stop=True), [gbc, bsel], [pS])
                S.op("act", lambda e: e.activation(out=sml[:, 0:6], in_=pS[:, 0:6], func=AF.Copy), [pS], [sml])
                gl = gl_b[h][par_]
                S.op("act", lambda e: e.activation(out=gl[:, :], in_=sml[:, 2:6], func=AF.Exp), [sml], [gl])
                S.op("act", lambda e: e.activation(out=colv[:, 0:1], in_=sml[:, 0:1], func=AF.Exp), [sml], [colv])
                S.op("dve", lambda e: e.tensor_tensor(out=colv[:, 3:4], in0=sml[:, 1:2], in1=sml[:, 0:1], op=ALU.subtract), [sml, colv], [colv])
                S.op("act", lambda e: e.activation(out=colv[:, 1:2], in_=colv[:, 3:4], func=AF.Exp), [colv], [colv])
                S.op("dve", lambda e: e.tensor_tensor(out=colv[:, 2:3], in0=colv[:, 0:1], in1=bcol, op=ALU.mult), [colv, bb], [colv])
                if c.intra_stop <= 1:
                    return
                S.op("dve", lambda e: e.tensor_scalar(out=d1[:, :], in0=pG[:, :], scalar1=sml[:, 0:1], scalar2=0.0,
                                                      op0=ALU.subtract, op1=ALU.max), [pG, sml], [d1])
                S.op("act", lambda e: e.activation(out=d1[:, :], in_=d1[:, :], func=AF.Exp, scale=-1.0), [d1], [d1])
                self.tt("pool", d1[:, :], d1[:, :], gm[:, 2, :], ALU.mult, [d1, gm], [d1])
                S.op("dve", lambda e: e.tensor_scalar(out=d2[:, :], in0=pG[:, :], scalar1=sml[:, 0:1], scalar2=0.0,
                                                      op0=ALU.subtract, op1=ALU.min), [pG, sml], [d2])
                S.op("act", lambda e: e.activation(out=d2[:, :], in_=d2[:, :], func=AF.Exp), [d2], [d2])
                self.tt("pool", d2[:, :], d2[:, :], gm[:, 0, :], ALU.mult, [d2, gm], [d2])
                if c.intra_stop <= 2:
                    return
                pK, pQ = pscr(), pscr()
                S.op("pe", lambda e: e.matmul(pK[:, :], lhsT=kT[:, h, cols], rhs=kT[:, h, cols], start=True, stop=True), [kT], [pK])
                S.op("pe", lambda e: e.matmul(pQ[:, :], lhsT=kT[:, h, cols], rhs=qT[:, h, cols], start=True, stop=True), [kT, qT], [pQ])
                S.op("dve", lambda e: e.scalar_tensor_tensor(out=Nm[:, :], in0=pK[:, :], scalar=nbcol, in1=d1[:, :],
                                                             op0=ALU.mult, op1=ALU.mult), [pK, nbb, d1], [Nm])
                self.tt("dve", qkf[:, :], pQ[:, :], d2[:, :], ALU.mult, [pQ, d2], [qkf])
                qkm = qkm_b[h][par_]
                self.tt("pool", qkm[:, :, :], qkf[:, :].unsqueeze(1).to_broadcast([128, 4, 128]),
                        bselb[:, :].unsqueeze(2).to_broadcast([128, 4, 128]), ALU.mult, [qkf, bselb], [qkm])
                if c.intra_stop <= 3:
                    return
                pT = pscr()
                S.op("pe", lambda e: e.transpose(out=pT[:, :], in_=Nm[:, :], identity=self.identf[:, :]), [Nm, self.identf], [pT])
                S.op("act", lambda e: e.activation(out=NmT[:, :], in_=pT[:, :], func=AF.Copy), [pT], [NmT])
                self.tt("dve", PT[:, :], pT[:, :], self.identf[:, :], ALU.add, [pT, self.identf], [PT])
                cur, curT = Nm, NmT
                nxt = [(Mx, MxT), (Nm, NmT)]
                for step in range(c.chain_steps):
                    M_, MT_ = nxt[step % 2]
                    pM = pscr()
                    S.op("pe", lambda e, cur=cur, curT=curT, pM=pM: e.matmul(pM[:, :], lhsT=curT[:, :], rhs=cur[:, :], start=True, stop=True),
                         [cur, curT], [pM])
                    if step < 3:
                        pMT = pscr()
                        S.op("pe", lambda e, cur=cur, curT=curT, pMT=pMT: e.matmul(pMT[:, :], lhsT=cur[:, :], rhs=curT[:, :], start=True, stop=True),
                             [cur, curT], [pMT])
                    S.op("act", lambda e, M_=M_, pM=pM: e.activation(out=M_[:, :], in_=pM[:, :], func=AF.Copy), [pM], [M_])
                    if step < 3:
                        S.op("dve", lambda e, MT_=MT_, pMT=pMT: e.tensor_copy(out=MT_[:, :], in_=pMT[:, :]), [pMT], [MT_])
                    pP = pscr()
                    S.op("pe", lambda e, M_=M_, pP=pP: e.matmul(pP[:, :], lhsT=M_[:, :], rhs=PT[:, :], start=True, stop=True), [M_, PT], [pP])
                    self.tt("dve", PT[:, :], PT[:, :], pP[:, :], ALU.add, [PT, pP], [PT])
                    cur, curT = M_, MT_
                if c.intra_stop <= 4:
                    return
                pkt, pvt = pscr(), pscr()
                pkt_b = Buf(pkt[:, :].bitcast(BF16)[:, 0:128], pkt.name)
                pkt_b.w, pkt_b.r = pkt.w, pkt.r
                S.op("pe", lambda e: e.transpose(out=pkt_b[:, :], in_=kT[:, h, cols], identity=identb[:, :]), [kT, identb], [pkt])
                pvt_b = Buf(pvt[:, :].bitcast(BF16)[:, 0:128], pvt.name)
                S.op("pe", lambda e: e.transpose(out=pvt_b[:, :], in_=vT[:, h, cols], identity=identb[:, :]), [vT, identb], [pvt])
                S.op("dve", lambda e: e.tensor_scalar(out=kbg[:, :], in0=pkt_b[:, :], scalar1=colv[:, 2:3], scalar2=None, op0=ALU.mult),
                     [pkt, colv], [kbg])
                S.op("act", lambda e: e.activation(out=kdec[:, :], in_=pkt_b[:, :], func=AF.Copy, scale=colv[:, 1:2]), [pkt, colv], [kdec])
                kdm = kdm_b[h][par_]
                self.tt("pool", kdm[:, :, :], kdec[:, :].unsqueeze(1).to_broadcast([128, 4, 128]),
                        bselb[:, :].unsqueeze(2).to_broadcast([128, 4, 128]), ALU.mult, [kdec, bselb], [kdm])
                S.op("dve", lambda e: e.tensor_scalar(out=vbt[:, :], in0=pvt_b[:, :], scalar1=bcol, scalar2=None, op0=ALU.mult),
                     [pvt, bb], [vbt])
                if c.intra_stop <= 5:
                    return
                pqt = pscr()
                pqt_b = Buf(pqt[:, :].bitcast(BF16)[:, 0:128], pqt.name)
                S.op("pe", lambda e: e.transpose(out=pqt_b[:, :], in_=qT[:, h, cols], identity=identb[:, :]), [qT, identb], [pqt])
                S.op("act", lambda e: e.activation(out=qdtm[:, :], in_=pqt_b[:, :], func=AF.Copy, scale=colv[:, 0:1]), [pqt, colv], [qdtm])
                pq2 = pscr()
                pq2_b = Buf(pq2[:, :].bitcast(BF16)[:, 0:128], pq2.name)
                S.op("pe", lambda e: e.transpose(out=pq2_b[:, :], in_=qdtm[:, :], identity=identb[:, :]), [qdtm, identb], [pq2])
                qdTm = qdTm_b[h][par_]
                S.op("dve", lambda e: e.tensor_copy(out=diag_ap(qdTm), in_=pq2_b[:, :].rearrange("p (c t) -> p c t", c=4)), [pq2], [qdTm])
                if c.intra_stop <= 6:
                    return
                pu, pw = pscr(), pscr()
                S.op("pe", lambda e: e.matmul(pu[:, :], lhsT=PT[:, :], rhs=vbt[:, :], start=True, stop=True), [PT, vbt], [pu])
                S.op("pe", lambda e: e.matmul(pw[:, :], lhsT=kbg[:, :], rhs=PT[:, :], start=True, stop=True), [kbg, PT], [pw])
                u_ = u_b[h][par_]
                S.op("act", lambda e: e.activation(out=u_[:, :], in_=pu[:, :], func=AF.Copy), [pu], [u_])
                wTm = wTm_b[h][par_]
                S.op("dve", lambda e: e.tensor_copy(out=diag_ap(wTm), in_=pw[:, :].rearrange("p (c t) -> p c t", c=4)), [pw], [wTm])

            def inter(h, tt_, cb):
                par_ = tt_ % 2
                pa, po, pS_, _ = hb[h]
                wTm, qdTm, kdm, qkm, u_, gl, oacc = (wTm_b[h][par_], qdTm_b[h][par_], kdm_b[h][par_], qkm_b[h][par_],
                                                     u_b[h][par_], gl_b[h][par_], oacc_b[h][par_])
                sample = (tt_ == NTP)
                if sample:
                    S.dma(Sf[h][:, :], self.state_a[l, cb, h], reads=[self.state_a], writes=[Sf[h]])
                    S.op("act", lambda e: e.activation(out=Sb[h][:, :], in_=Sf[h][:, :], func=AF.Copy), [Sf[h]], [Sb[h]])
                S.op("pe", lambda e: e.matmul(pa[:, :], lhsT=wTm[:, cb, :], rhs=Sb[h][:, :], start=True, stop=True), [wTm, Sb[h]], [pa])
                self.tt("dve", vnb[h][:, :], u_[:, :], pa[:, :], ALU.subtract, [u_, pa], [vnb[h]])
                S.op("pe", lambda e: e.matmul(po[:, :], lhsT=qdTm[:, cb, :], rhs=Sb[h][:, :], start=True, stop=False), [qdTm, Sb[h]], [po])
                S.op("pe", lambda e: e.matmul(po[:, :], lhsT=qkm[:, cb, :], rhs=vnb[h][:, :], start=False, stop=True), [qkm, vnb[h]], [po])
                S.op("pe", lambda e: e.matmul(pS_[:, :], lhsT=kdm[:, cb, :], rhs=vnb[h][:, :], start=True, stop=True), [kdm, vnb[h]], [pS_])
                if cb == 0:
                    S.op("act", lambda e: e.activation(out=oacc[:, :], in_=po[:, :], func=AF.Copy), [po], [oacc])
                else:
                    self.tt("pool" if False else "dve", oacc[:, :], oacc[:, :], po[:, :], ALU.add, [oacc, po], [oacc])
                S.op("dve", lambda e: e.scalar_tensor_tensor(out=Sf[h][:, :], in0=Sf[h][:, :], scalar=gl[:, cb:cb + 1], in1=pS_[:, :],
                                                             op0=ALU.mult, op1=ALU.add), [Sf[h], gl, pS_], [Sf[h]])
                if sample:
                    S.dma(self.o_sastate[l, cb, h], Sf[h][:, :], reads=[Sf[h]], writes=[self.o_sastate])
                else:
                    S.op("act", lambda e: e.activation(out=Sb[h][:, :], in_=Sf[h][:, :], func=AF.Copy), [Sf[h]], [Sb[h]])

            def outp(h, tt_):
                par_ = tt_ % 2
                oacc = oacc_b[h][par_]
                cols = slice(tt_ * 128, (tt_ + 1) * 128)
                k = (h + tt_) % 2
                z = zt[k]
                S.dma(z[:, :], self.featT[FZ + 128 * h:FZ + 128 * h + 128, cols], reads=[self.featT], writes=[z])
                S.op("act", lambda e: e.activation(out=z[:, :], in_=z[:, :], func=AF.Silu), [z], [z])
                S.op("act", lambda e: e.activation(out=onrm[:, :], in_=oacc[:, :], func=AF.Square, accum_out=ssq[:, 0:1]), [oacc], [onrm, ssq])
                S.op("act", lambda e: e.activation(out=ssq[:, 1:2], in_=ssq[:, 0:1], func=AF.Sqrt, scale=1.0 / 128.0, bias=epsc2[:, 0:1]), [ssq, epsc2], [ssq])
                S.op("dve", lambda e: e.reciprocal(out=ssq[:, 1:2], in_=ssq[:, 1:2]), [ssq], [ssq])
                S.op("dve", lambda e: e.scalar_tensor_tensor(out=onrm[:, :], in0=oacc[:, :], scalar=ssq[:, 1:2], in1=nw[:, :],
                                                             op0=ALU.mult, op1=ALU.mult), [oacc, ssq, nw], [onrm])
                pO = pscr()
                S.op("pe", lambda e: e.transpose(out=pO[:, :], in_=onrm[:, :], identity=self.identf[:, :]), [onrm, self.identf], [pO])
                o_ = og[k]
                self.tt("dve", o_[:, :], pO[:, :], z[:, :], ALU.mult, [pO, z], [o_])
                S.dma(self.mixT[128 * h:128 * h + 128, cols], o_[:, :], reads=[o_], writes=[self.mixT], q="act")

            epsc2 = self.sb(st, "a_eps2", [128, 1])
            S.op("dve", lambda e: e.memset(epsc2[:, :], 1e-6), [], [epsc2])
            GL = c.gdn_level
            for tt_ in range(NT if GL >= 1 else 0):
                for h in range(4):
                    intra(h, tt_)
                for cb in range(4 if GL >= 2 else 0):
                    for h in range(4):
                        inter(h, tt_, cb)
                for h in range(4 if GL >= 3 else 0):
                    outp(h, tt_)
                if tt_ == NTP - 1:
                    for h in range(4):
                        S.dma(self.o_pastate[l, h], Sf[h][:, :], reads=[Sf[h]], writes=[self.o_pastate])
        S.barrier()

    def convert_tables(self, l):
        S = self.S
        i = 0
        for (src, c0) in ((self.peer_u[l], 0), (self.peer_v[l], D)):
            for r0 in range(0, 16384, 2048):
                S.dma(self.uv16[r0:r0 + 2048, c0:c0 + D], src[r0:r0 + 2048, :], reads=[src], writes=[self.uv16], q="pool")
                i += 1

    def layernorm(self, r, g, b, out, junk, stt, epsc):
        S = self.S
        S.op("act", lambda e: e.activation(out=junk[:, :], in_=r[:, :], func=AF.Copy, accum_out=stt[:, 0:1]), [r], [junk, stt])
        S.op("dve", lambda e: e.tensor_scalar(out=stt[:, 1:2], in0=stt[:, 0:1], scalar1=-1.0 / D, scalar2=None, op0=ALU.mult), [stt], [stt])
        S.op("act", lambda e: e.activation(out=junk[:, :], in_=r[:, :], func=AF.Square, bias=stt[:, 1:2], accum_out=stt[:, 2:3]),
             [r, stt], [junk, stt])
        S.op("act", lambda e: e.activation(out=stt[:, 3:4], in_=stt[:, 2:3], func=AF.Sqrt, scale=1.0 / D, bias=epsc[:, 0:1]), [stt, epsc], [stt])
        S.op("dve", lambda e: e.reciprocal(out=stt[:, 4:5], in_=stt[:, 3:4]), [stt], [stt])
        S.op("dve", lambda e: e.tensor_scalar(out=out[:, :], in0=r[:, :], scalar1=stt[:, 1:2], scalar2=stt[:, 4:5],
                                              op0=ALU.add, op1=ALU.mult), [r, stt], [out])
        self.tt("pool", out[:, :], out[:, :], g[:, :], ALU.mult, [out, g], [out])
        self.tt("dve", out[:, :], out[:, :], b[:, :], ALU.add, [out, b], [out])

    def phase_post1(self, l):
        c = self.cfg
        S = self.S
        ALPHA = float(8 ** 0.25)
        with ExitStack() as st:
            wo = self.sb(st, "w1_wo", [128, 16, D], BF16)
            for cb in range(4):
                S.dma(wo[:, :, cb * 512:(cb + 1) * 512],
                      self.w_out[l, :, cb * 512:(cb + 1) * 512].rearrange("(kt p) c -> p kt c", p=128),
                      reads=[self.w_out], writes=[wo], q="pool")
            g1 = self.sb(st, "w1_g", [128, D])
            b1 = self.sb(st, "w1_b", [128, D])
            S.dma(g1[:, :], self.ln1_g[l:l + 1, :].to_broadcast([128, D]), reads=[self.ln1_g], writes=[g1])
            S.dma(b1[:, :], self.ln1_b[l:l + 1, :].to_broadcast([128, D]), reads=[self.ln1_b], writes=[b1])
            epsc = self.sb(st, "w1_eps", [128, 1])
            S.op("dve", lambda e: e.memset(epsc[:, :], 1e-5), [], [epsc])
            mt = [self.sb(st, "w1_mt%d" % i, [128, 16, 128], BF16) for i in range(2)]
            xr = [self.sb(st, "w1_xr%d" % i, [128, D]) for i in range(2)]
            rb = [self.sb(st, "w1_r0", [128, D])] * 2
            xo = [self.sb(st, "w1_xo%d" % i, [128, D]) for i in range(2)]
            junk = self.sb(st, "w1_junk", [128, D], BF16)
            stt = [self.sb(st, "w1_st%d" % i, [128, 8]) for i in range(2)]
            pb = [self.ps(st, "w1_pb%d" % i, [128, 512]) for i in range(4)]
            ptr = [self.ps(st, "w1_ptr%d" % i, [128, 512]) for i in range(2)]
            for tt in range(c.NT):
                k = tt % 2
                cols = slice(tt * 128, (tt + 1) * 128)
                S.dma(mt[k][:, :, :], self.mixT[:, cols].rearrange("(kt p) t -> p kt t", p=128), reads=[self.mixT], writes=[mt[k]], q="pool")
                S.dma(xr[k][:, :], self.xres[cols, :], reads=[self.xres], writes=[xr[k]])
                for cb in range(4):
                    for kt in range(16):
                        S.op("pe", lambda e, cb=cb, kt=kt: e.matmul(pb[cb][:, :], lhsT=mt[k][:, kt, :], rhs=wo[:, kt, cb * 512:(cb + 1) * 512],
                                                                    start=(kt == 0), stop=(kt == 15)), reads=[mt[k], wo], writes=[pb[cb]])
                    S.op("dve", lambda e, cb=cb: e.scalar_tensor_tensor(out=rb[k][:, cb * 512:(cb + 1) * 512], in0=xr[k][:, cb * 512:(cb + 1) * 512],
                                                                        scalar=ALPHA, in1=pb[cb][:, :], op0=ALU.mult, op1=ALU.add),
                         [xr[k], pb[cb]], [rb[k]])
                self.layernorm(rb[k], g1, b1, xo[k], junk, stt[k], epsc)
                S.dma(self.xres[cols, :], xo[k][:, :], reads=[xo[k]], writes=[self.xres], q="act")
                self.transpose_tile_to_xT(st, xo[k], tt, ptr, self.xT)
        S.barrier()

    def phase_peer(self, l):
        c = self.cfg
        S = self.S
        NT, NTOK = c.NT, c.NTOK
        ALPHA = float(8 ** 0.25)
        last = (l == c.L - 1)
        with ExitStack() as st:
            kT = self.sb(st, "p1_kT", [128, 16, 128], BF16)
            S.dma(kT[:, :, :], self.peer_keysT[l], reads=[self.peer_keysT], writes=[kT], q="pool")
            wq = [self.sb(st, "p1_wq%d" % i, [128, 16, 128], BF16) for i in range(2)]
            qT = [self.sb(st, "p1_qT%d" % i, [128, 512], BF16) for i in range(2)]
            ssb = [self.sb(st, "p1_s%d" % i, [128, 4, 128]) for i in range(2)]
            pq = [self.ps(st, "p1_pq%d" % i, [128, 512]) for i in range(2)]
            psc = [self.ps(st, "p1_ps%d" % i, [128, 512]) for i in range(2)]
            it = 0
            for hc in range(16):
                w = wq[hc % 2]
                S.dma(w[:, :, :], self.peer_wq[l, :, hc * 128:(hc + 1) * 128].rearrange("(kt p) c -> p kt c", p=128),
                      reads=[self.peer_wq], writes=[w], q="pool")
                for tb0 in range(0, NTOK, 512):
                    tn = min(512, NTOK - tb0)
                    k = it % 2
                    it += 1
                    for kt in range(16):
                        S.op("pe", lambda e, kt=kt: e.matmul(pq[k][:, 0:tn], lhsT=w[:, kt, :], rhs=self.xT[:, kt, tb0:tb0 + tn],
                                                             start=(kt == 0), stop=(kt == 15)), reads=[w, self.xT], writes=[pq[k]])
                    self.evac(qT[k][:, 0:tn], pq[k][:, 0:tn], [pq[k]], [qT[k]])
                    nti = tn // 128
                    for ti in range(nti):
                        S.op("pe", lambda e, ti=ti: e.matmul(psc[k][:, ti * 128:(ti + 1) * 128], lhsT=qT[k][:, ti * 128:(ti + 1) * 128],
                                                             rhs=kT[:, hc, :], start=True, stop=True), reads=[qT[k], kT], writes=[psc[k]])
                    self.evac(ssb[k][:, 0:nti, :], psc[k][:, 0:tn].rearrange("p (a b) -> p a b", b=128), [psc[k]], [ssb[k]])
                    S.dma(self.sc_d[tb0:tb0 + tn, hc * 128:(hc + 1) * 128].rearrange("(a p) k -> p a k", p=128), ssb[k][:, 0:nti, :],
                          reads=[ssb[k]], writes=[self.sc_d], q="act")
        S.barrier()
        with ExitStack() as st:
            g2 = self.sb(st, "p2_g", [128, D])
            b2 = self.sb(st, "p2_b", [128, D])
            S.dma(g2[:, :], self.ln2_g[l:l + 1, :].to_broadcast([128, D]), reads=[self.ln2_g], writes=[g2])
            S.dma(b2[:, :], self.ln2_b[l:l + 1, :].to_broadcast([128, D]), reads=[self.ln2_b], writes=[b2])
            epsc = self.sb(st, "p2_eps", [128, 1])
            S.op("dve", lambda e: e.memset(epsc[:, :], 1e-5), [], [epsc])
            io16 = self.sb(st, "p2_io16", [128, 16])
            S.dma(io16[:, :], self.iota16[:, :], reads=[self.iota16], writes=[io16])
            ssb = self.sb(st, "p2_s", [128, D])
            x1 = self.sb(st, "p2_x1", [128, D])
            NUB = 6
            ub = [self.sb(st, "p2_ub%d" % i, [128, 2 * D], BF16) for i in range(NUB)]
            hq = [self.sb(st, "p2_hq%d" % i, [128, 4]) for i in range(4)]
            acc = self.sb(st, "p2_acc", [128, D])
            x1b = self.sb(st, "p2_x1b", [128, D], BF16)
            junkb = self.sb(st, "p2_junkb", [128, D], BF16)
            idb = self.sb(st, "p2_idb", [128, 128], BF16)
            S.op("dve", lambda e: e.tensor_copy(out=idb[:, :], in_=self.identf[:, :]), [self.identf], [idb])
            dg = [self.sb(st, "p2_dg%d" % i, [128, 128], BF16) for i in range(4)]
            pout = [self.ps(st, "p2_po%d" % i, [128, 512]) for i in range(4)]
            xo = self.sb(st, "p2_xo", [128, D])
            stt = self.sb(st, "p2_st", [128, 8])
            v = self.sb(st, "p2_v", [128, 16, 16])
            iu = self.sb(st, "p2_iu", [128, 16, 16], U32)
            i12 = self.sb(st, "p2_i12", [128, 16, 16])
            tmp = self.sb(st, "p2_tmp", [128, 256])
            cand = self.sb(st, "p2_cand", [128, 8, 256])
            scv = self.sb(st, "p2_scv", [128, 8, 16])
            cu = self.sb(st, "p2_cu", [128, 8, 16], U32)
            ca = self.sb(st, "p2_ca", [128, 8, 16], U32)
            cb_ = self.sb(st, "p2_cb", [128, 8, 16], U32)
            caf = self.sb(st, "p2_caf", [128, 8, 16])
            cbf = self.sb(st, "p2_cbf", [128, 8, 16])
            oh = self.sb(st, "p2_oh", [128, 16, 16])
            isel = self.sb(st, "p2_isel", [128, 2, 8, 16])
            eidf = self.sb(st, "p2_eidf", [128, 128])
            eid = self.sb(st, "p2_eid", [128, 128], I32)
            gate = self.sb(st, "p2_gate", [128, 8, 16])
            zs = self.sb(st, "p2_zs", [128, 8])
            hh = self.sb(st, "p2_hh", [128, 128])
            coef = self.sb(st, "p2_coef", [128, 128])
            ptr = [self.ps(st, "p2_ptr%d" % i, [128, 512]) for i in range(2)]
            ui = 0
            for tt in range(NT):
                rows = slice(tt * 128, (tt + 1) * 128)
                S.dma(ssb[:, :], self.sc_d[rows, :], reads=[self.sc_d], writes=[ssb])
                S.dma(x1[:, :], self.xres[rows, :], reads=[self.xres], writes=[x1], q="act")
                for hc in range(16):
                    sv = ssb[:, hc * 128:(hc + 1) * 128]
                    S.op("dve", lambda e: e.max(out=v[:, hc, 0:8], in_=sv), [ssb], [v])
                    S.op("dve", lambda e: e.max_index(out=iu[:, hc, 0:8], in_max=v[:, hc, 0:8], in_values=sv), [ssb, v], [iu])
                    S.op("dve", lambda e: e.match_replace(out=tmp[:, 0:128], in_to_replace=v[:, hc, 0:8], in_values=sv, imm_value=NEG),
                         [ssb, v], [tmp])
                    S.op("dve", lambda e: e.max(out=v[:, hc, 8:16], in_=tmp[:, 0:128]), [tmp], [v])
                    S.op("dve", lambda e: e.max_index(out=iu[:, hc, 8:16], in_max=v[:, hc, 8:16], in_values=tmp[:, 0:128]), [tmp, v], [iu])
                S.op("dve", lambda e: e.tensor_copy(out=i12[:, :, :], in_=iu[:, :, :]), [iu], [i12])
                for h in range(8):
                    cv = cand[:, h, :]
                    S.op("dve", lambda e: e.tensor_tensor(out=cv.rearrange("p (a b) -> p a b", a=16),
                                                          in0=v[:, 2 * h, :].unsqueeze(2).to_broadcast([128, 16, 16]),
                                                          in1=v[:, 2 * h + 1, :].unsqueeze(1).to_broadcast([128, 16, 16]), op=ALU.add),
                         [v], [cand])
                    S.op("dve", lambda e: e.max(out=scv[:, h, 0:8], in_=cv), [cand], [scv])
                    S.op("dve", lambda e: e.max_index(out=cu[:, h, 0:8], in_max=scv[:, h, 0:8], in_values=cv), [cand, scv], [cu])
                    S.op("dve", lambda e: e.match_replace(out=tmp[:, :], in_to_replace=scv[:, h, 0:8], in_values=cv, imm_value=NEG),
                         [cand, scv], [tmp])
                    S.op("dve", lambda e: e.max(out=scv[:, h, 8:16], in_=tmp[:, :]), [tmp], [scv])
                    S.op("dve", lambda e: e.max_index(out=cu[:, h, 8:16], in_max=scv[:, h, 8:16], in_values=tmp[:, :]), [tmp, scv], [cu])
                self.tt("dve", gate[:, :, :], scv[:, :, :], scv[:, :, 0:1].to_broadcast([128, 8, 16]), ALU.subtract, [scv], [gate])
                S.op("act", lambda e: e.activation(out=gate[:, :, :], in_=gate[:, :, :], func=AF.Exp), [gate], [gate])
                S.op("dve", lambda e: e.tensor_reduce(out=zs[:, :], in_=gate[:, :, :], axis=AX.X, op=ALU.add), [gate], [zs])
                S.op("dve", lambda e: e.reciprocal(out=zs[:, :], in_=zs[:, :]), [zs], [zs])
                self.tt("dve", gate[:, :, :], gate[:, :, :], zs[:, :].unsqueeze(2).to_broadcast([128, 8, 16]), ALU.mult, [gate, zs], [gate])
                S.op("dve", lambda e: e.tensor_single_scalar(out=ca[:, :, :], in_=cu[:, :, :], scalar=4, op=ALU.logical_shift_right), [cu], [ca])
                S.op("dve", lambda e: e.tensor_single_scalar(out=cb_[:, :, :], in_=cu[:, :, :], scalar=15, op=ALU.bitwise_and), [cu], [cb_])
                S.op("dve", lambda e: e.tensor_copy(out=caf[:, :, :], in_=ca[:, :, :]), [ca], [caf])
                S.op("dve", lambda e: e.tensor_copy(out=cbf[:, :, :], in_=cb_[:, :, :]), [cb_], [cbf])
                for h in range(8):
                    for half, cf in ((0, caf), (1, cbf)):
                        self.tt("dve", oh[:, :, :], cf[:, h, :].unsqueeze(2).to_broadcast([128, 16, 16]),
                                io16[:, :].unsqueeze(1).to_broadcast([128, 16, 16]), ALU.is_equal, [cf, io16], [oh])
                        self.tt("dve", oh[:, :, :], oh[:, :, :], i12[:, 2 * h + half, :].unsqueeze(1).to_broadcast([128, 16, 16]),
                                ALU.mult, [oh, i12], [oh])
                        S.op("dve", lambda e, half=half: e.tensor_reduce(out=isel[:, half, h, :], in_=oh[:, :, :], axis=AX.X, op=ALU.add),
                             [oh], [isel])
                S.op("dve", lambda e: e.scalar_tensor_tensor(out=eidf[:, :], in0=isel[:, 0, :, :].rearrange("p a b -> p (a b)"), scalar=128.0,
                                                             in1=isel[:, 1, :, :].rearrange("p a b -> p (a b)"), op0=ALU.mult, op1=ALU.add),
                     [isel], [eidf])
                S.op("dve", lambda e: e.tensor_copy(out=eid[:, :], in_=eidf[:, :]), [eidf], [eid])
                S.op("act", lambda e: e.activation(out=x1b[:, :], in_=x1[:, :], func=AF.Copy), [x1], [x1b])
                gflat = gate[:, :, :].rearrange("p a b -> p (a b)")
                for sl in range(128):
                    u_ = ub[ui % NUB]
                    q_ = hq[ui % 4]
                    ui += 1
                    S.dma(None, None, reads=[eid, self.uv16], writes=[u_], q="pool",
                          fn=lambda g, u_=u_, sl=sl: g.indirect_dma_start(
                              out=u_[:, :], out_offset=None, in_=self.uv16[:, :],
                              in_offset=bass.IndirectOffsetOnAxis(ap=eid[:, sl:sl + 1], axis=0),
                              ))
                    S.op("dve", lambda e, u_=u_, q_=q_: e.scalar_tensor_tensor(out=junkb[:, :], in0=u_[:, 0:D], scalar=1.0, in1=x1b[:, :],
                                                                               op0=ALU.mult, op1=ALU.mult, accum_out=q_[:, 0:1]),
                         [u_, x1b], [junkb, q_])
                    S.op("act", lambda e, q_=q_: e.activation(out=q_[:, 1:2], in_=q_[:, 0:1], func=AF.Gelu), [q_], [q_])
                    S.op("dve", lambda e, q_=q_, sl=sl: e.tensor_tensor(out=q_[:, 2:3], in0=q_[:, 1:2], in1=gflat[:, sl:sl + 1], op=ALU.mult),
                         [q_, gate], [q_])
                    d_ = dg[sl % 4]
                    S.op("act", lambda e, d_=d_, q_=q_: e.activation(out=d_[:, :], in_=idb[:, :], func=AF.Copy, scale=q_[:, 2:3]),
                         [idb, q_], [d_])
                    for cb in range(4):
                        S.op("pe", lambda e, d_=d_, u_=u_, cb=cb, sl=sl: e.matmul(pout[cb][:, :], lhsT=d_[:, :],
                                                                                  rhs=u_[:, D + cb * 512:D + (cb + 1) * 512],
                                                                                  start=(sl == 0), stop=(sl == 127)),
                             reads=[d_, u_], writes=[pout[cb]])
                for cb in range(4):
                    cs_ = slice(cb * 512, (cb + 1) * 512)
                    S.op("dve", lambda e, cb=cb, cs_=cs_: e.scalar_tensor_tensor(out=acc[:, cs_], in0=x1[:, cs_], scalar=ALPHA, in1=pout[cb][:, :],
                                                                                 op0=ALU.mult, op1=ALU.add), [x1, pout[cb]], [acc])
                self.layernorm(acc, g2, b2, xo, junkb, stt, epsc)
                S.dma(self.xres[rows, :], xo[:, :], reads=[xo], writes=[self.xres], q="act")
                if last:
                    S.dma(self.y[rows, :], xo[:, :], reads=[xo], writes=[self.y], q="act")
                else:
                    self.transpose_tile_to_xT(st, xo, tt, ptr, self.xT)
        S.barrier()

    def build(self):
        c = self.cfg
        S = self.S
        self.declare()
        with ExitStack() as gst:
            self.identf = self.sb(gst, "identf", [128, 128])
            self.identb = self.identf
            S.dma(self.identf[:], self.ident[:, :], reads=[self.ident], writes=[self.identf])
            xst = ExitStack()
            self.xT = self.sb(xst, "xT", [128, 16, c.NTOK], BF16)
            self.phase0()
            self.phase_proj(0)
            xst.close()
            S.barrier()
            sg = c.stages
            for l in range(c.L):
                if l in c.epoch_at:
                    S.new_epoch()
                if "p" in sg or "v" in sg:
                    self.convert_tables(l)
                if "b" in sg:
                    self.phase_attn_b(l)
                if "d" in sg:
                    self.phase_attn_d(l)
                if "c" in sg:
                    self.phase_s5(l)
                if "a" in sg:
                    self.phase_gdn(l)
                xst = ExitStack()
                self.xT = self.sb(xst, "xT", [128, 16, c.NTOK], BF16)
                if "w" in sg:
                    self.phase_post1(l)
                if "p" in sg:
                    self.phase_peer(l)
                if l < c.L - 1:
                    self.phase_proj(l + 1)
                xst.close()
                S.barrier()
            S.finish()
        return self.nc


def _bw(d):
    d = np.asarray(d)
    w = ((d >= 0) & (d <= 128)).astype(np.float32)
    w += ((d >= 0) & (d <= 512) & (d % 4 == 0))
    w += ((d >= 0) & (d <= 2048) & (d % 16 == 0))
    return w.astype(np.float32)


def host_consts():
    kl = np.arange(128)[:, None]
    ql = np.arange(512)[None, :]
    bmask = np.stack([_bw(128 * m + ql - kl) for m in range(-3, 13)])
    t8 = np.arange(8)[None, :]
    bs = [_bw(2048 + t8 - kt * 128 - kl) for kt in range(16)]
    bs.append(_bw(t8 - kl) * (kl < 8))
    bmask_s = np.stack(bs)
    dm = []
    for m in (1, 0, -1, -2, -3):
        d = 128 * m + ql - kl
        dm.append(((d >= 0) & (d <= 127)).astype(np.float32))
    dmask = np.stack(dm)
    d0 = 128 + t8 - kl
    d1 = t8 - kl
    dmask_s = np.stack([((d0 >= 0) & (d0 <= 127)).astype(np.float32),
                        ((d1 >= 0) & (d1 <= 127) & (kl < 8)).astype(np.float32)])
    return {"bmask": bmask, "bmask_s": bmask_s, "dmask": dmask, "dmask_s": dmask_s,
            "ident": np.eye(128, dtype=np.float32),
            "rowmask": np.ascontiguousarray((np.arange(128)[:, None] // 32 == np.arange(4)[None, :]).astype(np.float32))}


def host_inputs(cfg, core, inp):
    c = cfg
    pc = core % 4
    xin = np.zeros((c.NTOK, D), np.float32)
    xin[0:c.TP] = inp["x_prompt"][pc]
    for s in range(4):
        xin[c.TP + 32 * s:c.TP + 32 * s + 8] = inp["x_sample"][4 * core + s]
    m = {"xin": xin, "w_in": inp["w_in"][:c.L]}
    m.update(host_consts())
    sl = slice(4 * core, 4 * core + 4)
    L = c.L
    m["cache_b_kT"] = np.ascontiguousarray(inp["cache_b_k"][:L, sl].transpose(0, 1, 3, 4, 2))
    m["cache_b_v"] = np.ascontiguousarray(inp["cache_b_v"][:L, sl].reshape(L, 4, -1, 512))
    m["cache_d_kT"] = np.ascontiguousarray(inp["cache_d_k"][:L, sl].transpose(0, 1, 3, 4, 2))
    m["cache_d_v"] = np.ascontiguousarray(inp["cache_d_v"][:L, sl].reshape(L, 4, 128, 128))
    m["d_sinks"] = np.ascontiguousarray(inp["d_sinks"][:L].reshape(L, 8))
    blk = np.arange(128) // 32
    same = blk[:, None] == blk[None, :]
    ii = np.arange(128)
    m["gmasks"] = np.stack([(same & (ii[:, None] <= ii[None, :])), same, (same & (ii[:, None] > ii[None, :]))]).astype(np.float32)
    valid = np.ones((128, c.NT), np.float32)
    valid[:, c.NT - 1] = (np.arange(128) % 32 < 8)
    m["valid"] = valid
    m["a_cw"] = np.ascontiguousarray(inp["a_conv_w"][:L].reshape(L, 4, 12, 128).transpose(0, 3, 2, 1))
    m["a_cc"] = np.ascontiguousarray(inp["cache_a_conv"][:L, sl].reshape(L, 4, 3, 12, 128).transpose(0, 4, 3, 1, 2))
    m["a_par"] = np.ascontiguousarray(np.concatenate([inp["a_log"][:L], inp["a_dt_bias"][:L]], axis=1))
    m["a_nw"] = inp["a_norm_w"][:L]
    m["state_a"] = np.ascontiguousarray(inp["state_a"][:L, sl])
    def st_layout(a):
        sh = a.shape
        a = a.reshape((sh[0], 16, 2, 64) + sh[3:])
        a = np.moveaxis(a, 1, 3)
        return np.ascontiguousarray(a.reshape((sh[0], 128, 16) + sh[3:]))
    m["c_lre"] = st_layout(inp["c_lambda_re"][:L])
    m["c_lim"] = st_layout(inp["c_lambda_im"][:L])
    m["c_lst"] = st_layout(np.repeat(inp["c_log_step"][:L, :, None], 64, axis=2))
    def blockdiag(a):
        o = np.zeros((L, 2, 64, 16, 2, 16), np.float32)
        ar = a.reshape(L, 16, 2, 64, 16)
        for gl in range(2):
            o[:, gl, :, :, gl, :] = np.moveaxis(ar[:, :, gl], 1, 2)
        return np.ascontiguousarray(o.reshape(L, 128, 16, 32))
    m["c_bsd_re"] = blockdiag(inp["c_b_re"][:L])
    m["c_bsd_im"] = blockdiag(inp["c_b_im"][:L])
    m["c_csd_re"] = blockdiag(np.swapaxes(inp["c_c_re"][:L], 2, 3))
    m["c_csd_im"] = blockdiag(np.swapaxes(inp["c_c_im"][:L], 2, 3))
    m["c_dT"] = np.ascontiguousarray(inp["c_d"][:L].reshape(L, 4, 128).transpose(0, 2, 1))
    m["c_glu_w"] = inp["c_glu_w"][:L]
    m["c_glu_bT"] = np.ascontiguousarray(inp["c_glu_b"][:L].reshape(L, 8, 128).transpose(0, 2, 1))
    m["w_out"] = inp["w_out"][:L]
    for k in ("ln1_g", "ln1_b", "ln2_g", "ln2_b", "peer_wq"):
        m[k] = inp[k][:L]
    for i in range(L):
        m["peer_u%d" % i] = inp["peer_u"][i]
        m["peer_v%d" % i] = inp["peer_v"][i]
    m["peer_keysT"] = np.ascontiguousarray(inp["peer_sub_keys"][:L].reshape(L, 16, 128, 128).transpose(0, 3, 1, 2))
    m["iota16"] = np.ascontiguousarray(np.broadcast_to(np.arange(16, dtype=np.float32)[None, :], (128, 16)))
    m["c_h0re"] = np.ascontiguousarray(np.moveaxis(st_layout(np.moveaxis(inp["state_c_re"][:L, sl], 1, 3)), 3, 3))
    m["c_h0im"] = np.ascontiguousarray(np.moveaxis(st_layout(np.moveaxis(inp["state_c_im"][:L, sl], 1, 3)), 3, 3))
    return m


def unstate(a):
    sh = a.shape
    a = a.reshape((2, 64, 16) + sh[2:])
    a = np.moveaxis(a, 2, 0)
    return np.ascontiguousarray(a.reshape((32, 64) + sh[2:]))


def kernel(**inp):
    cfg = Cfg()
    b = Builder(cfg)
    nc = b.build()
    inp = {k: np.ascontiguousarray(np.asarray(v)) for k, v in inp.items()}
    in_maps = [host_inputs(cfg, core, inp) for core in range(8)]
    res = run_bass_kernel_spmd(nc, in_maps, core_ids=list(range(8))).results
    L, TP = cfg.L, cfg.TP
    f = np.float32
    y_p = np.zeros((4, TP, D), f)
    y_s = np.zeros((32, 8, D), f)
    p_a_conv = np.zeros((L, 4, 3, 1536), f)
    p_a_state = np.zeros((L, 4, 4, 128, 128), f)
    p_b_k = np.zeros((L, 4, TP, 8, 64), f)
    p_b_v = np.zeros((L, 4, TP, 8, 64), f)
    p_c_re = np.zeros((L, 4, 32, 64), f)
    p_c_im = np.zeros((L, 4, 32, 64), f)
    p_d_k = np.zeros((L, 4, 128, 2, 64), f)
    p_d_v = np.zeros((L, 4, 128, 2, 64), f)
    s_a_conv = np.zeros((L, 32, 3, 1536), f)
    s_a_state = np.zeros((L, 32, 4, 128, 128), f)
    s_b_k = np.zeros((L, 32, 8, 8, 64), f)
    s_b_v = np.zeros((L, 32, 8, 8, 64), f)
    s_c_re = np.zeros((L, 32, 32, 64), f)
    s_c_im = np.zeros((L, 32, 32, 64), f)
    s_d_k = np.zeros((L, 32, 8, 2, 64), f)
    s_d_v = np.zeros((L, 32, 8, 2, 64), f)
    for core in range(8):
        r = {k: np.asarray(v) for k, v in res[core].items()}
        sl = slice(4 * core, 4 * core + 4)
        for s_ in range(4):
            y_s[4 * core + s_] = r["y"][TP + 32 * s_:TP + 32 * s_ + 8]
        s_a_conv[:, sl] = np.swapaxes(r["s_a_convT"], 2, 3)
        s_a_state[:, sl] = r["s_a_state"]
        s_b_k[:, sl] = r["s_b_k"].reshape(L, 4, 8, 8, 64)
        s_b_v[:, sl] = r["s_b_v"].reshape(L, 4, 8, 8, 64)
        s_d_k[:, sl] = r["s_d_k"].reshape(L, 4, 8, 2, 64)
        s_d_v[:, sl] = r["s_d_v"].reshape(L, 4, 8, 2, 64)
        for l in range(L):
            s_c_re[l, sl] = np.moveaxis(unstate(r["s_c_re"][l]), 2, 0)
            s_c_im[l, sl] = np.moveaxis(unstate(r["s_c_im"][l]), 2, 0)
        if core < 4:
            pc = core
            y_p[pc] = r["y"][0:TP]
            p_a_conv[:, pc] = np.swapaxes(r["p_a_convT"], 1, 2)
            p_a_state[:, pc] = r["p_a_state"]
            p_b_k[:, pc] = r["p_b_k"].reshape(L, TP, 8, 64)
            p_b_v[:, pc] = r["p_b_v"].reshape(L, TP, 8, 64)
            p_d_k[:, pc] = r["p_d_k"].reshape(L, 128, 2, 64)
            p_d_v[:, pc] = r["p_d_v"].reshape(L, 128, 2, 64)
            for l in range(L):
                p_c_re[l, pc] = unstate(r["p_c_re"][l])
                p_c_im[l, pc] = unstate(r["p_c_im"][l])
    return (y_p, y_s, p_a_conv, p_a_state, p_b_k, p_b_v, p_c_re, p_c_im, p_d_k, p_d_v,
            s_a_conv, s_a_state, s_b_k, s_b_v, s_c_re, s_c_im, s_d_k, s_d_v)
```

```python
import numpy as np
from contextlib import ExitStack
import concourse.bass as bass
import concourse.mybir as mybir
from concourse.bass_types import AP
from concourse.bass_utils import run_bass_kernel_spmd

F32 = mybir.dt.float32
BF16 = mybir.dt.bfloat16
I32 = mybir.dt.int32
U32 = mybir.dt.uint32
AF = mybir.ActivationFunctionType
ALU = mybir.AluOpType
AX = mybir.AxisListType

D = 2048
INC = 4872
NEG = -1.0e30


class Buf:
    __slots__ = ("t", "w", "r", "name")

    ALL = []

    def __init__(self, t, name=""):
        self.t = t
        self.w = None
        self.r = []
        self.name = name
        Buf.ALL.append(self)

    def __getitem__(self, idx):
        return self.t[idx]


class Sub:
    __slots__ = ("t", "p", "name")

    def __init__(self, t, parent, name=""):
        self.t = t
        self.p = parent
        self.name = name

    @property
    def w(self):
        return self.p.w

    @w.setter
    def w(self, v):
        self.p.w = v

    @property
    def r(self):
        return self.p.r

    @r.setter
    def r(self, v):
        self.p.r = v

    def __getitem__(self, idx):
        return self.t[idx]


class Sched:
    NDMA = 4

    def __init__(self, nc):
        self.nc = nc
        self.eng = {"pe": nc.tensor, "dve": nc.vector, "act": nc.scalar, "pool": nc.gpsimd, "sp": nc.sync}
        self.sem = {}
        self.cnt = {}
        for k in ("pe", "dve", "act", "pool"):
            self.sem[k] = nc.alloc_semaphore("sem_" + k)
            self.cnt[k] = 0
        self.dq = {}
        self.nslot = {"sp": 4, "pool": 8, "act": 4}
        for q in ("sp", "pool", "act"):
            self.dq[q] = 0
            for s in range(self.nslot[q]):
                k = ("dma", q, s)
                self.sem[k] = nc.alloc_semaphore("sem_dma_%s_%d" % (q, s))
                self.cnt[k] = 0
        self.seen = {e: {} for e in self.eng}
        self.n_inst = 0

    def _wait(self, e, key, val):
        if val <= 0 or self.seen[e].get(key, 0) >= val:
            return
        self.eng[e].wait_ge(self.sem[key], val)
        self.seen[e][key] = val
        self.n_inst += 1

    def _deps(self, e, reads, writes, skip_same=None):
        deps = {}
        for b in reads:
            if b.w is not None:
                k, v = b.w
                deps[k] = max(deps.get(k, 0), v)
        for b in writes:
            if b.w is not None:
                k, v = b.w
                deps[k] = max(deps.get(k, 0), v)
            for (k, v) in b.r:
                deps[k] = max(deps.get(k, 0), v)
        for k, v in deps.items():
            if skip_same is not None and k == skip_same:
                continue
            self._wait(e, k, v)

    def _mark(self, key, val, reads, writes):
        for b in writes:
            b.w = (key, val)
            b.r = []
        wroots = [getattr(b, "p", b) for b in writes]
        for b in reads:
            if any(getattr(b, "p", b) is wr for wr in wroots):
                continue
            b.r = [(k, v) for (k, v) in b.r if k != key] + [(key, val)]

    def op(self, e, fn, reads=(), writes=()):
        ex = [b for b in reads if isinstance(b, Sub)]
        if ex:
            reads = [b for b in reads if not isinstance(b, Sub)]
            writes = list(writes) + ex
        self._deps(e, reads, writes, skip_same=("pe" if e == "pe" else None))
        ins = fn(self.eng[e])
        self.cnt[e] += 1
        ins.then_inc(self.sem[e], 1)
        self._mark(e, self.cnt[e], reads, writes)
        self.n_inst += 1
        return ins

    def dma(self, out_ap, in_ap, reads=(), writes=(), q="sp", fn=None, **kw):
        slot = self.dq[q] % self.nslot[q]
        self.dq[q] += 1
        key = ("dma", q, slot)
        self._wait(q, key, self.cnt[key])
        self._deps(q, reads, writes)
        if fn is not None:
            ins = fn(self.eng[q])
        else:
            ins = self.eng[q].dma_start(out=out_ap, in_=in_ap, **kw)
        self.cnt[key] += 16
        ins.then_inc(self.sem[key], 16)
        self._mark(key, self.cnt[key], reads, writes)
        self.n_inst += 1
        return ins

    def barrier(self):
        keys = [k for k in self.cnt if self.cnt[k] > 0]
        for e in self.eng:
            for k in keys:
                self._wait(e, k, self.cnt[k])

    def new_epoch(self):
        self.ep = getattr(self, "ep", 0) + 1
        for k in list(self.sem.keys()):
            nm = k if isinstance(k, str) else "dma_%s_%d" % (k[1], k[2])
            self.sem[k] = self.nc.alloc_semaphore("sem_%s_e%d" % (nm, self.ep))
            self.cnt[k] = 0
        self.seen = {e: {} for e in self.eng}
        for b in Buf.ALL:
            b.w = None
            b.r = []

    def finish(self):
        for k in self.cnt:
            if self.cnt[k] > 0:
                self._wait("sp", k, self.cnt[k])


class Proxy:
    def __init__(self, real):
        self.real = real
        self.q = None

    def op(self, *a, **k):
        if self.q is not None:
            self.q.append((self.real.op, a, k))
            return None
        return self.real.op(*a, **k)

    def dma(self, *a, **k):
        if self.q is not None:
            self.q.append((self.real.dma, a, k))
            return None
        return self.real.dma(*a, **k)

    def __getattr__(self, n):
        return getattr(self.real, n)


def interleave(proxy, fns):
    lists = []
    for f in fns:
        proxy.q = []
        f()
        lists.append(proxy.q)
    proxy.q = None
    i = 0
    while any(lists):
        for L_ in lists:
            if i < len(L_):
                fn, a, k = L_[i]
                fn(*a, **k)
        i += 1
        if all(i >= len(L_) for L_ in lists):
            break


class Cfg:
    def __init__(self, TP=2048, L=4, stages="abcdwp", debug=False, gdn_level=9, intra_stop=99, chain_steps=4, epoch_at=(2,)):
        self.debug = debug
        self.TP = TP
        self.L = L
        self.NTP = TP // 128
        self.NT = self.NTP + 1
        self.NTOK = self.NT * 128
        self.stages = stages
        self.gdn_level = gdn_level
        self.intra_stop = intra_stop
        self.chain_steps = chain_steps
        self.epoch_at = epoch_at


FQKV, FZ, FBQ, FBK, FCU, FDQ, FDK, FROWS = 0, 1536, 2048, 2560, 3072, 3584, 4096, 4224
TAB, TBK, TBV, TDK, TDV, TCOLS = 0, 8, 520, 1032, 1160, 1288
GROUPS = [
    (0, 2048, FQKV, None),
    (2048, 8, None, TAB),
    (2056, 512, FBQ, None),
    (2568, 512, FBK, TBK),
    (3080, 512, None, TBV),
    (3592, 512, FCU, None),
    (4104, 512, FDQ, None),
    (4616, 128, FDK, TDK),
    (4744, 128, None, TDV),
]


class Builder:
    def __init__(self, cfg):
        self.cfg = cfg
        self.nc = bass.Bass("TRN2", target_bir_lowering=False)
        Buf.ALL = []
        self.S = Sched(self.nc)
        self.ins = {}
        self.outs = {}
        self._evac_i = 0
        self._uid = 0

    def din(self, name, shape, dt=F32):
        t = self.nc.dram_tensor(name, list(shape), dt, kind="ExternalInput")
        b = Buf(t.ap(), name)
        self.ins[name] = b
        return b

    def dout(self, name, shape, dt=F32):
        t = self.nc.dram_tensor(name, list(shape), dt, kind="ExternalOutput")
        b = Buf(t.ap(), name)
        self.outs[name] = b
        return b

    def dscr(self, name, shape, dt=F32):
        kind = "ExternalOutput" if self.cfg.debug else "Internal"
        t = self.nc.dram_tensor(name, list(shape), dt, kind=kind)
        return Buf(t.ap(), name)

    def sb(self, st, name, shape, dt=F32):
        self._uid += 1
        name = "sb%d_%s" % (self._uid, name)
        return Buf(st.enter_context(self.nc.sbuf_tensor(name, list(shape), dt)), name)

    def ps(self, st, name, shape, dt=F32):
        self._uid += 1
        name = "ps%d_%s" % (self._uid, name)
        return Buf(st.enter_context(self.nc.psum_tensor(name, list(shape), dt)), name)

    def evac(self, out_ap, in_ap, reads, writes, eng=None):
        if eng is None:
            eng = "act" if (self._evac_i % 2 == 0) else "dve"
            self._evac_i += 1
        if eng == "act":
            return self.S.op("act", lambda e: e.activation(out=out_ap, in_=in_ap, func=AF.Copy), reads, writes)
        return self.S.op(eng, lambda e: e.tensor_copy(out=out_ap, in_=in_ap), reads, writes)

    def declare(self):
        c = self.cfg
        L, NTOK, TP = c.L, c.NTOK, c.TP
        self.xin = self.din("xin", [NTOK, D])
        self.ident = self.din("ident", [128, 128])
        self.w_in = self.din("w_in", [L, D, INC])
        self.y = self.dout("y", [NTOK, D])
        self.o_pbk = self.dout("p_b_k", [L, TP, 512])
        self.o_pbv = self.dout("p_b_v", [L, TP, 512])
        self.o_pdk = self.dout("p_d_k", [L, 128, 128])
        self.o_pdv = self.dout("p_d_v", [L, 128, 128])
        self.o_sbk = self.dout("s_b_k", [L, 4, 8, 512])
        self.o_sbv = self.dout("s_b_v", [L, 4, 8, 512])
        self.o_sdk = self.dout("s_d_k", [L, 4, 8, 128])
        self.o_sdv = self.dout("s_d_v", [L, 4, 8, 128])
        self.bmask = self.din("bmask", [16, 128, 512])
        self.bmask_s = self.din("bmask_s", [17, 128, 8])
        self.dmask = self.din("dmask", [5, 128, 512])
        self.dmask_s = self.din("dmask_s", [2, 128, 8])
        self.cache_b_kT = self.din("cache_b_kT", [L, 4, 8, 64, 2048])
        self.cache_b_v = self.din("cache_b_v", [L, 4, 2048, 512])
        self.cache_d_kT = self.din("cache_d_kT", [L, 4, 2, 64, 128])
        self.cache_d_v = self.din("cache_d_v", [L, 4, 128, 128])
        self.d_sinks = self.din("d_sinks", [L, 8])
        self.rowmask = self.din("rowmask", [128, 4])
        self.gmasks = self.din("gmasks", [3, 128, 128])
        self.valid = self.din("valid", [128, c.NT])
        self.a_cw = self.din("a_cw", [L, 128, 12, 4])
        self.a_cc = self.din("a_cc", [L, 128, 12, 4, 3])
        self.a_par = self.din("a_par", [L, 8])
        self.a_nw = self.din("a_nw", [L, 128])
        self.state_a = self.din("state_a", [L, 4, 4, 128, 128])
        self.o_paconv = self.dout("p_a_convT", [L, 1536, 3])
        self.o_saconv = self.dout("s_a_convT", [L, 4, 1536, 3])
        self.o_pastate = self.dout("p_a_state", [L, 4, 128, 128])
        self.o_sastate = self.dout("s_a_state", [L, 4, 4, 128, 128])
        self.c_lre = self.din("c_lre", [L, 128, 16])
        self.c_lim = self.din("c_lim", [L, 128, 16])
        self.c_lst = self.din("c_lst", [L, 128, 16])
        self.c_bsd_re = self.din("c_bsd_re", [L, 128, 16, 32])
        self.c_bsd_im = self.din("c_bsd_im", [L, 128, 16, 32])
        self.c_csd_re = self.din("c_csd_re", [L, 128, 16, 32])
        self.c_csd_im = self.din("c_csd_im", [L, 128, 16, 32])
        self.c_dT = self.din("c_dT", [L, 128, 4])
        self.c_glu_w = self.din("c_glu_w", [L, 512, 1024])
        self.c_glu_bT = self.din("c_glu_bT", [L, 128, 8])
        self.c_h0re = self.din("c_h0re", [L, 128, 16, 4])
        self.c_h0im = self.din("c_h0im", [L, 128, 16, 4])
        self.o_pcre = self.dout("p_c_re", [L, 128, 16])
        self.o_pcim = self.dout("p_c_im", [L, 128, 16])
        self.o_scre = self.dout("s_c_re", [L, 128, 16, 4])
        self.o_scim = self.dout("s_c_im", [L, 128, 16, 4])
        self.w_out = self.din("w_out", [L, D, D])
        self.ln1_g = self.din("ln1_g", [L, D])
        self.ln1_b = self.din("ln1_b", [L, D])
        self.ln2_g = self.din("ln2_g", [L, D])
        self.ln2_b = self.din("ln2_b", [L, D])
        self.peer_wq = self.din("peer_wq", [L, D, D])
        self.peer_keysT = self.din("peer_keysT", [L, 128, 16, 128])
        self.peer_u = [self.din("peer_u%d" % i, [16384, D]) for i in range(L)]
        self.peer_v = [self.din("peer_v%d" % i, [16384, D]) for i in range(L)]
        self.iota16 = self.din("iota16", [128, 16])
        self.sc_d = self.dscr("sc_d", [NTOK, D])
        self.uv16 = self.dscr("uv16", [16384, 2 * D], BF16)
        self.mixT = self.dscr("mixT", [2048, NTOK])
        self.featT = self.dscr("featT", [FROWS, NTOK])
        self.tokM = self.dscr("tokM", [NTOK, TCOLS])
        self.xres = self.dscr("xres", [NTOK, D])

    def transpose_tile_to_xT(self, st, xt_tile, tt, ptr, xT):
        S = self.S
        for g in range(4):
            p = ptr[g % 2]
            for j in range(4):
                kt = g * 4 + j
                S.op("pe", lambda e, kt=kt, j=j, p=p: e.transpose(out=p[:, j * 128:(j + 1) * 128],
                                                                   in_=xt_tile[:, kt * 128:(kt + 1) * 128],
                                                                   identity=self.identb[:]),
                     reads=[xt_tile, self.identb], writes=[p])
            o = xT[:, g * 4:(g + 1) * 4, tt * 128:(tt + 1) * 128]
            i = p[:, :].rearrange("p (a b) -> p a b", a=4)
            self.evac(o, i, [p], [xT])

    def phase0(self):
        c = self.cfg
        S = self.S
        with ExitStack() as st:
            xt = [self.sb(st, "p0_xt%d" % i, [128, D]) for i in range(2)]
            ptr = [self.ps(st, "p0_ptr%d" % i, [128, 512]) for i in range(2)]
            zt = self.sb(st, "p0_zero", [128, 128])
            S.op("pool", lambda e: e.memset(zt[:, :], 0.0), [], [zt])
            for kt in range(16):
                S.dma(self.mixT[kt * 128:(kt + 1) * 128, c.TP:c.NTOK], zt[:, :], reads=[zt], writes=[self.mixT], q="pool")
            for tt in range(c.NT):
                b = xt[tt % 2]
                S.dma(b[:], self.xin[tt * 128:(tt + 1) * 128, :], reads=[self.xin], writes=[b])
                S.dma(self.xres[tt * 128:(tt + 1) * 128, :], b[:], reads=[b], writes=[self.xres], q="act")
                self.transpose_tile_to_xT(st, b, tt, ptr, self.xT)
        S.barrier()

    def phase_proj(self, l):
        c = self.cfg
        S = self.S
        NTOK = c.NTOK
        tblocks = []
        t0 = 0
        while t0 < NTOK:
            n = min(512, NTOK - t0)
            tblocks.append((t0, n))
            t0 += n
        with ExitStack() as st:
            wb = [self.sb(st, "pj_w%d" % i, [128, 16, 512], BF16) for i in range(2)]
            stg = [self.sb(st, "pj_stg%d" % i, [128, NTOK]) for i in range(2)]
            stm = [self.sb(st, "pj_stm%d" % i, [128, 512]) for i in range(2)]
            pp = [self.ps(st, "pj_ps%d" % i, [128, 512]) for i in range(4)]
            wi = 0
            si = 0
            mi = 0
            pi = 0
            for (c0, ncols, frow, tcol) in GROUPS:
                for b0 in range(0, ncols, 512):
                    nb = min(512, ncols - b0)
                    w = wb[wi % 2]
                    wi += 1
                    src = self.w_in[l, :, c0 + b0:c0 + b0 + nb].rearrange("(kt p) c -> p kt c", p=128)
                    S.dma(w[:, :, 0:nb], src, reads=[self.w_in], writes=[w], q="pool")
                    if frow is not None:
                        for ct in range(0, nb, 128):
                            m = min(128, nb - ct)
                            sg = stg[si % 2]
                            si += 1
                            for (tb0, tn) in tblocks:
                                p = pp[pi % 4]
                                pi += 1
                                for kt in range(16):
                                    S.op("pe", lambda e, p=p, w=w, kt=kt, ct=ct, m=m, tb0=tb0, tn=tn: e.matmul(
                                        p[0:m, 0:tn], lhsT=w[:, kt, ct:ct + m], rhs=self.xT[:, kt, tb0:tb0 + tn],
                                        start=(kt == 0), stop=(kt == 15)), reads=[w, self.xT], writes=[p])
                                self.evac(sg[0:m, tb0:tb0 + tn], p[0:m, 0:tn], [p], [sg])
                            r0 = frow + b0 + ct
                            S.dma(self.featT[r0:r0 + m, :], sg[0:m, :], reads=[sg], writes=[self.featT])
                    if tcol is not None:
                        for tt in range(c.NT):
                            p = pp[pi % 4]
                            pi += 1
                            for kt in range(16):
                                S.op("pe", lambda e, p=p, w=w, kt=kt, tt=tt, nb=nb: e.matmul(
                                    p[:, 0:nb], lhsT=self.xT[:, kt, tt * 128:(tt + 1) * 128], rhs=w[:, kt, 0:nb],
                                    start=(kt == 0), stop=(kt == 15)), reads=[w, self.xT], writes=[p])
                            sm = stm[mi % 2]
                            mi += 1
                            self.evac(sm[:, 0:nb], p[:, 0:nb], [p], [sm])
                            S.dma(self.tokM[tt * 128:(tt + 1) * 128, tcol + b0:tcol + b0 + nb], sm[:, 0:nb],
                                  reads=[sm], writes=[self.tokM], q="act")
        S.barrier()
        TP = c.TP
        S.dma(self.o_pbk[l, :, :], self.tokM[0:TP, TBK:TBK + 512], reads=[self.tokM], writes=[self.o_pbk])
        S.dma(self.o_pbv[l, :, :], self.tokM[0:TP, TBV:TBV + 512], reads=[self.tokM], writes=[self.o_pbv])
        S.dma(self.o_pdk[l, :, :], self.tokM[TP - 128:TP, TDK:TDK + 128], reads=[self.tokM], writes=[self.o_pdk])
        S.dma(self.o_pdv[l, :, :], self.tokM[TP - 128:TP, TDV:TDV + 128], reads=[self.tokM], writes=[self.o_pdv])
        for s in range(4):
            r0 = TP + 32 * s
            S.dma(self.o_sbk[l, s, :, :], self.tokM[r0:r0 + 8, TBK:TBK + 512], reads=[self.tokM], writes=[self.o_sbk])
            S.dma(self.o_sbv[l, s, :, :], self.tokM[r0:r0 + 8, TBV:TBV + 512], reads=[self.tokM], writes=[self.o_sbv])
            S.dma(self.o_sdk[l, s, :, :], self.tokM[r0:r0 + 8, TDK:TDK + 128], reads=[self.tokM], writes=[self.o_sdk])
            S.dma(self.o_sdv[l, s, :, :], self.tokM[r0:r0 + 8, TDV:TDV + 128], reads=[self.tokM], writes=[self.o_sdv])

    def attn_core(self, qT_ap, q_reads, nq, keys, num_ps, den_ps, pr0, scale, W):
        S = self.S
        n = len(keys)
        for i, (kT_ap, nk, v_ap, mask_ap, rd) in enumerate(keys):
            sp = W["sp"][self._ai % 2]
            pe_ = W["pe"][self._ai % 2]
            pm = W["pm"][self._ai % 2]
            self._ai += 1
            S.op("pe", lambda e: e.matmul(sp[0:nk, 0:nq], lhsT=kT_ap, rhs=qT_ap, start=True, stop=True),
                 reads=list(rd) + list(q_reads), writes=[sp])
            S.op("act", lambda e: e.activation(out=pe_[0:nk, 0:nq], in_=sp[0:nk, 0:nq], func=AF.Exp, scale=scale),
                 reads=[sp], writes=[pe_])
            meng = "dve" if (self._ai % 2 == 0) else "pool"
            S.op(meng, lambda e: e.tensor_tensor(out=pm[0:nk, 0:nq], in0=pe_[0:nk, 0:nq], in1=mask_ap, op=ALU.mult),
                 reads=[pe_, W["maskbuf"]], writes=[pm])
            S.op("pe", lambda e: e.matmul(num_ps[pr0:pr0 + 64, 0:nq], lhsT=v_ap, rhs=pm[0:nk, 0:nq],
                                          start=(i == 0), stop=(i == n - 1)),
                 reads=list(rd) + [pm], writes=[num_ps])
            S.op("pe", lambda e: e.matmul(den_ps[pr0:pr0 + 64, 0:nq], lhsT=W["ones"][0:nk, 0:64], rhs=pm[0:nk, 0:nq],
                                          start=(i == 0), stop=(i == n - 1)),
                 reads=[W["ones"], pm], writes=[den_ps])

    def attn_finish(self, num_ps, den_ps, pr0, nq, sink_ap, sink_reads, W, dst_ap, dst_buf):
        S = self.S
        rc = W["rc"][self._fi % 2]
        og = W["og"][self._fi % 2]
        self._fi += 1
        if sink_ap is not None:
            S.op("dve", lambda e: e.tensor_scalar(out=rc[pr0:pr0 + 64, 0:nq], in0=den_ps[pr0:pr0 + 64, 0:nq],
                                                  scalar1=sink_ap, scalar2=None, op0=ALU.add),
                 reads=[den_ps] + list(sink_reads), writes=[rc])
            S.op("dve", lambda e: e.reciprocal(out=rc[pr0:pr0 + 64, 0:nq], in_=rc[pr0:pr0 + 64, 0:nq]),
                 reads=[rc], writes=[rc])
        else:
            S.op("dve", lambda e: e.reciprocal(out=rc[pr0:pr0 + 64, 0:nq], in_=den_ps[pr0:pr0 + 64, 0:nq]),
                 reads=[den_ps], writes=[rc])
        S.op("dve", lambda e: e.tensor_tensor(out=og[pr0:pr0 + 64, 0:nq], in0=num_ps[pr0:pr0 + 64, 0:nq],
                                              in1=rc[pr0:pr0 + 64, 0:nq], op=ALU.mult),
             reads=[num_ps, rc], writes=[og])
        S.dma(dst_ap, og[pr0:pr0 + 64, 0:nq], reads=[og], writes=[dst_buf])

    def attn_work(self, st, pfx):
        W = {}
        W["sp"] = [self.ps(st, pfx + "sp%d" % i, [128, 512]) for i in range(2)]
        W["pe"] = [self.sb(st, pfx + "pe%d" % i, [128, 512], BF16) for i in range(2)]
        W["pm"] = [self.sb(st, pfx + "pm%d" % i, [128, 512], BF16) for i in range(2)]
        W["rc"] = [self.sb(st, pfx + "rc%d" % i, [128, 512]) for i in range(2)]
        W["og"] = [self.sb(st, pfx + "og%d" % i, [128, 512]) for i in range(2)]
        W["num"] = [self.ps(st, pfx + "num%d" % i, [128, 512]) for i in range(2)]
        W["den"] = [self.ps(st, pfx + "den%d" % i, [128, 512]) for i in range(2)]
        ones = self.sb(st, pfx + "ones", [128, 64], BF16)
        self.S.op("pool", lambda e: e.memset(ones[:], 1.0), writes=[ones])
        W["ones"] = ones
        return W

    def phase_attn_b(self, l):
        c = self.cfg
        S = self.S
        TP, NTP = c.TP, c.NTP
        self._ai = 0
        self._fi = 0
        with ExitStack() as st:
            W = self.attn_work(st, "ab_")
            qT = self.sb(st, "ab_qT", [64, 8, c.NTOK], BF16)
            kT = self.sb(st, "ab_kT", [64, 8, c.NTOK], BF16)
            V = self.sb(st, "ab_V", [128, c.NT, 512], BF16)
            mk = self.sb(st, "ab_mk", [128, 16, 512], BF16)
            mks = self.sb(st, "ab_mks", [128, 17, 8], BF16)
            W["maskbuf"] = mk
            for h in range(8):
                S.dma(qT[:, h, :], self.featT[FBQ + 64 * h:FBQ + 64 * h + 64, :], reads=[self.featT], writes=[qT], q="pool")
                S.dma(kT[:, h, :], self.featT[FBK + 64 * h:FBK + 64 * h + 64, :], reads=[self.featT], writes=[kT], q="pool")
            S.dma(V[:, :, :], self.tokM[:, TBV:TBV + 512].rearrange("(t p) c -> p t c", p=128),
                  reads=[self.tokM], writes=[V], q="pool")
            S.dma(mk[:, :, :], self.bmask[:, :, :].rearrange("m p q -> p m q"), reads=[self.bmask], writes=[mk], q="pool")
            S.dma(mks[:, :, :], self.bmask_s[:, :, :].rearrange("m p q -> p m q"), reads=[self.bmask_s], writes=[mks], q="pool")
            for qb in range(0, TP, 512):
                nq = min(512, TP - qb)
                for h in range(8):
                    num = W["num"][h % 2]
                    den = W["den"][h % 2]
                    keys = []
                    for kt in range(0, (qb + nq) // 128):
                        m = (qb - kt * 128) // 128
                        keys.append((kT[:, h, kt * 128:(kt + 1) * 128], 128, V[:, kt, 64 * h:64 * h + 64],
                                     mk[:, m + 3, 0:nq], [kT, V]))
                    self.attn_core(qT[:, h, qb:qb + nq], [qT], nq, keys, num, den, 0, 0.125, W)
                    r0 = 512 + 64 * h
                    self.attn_finish(num, den, 0, nq, None, [], W, self.mixT[r0:r0 + 64, qb:qb + nq], self.mixT)
            W["maskbuf"] = mks
            with ExitStack() as st2:
                Vn = self.sb(st2, "ab_Vn", [8, 4, 512], BF16)
                for s in range(4):
                    S.dma(Vn[:, s, :], self.tokM[TP + 32 * s:TP + 32 * s + 8, TBV:TBV + 512], reads=[self.tokM], writes=[Vn], q="pool")
                ckT = [self.sb(st2, "ab_ckT%d" % i, [64, 8, 2048], BF16) for i in range(2)]
                cV = [self.sb(st2, "ab_cV0", [128, 16, 512], BF16)] * 2
                for s in range(4):
                    ck = ckT[s % 2]
                    cv = cV[s % 2]
                    S.dma(ck[:, :, :], self.cache_b_kT[l, s].rearrange("h d r -> d h r"), reads=[self.cache_b_kT], writes=[ck], q="pool")
                    S.dma(cv[:, :, :], self.cache_b_v[l, s].rearrange("(t p) c -> p t c", p=128), reads=[self.cache_b_v], writes=[cv], q="pool")
                    q0 = TP + 32 * s
                    for h in range(8):
                        num = W["num"][h % 2]
                        den = W["den"][h % 2]
                        keys = []
                        for kt in range(16):
                            keys.append((ck[:, h, kt * 128:(kt + 1) * 128], 128, cv[:, kt, 64 * h:64 * h + 64],
                                         mks[:, kt, :], [ck, cv]))
                        keys.append((kT[:, h, q0:q0 + 8], 8, Vn[0:8, s, 64 * h:64 * h + 64],
                                     mks[0:8, 16, :], [kT, Vn]))
                        self.attn_core(qT[:, h, q0:q0 + 8], [qT], 8, keys, num, den, 0, 0.125, W)
                        r0 = 512 + 64 * h
                        self.attn_finish(num, den, 0, 8, None, [], W, self.mixT[r0:r0 + 64, q0:q0 + 8], self.mixT)
        S.barrier()

    def phase_attn_d(self, l):
        c = self.cfg
        S = self.S
        TP, NTP = c.TP, c.NTP
        self._ai = 0
        self._fi = 0
        with ExitStack() as st:
            W = self.attn_work(st, "ad_")
            qT = self.sb(st, "ad_qT", [64, 8, c.NTOK], BF16)
            kT = self.sb(st, "ad_kT", [64, 2, c.NTOK], BF16)
            V = self.sb(st, "ad_V", [128, c.NT, 128], BF16)
            Vn = self.sb(st, "ad_Vn", [8, 4, 128], BF16)
            mk = self.sb(st, "ad_mk", [128, 5, 512], BF16)
            mks = self.sb(st, "ad_mks", [128, 2, 8], BF16)
            ckT = self.sb(st, "ad_ckT", [64, 4, 2, 128], BF16)
            cV = self.sb(st, "ad_cV", [128, 4, 128], BF16)
            snk = self.sb(st, "ad_snk", [64, 8])
            esk = self.sb(st, "ad_esk", [64, 8])
            S.dma(snk[:, :], self.d_sinks[l:l + 1, :].to_broadcast([64, 8]), reads=[self.d_sinks], writes=[snk])
            S.op("act", lambda e: e.activation(out=esk[:, :], in_=snk[:, :], func=AF.Exp), reads=[snk], writes=[esk])
            for h in range(8):
                S.dma(qT[:, h, :], self.featT[FDQ + 64 * h:FDQ + 64 * h + 64, :], reads=[self.featT], writes=[qT], q="pool")
            for h in range(2):
                S.dma(kT[:, h, :], self.featT[FDK + 64 * h:FDK + 64 * h + 64, :], reads=[self.featT], writes=[kT], q="pool")
            S.dma(V[:, :, :], self.tokM[:, TDV:TDV + 128].rearrange("(t p) c -> p t c", p=128),
                  reads=[self.tokM], writes=[V], q="pool")
            for s in range(4):
                S.dma(Vn[:, s, :], self.tokM[TP + 32 * s:TP + 32 * s + 8, TDV:TDV + 128], reads=[self.tokM], writes=[Vn], q="pool")
                S.dma(ckT[:, s, :, :], self.cache_d_kT[l, s].rearrange("h d r -> d h r"), reads=[self.cache_d_kT], writes=[ckT], q="pool")
                S.dma(cV[:, s, :], self.cache_d_v[l, s], reads=[self.cache_d_v], writes=[cV], q="pool")
            S.dma(mk[:, :, :], self.dmask[:, :, :].rearrange("m p q -> p m q"), reads=[self.dmask], writes=[mk], q="pool")
            S.dma(mks[:, :, :], self.dmask_s[:, :, :].rearrange("m p q -> p m q"), reads=[self.dmask_s], writes=[mks], q="pool")
            W["maskbuf"] = mk
            for qb in range(0, TP, 512):
                nq = min(512, TP - qb)
                for h in range(8):
                    kv = h // 4
                    num = W["num"][h % 2]
                    den = W["den"][h % 2]
                    keys = []
                    for kt in range(max(0, qb // 128 - 1), (qb + nq) // 128):
                        m = (qb - kt * 128) // 128
                        keys.append((kT[:, kv, kt * 128:(kt + 1) * 128], 128, V[:, kt, 64 * kv:64 * kv + 64],
                                     mk[:, 1 - m, 0:nq], [kT, V]))
                    self.attn_core(qT[:, h, qb:qb + nq], [qT], nq, keys, num, den, 0, 0.125, W)
                    r0 = 1536 + 64 * h
                    self.attn_finish(num, den, 0, nq, esk[:, h:h + 1], [esk], W, self.mixT[r0:r0 + 64, qb:qb + nq], self.mixT)
            W["maskbuf"] = mks
            for s in range(4):
                q0 = TP + 32 * s
                for h in range(8):
                    kv = h // 4
                    num = W["num"][h % 2]
                    den = W["den"][h % 2]
                    keys = [(ckT[:, s, kv, :], 128, cV[:, s, 64 * kv:64 * kv + 64], mks[:, 0, :], [ckT, cV]),
                            (kT[:, kv, q0:q0 + 8], 8, Vn[0:8, s, 64 * kv:64 * kv + 64], mks[0:8, 1, :], [kT, Vn])]
                    self.attn_core(qT[:, h, q0:q0 + 8], [qT], 8, keys, num, den, 0, 0.125, W)
                    r0 = 1536 + 64 * h
                    self.attn_finish(num, den, 0, 8, esk[:, h:h + 1], [esk], W, self.mixT[r0:r0 + 64, q0:q0 + 8], self.mixT)
        S.barrier()

    def tt(self, e, out, in0, in1, op, reads, writes):
        return self.S.op(e, lambda en: en.tensor_tensor(out=out, in0=in0, in1=in1, op=op), reads, writes)

    def phase_s5(self, l):
        c = self.cfg
        S = self.S
        TP, NTOK = c.TP, c.NTOK
        LC = 512
        PI = float(np.pi)
        with ExitStack() as st:
            def sm(name, n=16):
                return self.sb(st, "c_" + name, [128, n])
            lre, lim, lst = sm("lre"), sm("lim"), sm("lst")
            S.dma(lre[:, :], self.c_lre[l], reads=[self.c_lre], writes=[lre])
            S.dma(lim[:, :], self.c_lim[l], reads=[self.c_lim], writes=[lim])
            S.dma(lst[:, :], self.c_lst[l], reads=[self.c_lst], writes=[lst])
            dl, ar, th, rr = sm("dl"), sm("ar"), sm("th"), sm("rr")
            S.op("act", lambda e: e.activation(out=dl[:, :], in_=lst[:, :], func=AF.Exp), [lst], [dl])
            self.tt("dve", ar[:, :], lre[:, :], dl[:, :], ALU.mult, [lre, dl], [ar])
            self.tt("dve", th[:, :], lim[:, :], dl[:, :], ALU.mult, [lim, dl], [th])
            S.op("act", lambda e: e.activation(out=rr[:, :], in_=ar[:, :], func=AF.Exp), [ar], [rr])
            tq, ki, kf, ph, fx = sm("tq"), self.sb(st, "c_ki", [128, 16], I32), sm("kf"), sm("ph"), sm("fx")
            S.op("dve", lambda e: e.tensor_scalar(out=tq[:, :], in0=th[:, :], scalar1=1.0 / (2 * PI), scalar2=64.5,
                                                  op0=ALU.mult, op1=ALU.add), [th], [tq])
            S.op("dve", lambda e: e.tensor_copy(out=ki[:, :], in_=tq[:, :]), [tq], [ki])
            S.op("dve", lambda e: e.tensor_copy(out=kf[:, :], in_=ki[:, :]), [ki], [kf])
            S.op("dve", lambda e: e.tensor_scalar(out=kf[:, :], in0=kf[:, :], scalar1=-64.0, scalar2=-2 * PI,
                                                  op0=ALU.add, op1=ALU.mult), [kf], [kf])
            self.tt("dve", ph[:, :], th[:, :], kf[:, :], ALU.add, [th, kf], [ph])
            S.op("dve", lambda e: e.tensor_scalar(out=fx[:, :], in0=ph[:, :], scalar1=-PI, scalar2=2 * PI,
                                                  op0=ALU.is_lt, op1=ALU.mult), [ph], [fx])
            self.tt("dve", ph[:, :], ph[:, :], fx[:, :], ALU.add, [ph, fx], [ph])
            S.op("dve", lambda e: e.tensor_scalar(out=fx[:, :], in0=ph[:, :], scalar1=PI, scalar2=-2 * PI,
                                                  op0=ALU.is_gt, op1=ALU.mult), [ph], [fx])
            self.tt("dve", ph[:, :], ph[:, :], fx[:, :], ALU.add, [ph, fx], [ph])
            S.op("dve", lambda e: e.tensor_scalar(out=ph[:, :], in0=ph[:, :], scalar1=PI, scalar2=-PI,
                                                  op0=ALU.min, op1=ALU.max), [ph], [ph])
            sn, cs, ab = sm("sn"), sm("cs"), sm("ab")
            hpi = sm("hpi", 1)
            S.op("dve", lambda e: e.memset(hpi[:, :], PI / 2), [], [hpi])
            S.op("act", lambda e: e.activation(out=sn[:, :], in_=ph[:, :], func=AF.Sin), [ph], [sn])
            S.op("act", lambda e: e.activation(out=ab[:, :], in_=ph[:, :], func=AF.Abs), [ph], [ab])
            S.op("act", lambda e: e.activation(out=cs[:, :], in_=ab[:, :], func=AF.Sin, scale=-1.0, bias=hpi[:, 0:1]),
                 [ab, hpi], [cs])
            lbr, lbi, t1, t2, den, cr, ci = sm("lbr"), sm("lbi"), sm("t1"), sm("t2"), sm("den"), sm("cr"), sm("ci")
            self.tt("dve", lbr[:, :], rr[:, :], cs[:, :], ALU.mult, [rr, cs], [lbr])
            self.tt("dve", lbi[:, :], rr[:, :], sn[:, :], ALU.mult, [rr, sn], [lbi])
            lm1 = sm("lm1")
            S.op("dve", lambda e: e.tensor_scalar(out=lm1[:, :], in0=lbr[:, :], scalar1=-1.0, scalar2=None, op0=ALU.add), [lbr], [lm1])
            self.tt("dve", t1[:, :], lre[:, :], lre[:, :], ALU.mult, [lre], [t1])
            self.tt("dve", t2[:, :], lim[:, :], lim[:, :], ALU.mult, [lim], [t2])
            self.tt("dve", den[:, :], t1[:, :], t2[:, :], ALU.add, [t1, t2], [den])
            S.op("dve", lambda e: e.reciprocal(out=den[:, :], in_=den[:, :]), [den], [den])
            self.tt("dve", t1[:, :], lm1[:, :], lre[:, :], ALU.mult, [lm1, lre], [t1])
            self.tt("dve", t2[:, :], lbi[:, :], lim[:, :], ALU.mult, [lbi, lim], [t2])
            self.tt("dve", cr[:, :], t1[:, :], t2[:, :], ALU.add, [t1, t2], [cr])
            self.tt("dve", cr[:, :], cr[:, :], den[:, :], ALU.mult, [cr, den], [cr])
            self.tt("dve", t1[:, :], lbi[:, :], lre[:, :], ALU.mult, [lbi, lre], [t1])
            self.tt("dve", t2[:, :], lm1[:, :], lim[:, :], ALU.mult, [lm1, lim], [t2])
            self.tt("dve", ci[:, :], t1[:, :], t2[:, :], ALU.subtract, [t1, t2], [ci])
            self.tt("dve", ci[:, :], ci[:, :], den[:, :], ALU.mult, [ci, den], [ci])
            bre = self.sb(st, "c_bre", [128, 16, 32])
            bim = self.sb(st, "c_bim", [128, 16, 32])
            bpr = self.sb(st, "c_bpr", [128, 16, 32])
            bpi = self.sb(st, "c_bpi", [128, 16, 32])
            btmp = self.sb(st, "c_btmp", [128, 16, 32])
            S.dma(bre[:, :, :], self.c_bsd_re[l], reads=[self.c_bsd_re], writes=[bre])
            S.dma(bim[:, :, :], self.c_bsd_im[l], reads=[self.c_bsd_im], writes=[bim])
            crb = cr[:, :].unsqueeze(2).to_broadcast([128, 16, 32])
            cib = ci[:, :].unsqueeze(2).to_broadcast([128, 16, 32])
            self.tt("dve", bpr[:, :, :], bre[:, :, :], crb, ALU.mult, [bre, cr], [bpr])
            self.tt("dve", btmp[:, :, :], bim[:, :, :], cib, ALU.mult, [bim, ci], [btmp])
            self.tt("dve", bpr[:, :, :], bpr[:, :, :], btmp[:, :, :], ALU.subtract, [bpr, btmp], [bpr])
            self.tt("dve", bpi[:, :, :], bim[:, :, :], crb, ALU.mult, [bim, cr], [bpi])
            self.tt("dve", btmp[:, :, :], bre[:, :, :], cib, ALU.mult, [bre, ci], [btmp])
            self.tt("dve", bpi[:, :, :], bpi[:, :, :], btmp[:, :, :], ALU.add, [bpi, btmp], [bpi])
            ptr = self.ps(st, "c_ptr", [128, 512])
            rmask = self.sb(st, "c_rmask", [128, 4])
            S.dma(rmask[:, :], self.rowmask[:, :], reads=[self.rowmask], writes=[rmask])
            btr = self.sb(st, "c_btr", [128, 16, 128], BF16)
            bti = self.sb(st, "c_bti", [128, 16, 128], BF16)
            for (src, dst) in ((bpr, btr), (bpi, bti)):
                for jj in range(4):
                    S.op("pe", lambda e, src=src, jj=jj: e.transpose(out=ptr[:, jj * 128:(jj + 1) * 128],
                                                                    in_=src[:, 4 * jj:4 * jj + 4, :],
                                                                    identity=self.identf[:]),
                         reads=[src, self.identf], writes=[ptr])
                for jq in range(4):
                    S.op("dve", lambda e, dst=dst, jq=jq: e.tensor_scalar(
                        out=dst[:, jq:16:4, :], in0=ptr[:, :].rearrange("p (a b) -> p a b", a=4),
                        scalar1=rmask[:, jq:jq + 1], scalar2=None, op0=ALU.mult), [ptr, rmask], [dst])
            csr0 = self.sb(st, "c_csr0", [128, 16, 32], BF16)
            csi0 = self.sb(st, "c_csi0", [128, 16, 32], BF16)
            S.dma(csr0[:, :, :], self.c_csd_re[l], reads=[self.c_csd_re], writes=[csr0], q="pool")
            S.dma(csi0[:, :, :], self.c_csd_im[l], reads=[self.c_csd_im], writes=[csi0], q="pool")
            csr = self.sb(st, "c_csr", [128, 16, 128], BF16)
            csi = self.sb(st, "c_csi", [128, 16, 128], BF16)
            for (src, dst) in ((csr0, csr), (csi0, csi)):
                S.op("pool", lambda e, dst=dst: e.memset(dst[:, :, :], 0.0), [], [dst])
                for jq in range(4):
                    S.op("dve", lambda e, src=src, dst=dst, jq=jq: e.tensor_copy(
                        out=dst[:, jq:16:4, 32 * jq:32 * jq + 32], in_=src[:, jq:16:4, :]), [src], [dst])
            Ec = self.sb(st, "c_Ec", [128, 16, LC])
            Es = self.sb(st, "c_Es", [128, 16, LC])
            cm, smm, c2, s2, tA, tB = sm("cm"), sm("smm"), sm("c2"), sm("s2"), sm("tA"), sm("tB")
            S.op("dve", lambda e: e.memset(Ec[:, :, 0:1], 1.0), [], [Ec])
            S.op("dve", lambda e: e.memset(Es[:, :, 0:1], 0.0), [], [Es])
            S.op("dve", lambda e: e.tensor_copy(out=cm[:, :], in_=cs[:, :]), [cs], [cm])
            S.op("dve", lambda e: e.tensor_scalar(out=smm[:, :], in0=sn[:, :], scalar1=-1.0, scalar2=None, op0=ALU.mult), [sn], [smm])
            with ExitStack() as stw:
                tw1 = self.sb(stw, "c_tw1", [128, 16, LC // 2])
                tw2 = self.sb(stw, "c_tw2", [128, 16, LC // 2])
                m = 1
                while m < LC:
                    cb = cm[:, :].unsqueeze(2).to_broadcast([128, 16, m])
                    sbb = smm[:, :].unsqueeze(2).to_broadcast([128, 16, m])
                    self.tt("dve", tw1[:, :, 0:m], Ec[:, :, 0:m], cb, ALU.mult, [Ec, cm], [tw1])
                    self.tt("pool", tw2[:, :, 0:m], Es[:, :, 0:m], sbb, ALU.mult, [Es, smm], [tw2])
                    self.tt("dve", Ec[:, :, m:2 * m], tw1[:, :, 0:m], tw2[:, :, 0:m], ALU.subtract, [tw1, tw2, Ec], [Ec])
                    self.tt("dve", tw1[:, :, 0:m], Ec[:, :, 0:m], sbb, ALU.mult, [Ec, smm], [tw1])
                    self.tt("pool", tw2[:, :, 0:m], Es[:, :, 0:m], cb, ALU.mult, [Es, cm], [tw2])
                    self.tt("dve", Es[:, :, m:2 * m], tw1[:, :, 0:m], tw2[:, :, 0:m], ALU.add, [tw1, tw2, Es], [Es])
                    self.tt("dve", tA[:, :], cm[:, :], cm[:, :], ALU.mult, [cm], [tA])
                    self.tt("dve", tB[:, :], smm[:, :], smm[:, :], ALU.mult, [smm], [tB])
                    self.tt("dve", c2[:, :], tA[:, :], tB[:, :], ALU.subtract, [tA, tB], [c2])
                    self.tt("dve", s2[:, :], cm[:, :], smm[:, :], ALU.mult, [cm, smm], [s2])
                    S.op("dve", lambda e: e.tensor_scalar(out=smm[:, :], in0=s2[:, :], scalar1=2.0, scalar2=None, op0=ALU.mult), [s2], [smm])
                    S.op("dve", lambda e: e.tensor_copy(out=cm[:, :], in_=c2[:, :]), [c2], [cm])
                    m *= 2
            S.barrier()
            uT = self.sb(st, "c_uT", [128, 4, NTOK])
            uTb = self.sb(st, "c_uTb", [128, 4, NTOK], BF16)
            for jj in range(4):
                S.dma(uT[:, jj, :], self.featT[FCU + 128 * jj:FCU + 128 * jj + 128, :], reads=[self.featT], writes=[uT])
                S.dma(uTb[:, jj, :], self.featT[FCU + 128 * jj:FCU + 128 * jj + 128, :], reads=[self.featT], writes=[uTb], q="pool")
            yT = self.sb(st, "c_yT", [128, 4, NTOK], BF16)
            dcol = self.sb(st, "c_dcol", [128, 4])
            S.dma(dcol[:, :], self.c_dT[l], reads=[self.c_dT], writes=[dcol])
            hpr = self.sb(st, "c_hpr", [128, 16])
            hpi_ = self.sb(st, "c_hpi", [128, 16])
            S.op("dve", lambda e: e.memset(hpr[:, :], 0.0), [], [hpr])
            S.op("dve", lambda e: e.memset(hpi_[:, :], 0.0), [], [hpi_])
            h0r = self.sb(st, "c_h0r", [128, 16, 4])
            h0i = self.sb(st, "c_h0i", [128, 16, 4])
            S.dma(h0r[:, :, :], self.c_h0re[l], reads=[self.c_h0re], writes=[h0r])
            S.dma(h0i[:, :, :], self.c_h0im[l], reads=[self.c_h0im], writes=[h0i])
            hsr = self.sb(st, "c_hsr", [128, 16, 4])
            hsi = self.sb(st, "c_hsi", [128, 16, 4])
            inr = self.sb(st, "c_inr", [128, 16, 4])
            ini = self.sb(st, "c_ini", [128, 16, 4])
            ia = self.sb(st, "c_ia", [128, 16, 4])
            ib = self.sb(st, "c_ib", [128, 16, 4])
            W = {}
            stw2 = ExitStack()
            for nm in ("zr", "zi", "ta", "tb", "gr", "gi", "hr", "hi"):
                W[nm] = [self.sb(stw2, "c_w%s%d" % (nm, i), [128, LC]) for i in range(2)]
            W["hrb"] = [self.sb(stw2, "c_whrb%d" % i, [128, LC], BF16) for i in range(2)]
            W["hib"] = [self.sb(stw2, "c_whib%d" % i, [128, LC], BF16) for i in range(2)]
            xr = [self.ps(st, "c_xr%d" % i, [128, LC]) for i in range(2)]
            xi = [self.ps(st, "c_xi%d" % i, [128, LC]) for i in range(2)]
            yp = [self.ps(st, "c_yp%d" % i, [128, LC]) for i in range(2)]
            it = [0]

            def init_from(prev_r, prev_i, n):
                csb = cs[:, :].unsqueeze(2).to_broadcast([128, 16, n])
                snb = sn[:, :].unsqueeze(2).to_broadcast([128, 16, n])
                self.tt("dve", ia[:, :, 0:n], prev_r, csb, ALU.mult, [cs, hpr, h0r], [ia])
                self.tt("dve", ib[:, :, 0:n], prev_i, snb, ALU.mult, [sn, hpi_, h0i], [ib])
                self.tt("dve", inr[:, :, 0:n], ia[:, :, 0:n], ib[:, :, 0:n], ALU.subtract, [ia, ib], [inr])
                self.tt("dve", ia[:, :, 0:n], prev_r, snb, ALU.mult, [sn, hpr, h0r], [ia])
                self.tt("dve", ib[:, :, 0:n], prev_i, csb, ALU.mult, [cs, hpi_, h0i], [ib])
                self.tt("dve", ini[:, :, 0:n], ia[:, :, 0:n], ib[:, :, 0:n], ALU.add, [ia, ib], [ini])

            def block(col0, nseg, seglen, last_r, last_i):
                n = nseg * seglen
                ncols = n if nseg == 1 else 32 * nseg
                for j in range(16):
                    k = it[0] % 2
                    it[0] += 1
                    jj, jq = j // 4, j % 4
                    pr = slice(0, 128)
                    if nseg == 1:
                        ucols = uTb[pr, jj, col0:col0 + n]
                        def v(t):
                            return t[:, 0:n]
                        def tab(T):
                            return T[:, j, 0:n]
                    else:
                        ucols = uTb[pr, jj, col0:col0 + 32 * nseg].rearrange("p (s t) -> p s t", s=nseg)[:, :, 0:seglen]
                        def v(t):
                            return t[:, 0:n].rearrange("p (s t) -> p s t", s=nseg)
                        def tab(T):
                            return T[:, j, 0:seglen].unsqueeze(1).to_broadcast([128, nseg, seglen])
                    S.op("pe", lambda e: e.matmul(v(xr[k]), lhsT=btr[:, j, :], rhs=ucols, start=True, stop=True),
                         reads=[btr, uTb], writes=[xr[k]])
                    S.op("pe", lambda e: e.matmul(v(xi[k]), lhsT=bti[:, j, :], rhs=ucols, start=True, stop=True),
                         reads=[bti, uTb], writes=[xi[k]])
                    zr, zi, ta, tb, gr, gi, hr, hi = [W[nm][k] for nm in ("zr", "zi", "ta", "tb", "gr", "gi", "hr", "hi")]
                    self.tt("dve", v(ta), v(xr[k]), tab(Ec), ALU.mult, [xr[k], Ec], [ta])
                    self.tt("dve", v(tb), v(xi[k]), tab(Es), ALU.mult, [xi[k], Es], [tb])
                    self.tt("pool", v(zr), v(ta), v(tb), ALU.subtract, [ta, tb], [zr])
                    self.tt("dve", v(ta), v(xi[k]), tab(Ec), ALU.mult, [xi[k], Ec], [ta])
                    self.tt("dve", v(tb), v(xr[k]), tab(Es), ALU.mult, [xr[k], Es], [tb])
                    self.tt("pool", v(zi), v(ta), v(tb), ALU.add, [ta, tb], [zi])
                    rb = rr[:, j:j + 1].to_broadcast([128, seglen])
                    for sg in range(nseg):
                        cs_ = slice(sg * seglen, (sg + 1) * seglen)
                        S.op("dve", lambda e, sg=sg, cs_=cs_: e.tensor_tensor_scan(
                            out=gr[:, cs_], data0=rb, data1=zr[:, cs_], initial=inr[:, j, sg:sg + 1],
                            op0=ALU.mult, op1=ALU.add), [zr, rr, inr], [gr])
                        S.op("dve", lambda e, sg=sg, cs_=cs_: e.tensor_tensor_scan(
                            out=gi[:, cs_], data0=rb, data1=zi[:, cs_], initial=ini[:, j, sg:sg + 1],
                            op0=ALU.mult, op1=ALU.add), [zi, rr, ini], [gi])
                    self.tt("pool", v(ta), v(gr), tab(Ec), ALU.mult, [gr, Ec], [ta])
                    self.tt("pool", v(tb), v(gi), tab(Es), ALU.mult, [gi, Es], [tb])
                    self.tt("dve", v(hr), v(ta), v(tb), ALU.add, [ta, tb], [hr])
                    self.tt("pool", v(ta), v(gr), tab(Es), ALU.mult, [gr, Es], [ta])
                    self.tt("pool", v(tb), v(gi), tab(Ec), ALU.mult, [gi, Ec], [tb])
                    self.tt("dve", v(hi), v(ta), v(tb), ALU.subtract, [ta, tb], [hi])
                    hrb, hib = W["hrb"][k], W["hib"][k]
                    S.op("act", lambda e: e.activation(out=hrb[:, 0:n], in_=hr[:, 0:n], func=AF.Copy), [hr], [hrb])
                    S.op("act", lambda e: e.activation(out=hib[:, 0:n], in_=hi[:, 0:n], func=AF.Copy), [hi], [hib])
                    if nseg == 1:
                        S.op("act", lambda e: e.activation(out=last_r[:, j:j + 1], in_=hr[:, n - 1:n], func=AF.Copy), [hr], [hpr])
                        S.op("act", lambda e: e.activation(out=last_i[:, j:j + 1], in_=hi[:, n - 1:n], func=AF.Copy, scale=-1.0), [hi], [hpi_])
                    else:
                        lv = slice(seglen - 1, n, seglen)
                        S.op("act", lambda e: e.activation(out=last_r[:, j, :], in_=hr[:, lv], func=AF.Copy), [hr], [hsr])
                        S.op("act", lambda e: e.activation(out=last_i[:, j, :], in_=hi[:, lv], func=AF.Copy, scale=-1.0), [hi], [hsi])
                    ypk = yp[(it[0] // 8) % 2] if False else yp[0]
                    if nseg == 1:
                        yv = ypk[pr, 0:n]
                        hbv_r, hbv_i = hrb[:, 0:n], hib[:, 0:n]
                    else:
                        yv = ypk[pr, 0:n]
                        hbv_r, hbv_i = hrb[:, 0:n], hib[:, 0:n]
                    S.op("pe", lambda e: e.matmul(yv, lhsT=csr[:, j, :], rhs=hbv_r, start=(jq == 0), stop=False),
                         reads=[csr, hrb], writes=[ypk])
                    S.op("pe", lambda e: e.matmul(yv, lhsT=csi[:, j, :], rhs=hbv_i, start=False, stop=(jq == 3)),
                         reads=[csi, hib], writes=[ypk])
                    if jq == 3:
                        if nseg == 1:
                            uin = uT[:, jj, col0:col0 + n]
                            yout = yT[:, jj, col0:col0 + n]
                            yin = ypk[:, 0:n]
                        else:
                            uin = uT[:, jj, col0:col0 + 32 * nseg].rearrange("p (s t) -> p s t", s=nseg)[:, :, 0:seglen]
                            yout = yT[:, jj, col0:col0 + 32 * nseg].rearrange("p (s t) -> p s t", s=nseg)[:, :, 0:seglen]
                            yin = ypk[:, 0:n].rearrange("p (s t) -> p s t", s=nseg)
                        S.op("dve", lambda e: e.scalar_tensor_tensor(out=yout, in0=uin, scalar=dcol[:, jj:jj + 1], in1=yin,
                                                                     op0=ALU.mult, op1=ALU.add),
                             [uT, dcol, ypk], [yT])

            S.op("pool", lambda e: e.memset(yT[:, :, :], 0.0), [], [yT])
            for ch0 in range(0, TP, LC):
                n = min(LC, TP - ch0)
                init_from(hpr[:, :].unsqueeze(2), hpi_[:, :].unsqueeze(2), 1)
                block(ch0, 1, n, hpr, hpi_)
            S.dma(self.o_pcre[l], hpr[:, :], reads=[hpr], writes=[self.o_pcre])
            S.dma(self.o_pcim[l], hpi_[:, :], reads=[hpi_], writes=[self.o_pcim])
            init_from(h0r[:, :, :], h0i[:, :, :], 4)
            block(TP, 4, 8, hsr, hsi)
            S.dma(self.o_scre[l], hsr[:, :, :], reads=[hsr], writes=[self.o_scre])
            S.dma(self.o_scim[l], hsi[:, :, :], reads=[hsi], writes=[self.o_scim])
            S.barrier()
            stw2.close()
            gw = self.sb(st, "c_gw", [128, 4, 1024], BF16)
            gb = self.sb(st, "c_gb", [128, 8])
            S.dma(gw[:, :, :], self.c_glu_w[l].rearrange("(k p) c -> p k c", p=128), reads=[self.c_glu_w], writes=[gw], q="pool")
            S.dma(gb[:, :], self.c_glu_bT[l], reads=[self.c_glu_bT], writes=[gb])
            sg_ = [self.sb(st, "c_sig%d" % i, [128, 512]) for i in range(2)]
            og = [self.sb(st, "c_og%d" % i, [128, 512]) for i in range(2)]
            gi_ = 0
            for tb0 in range(0, NTOK, 512):
                tn = min(512, NTOK - tb0)
                for i in range(4):
                    pv, pg = xr[gi_ % 2], xi[gi_ % 2]
                    for kk in range(4):
                        S.op("pe", lambda e, kk=kk: e.matmul(pv[:, 0:tn], lhsT=gw[:, kk, 128 * i:128 * i + 128],
                                                             rhs=yT[:, kk, tb0:tb0 + tn], start=(kk == 0), stop=(kk == 3)),
                             reads=[gw, yT], writes=[pv])
                    for kk in range(4):
                        S.op("pe", lambda e, kk=kk: e.matmul(pg[:, 0:tn], lhsT=gw[:, kk, 512 + 128 * i:512 + 128 * i + 128],
                                                             rhs=yT[:, kk, tb0:tb0 + tn], start=(kk == 0), stop=(kk == 3)),
                             reads=[gw, yT], writes=[pg])
                    sgb, ogb = sg_[gi_ % 2], og[gi_ % 2]
                    gi_ += 1
                    S.op("act", lambda e: e.activation(out=sgb[:, 0:tn], in_=pg[:, 0:tn], func=AF.Sigmoid, bias=gb[:, 4 + i:5 + i]),
                         [pg, gb], [sgb])
                    S.op("dve", lambda e: e.scalar_tensor_tensor(out=ogb[:, 0:tn], in0=pv[:, 0:tn], scalar=gb[:, i:i + 1],
                                                                 in1=sgb[:, 0:tn], op0=ALU.add, op1=ALU.mult),
                         [pv, gb, sgb], [ogb])
                    S.dma(self.mixT[1024 + 128 * i:1024 + 128 * i + 128, tb0:tb0 + tn], ogb[:, 0:tn], reads=[ogb], writes=[self.mixT])
        S.barrier()

    def phase_gdn(self, l):
        c = self.cfg
        real_S = self.S
        S = Proxy(real_S)
        self.S = S
        try:
            self._phase_gdn(l, S)
        finally:
            self.S = real_S

    def _phase_gdn(self, l, S):
        c = self.cfg
        TP, NTOK, NT, NTP = c.TP, c.NTOK, c.NT, c.NTP
        with ExitStack() as st:
            qT = self.sb(st, "a_qT", [128, 4, NTOK], BF16)
            kT = self.sb(st, "a_kT", [128, 4, NTOK], BF16)
            vT = self.sb(st, "a_vT", [128, 4, NTOK], BF16)
            onesf = self.sb(st, "a_onesf", [128, 128])
            S.op("pool", lambda e: e.memset(onesf[:, :], 1.0), [], [onesf])
            with ExitStack() as st1:
                cw = self.sb(st1, "a_cw", [128, 12, 4])
                S.dma(cw[:, :, :], self.a_cw[l], reads=[self.a_cw], writes=[cw])
                EW = TP + 3
                Eb = [self.sb(st1, "a_E%d" % i, [128, EW]) for i in range(2)]
                Esb = [self.sb(st1, "a_Es%d" % i, [128, 4, 11]) for i in range(2)]
                Ob = [self.sb(st1, "a_O%d" % i, [128, NTOK]) for i in range(2)]
                Sq = [self.sb(st1, "a_Sq%d" % i, [128, 512]) for i in range(2)]
                Rs = [self.sb(st1, "a_Rs%d" % i, [128, 512]) for i in range(2)]
                pss = [self.ps(st1, "a_pss%d" % i, [128, 512]) for i in range(2)]
                epsc = self.sb(st1, "a_eps", [128, 1])
                S.op("dve", lambda e: e.memset(epsc[:, :], 1e-6), [], [epsc])
                for i in range(2):
                    S.op("dve", lambda e, i=i: e.memset(Eb[i][:, 0:3], 0.0), [], [Eb[i]])
                for ct in range(12):
                    E, Es_, O = Eb[ct % 2], Esb[ct % 2], Ob[ct % 2]
                    r0 = ct * 128
                    S.dma(E[:, 3:3 + TP], self.featT[r0:r0 + 128, 0:TP], reads=[self.featT], writes=[E])
                    S.dma(Es_[:, :, 3:11], self.featT[r0:r0 + 128, TP:TP + 128].rearrange("p (s t) -> p s t", s=4)[:, :, 0:8],
                          reads=[self.featT], writes=[Es_], q="act")
                    S.dma(Es_[:, :, 0:3], self.a_cc[l, :, ct, :, :], reads=[self.a_cc], writes=[Es_], q="act")
                    S.op("pool", lambda e: e.memset(O[:, TP:NTOK], 0.0), [], [O])
                    Osv = O[:, TP:NTOK].rearrange("p (s t) -> p s t", s=4)[:, :, 0:8]
                    for (ov, ev) in ((O[:, 0:TP], lambda j: E[:, j:j + TP]), (Osv, lambda j: Es_[:, :, j:j + 8])):
                        S.op("dve", lambda e: e.tensor_scalar(out=ov, in0=ev(0), scalar1=cw[:, ct, 0:1], scalar2=None, op0=ALU.mult),
                             [E, Es_, cw], [O])
                        for j in range(1, 4):
                            S.op("dve", lambda e, j=j: e.scalar_tensor_tensor(out=ov, in0=ev(j), scalar=cw[:, ct, j:j + 1], in1=ov,
                                                                             op0=ALU.mult, op1=ALU.add), [E, Es_, cw, O], [O])
                    S.op("act", lambda e: e.activation(out=O[:, :], in_=O[:, :], func=AF.Silu), [O], [O])
                    grp, h = ct // 4, ct % 4
                    dst = (qT, kT, vT)[grp]
                    if grp == 2:
                        S.op("act", lambda e: e.activation(out=dst[:, h, :], in_=O[:, :], func=AF.Copy), [O], [dst])
                        continue
                    for tb0 in range(0, NTOK, 512):
                        tn = min(512, NTOK - tb0)
                        sq, rs, pq = Sq[(tb0 // 512) % 2], Rs[(tb0 // 512) % 2], pss[(tb0 // 512) % 2]
                        S.op("act", lambda e: e.activation(out=sq[:, 0:tn], in_=O[:, tb0:tb0 + tn], func=AF.Square), [O], [sq])
                        S.op("pe", lambda e: e.matmul(pq[:, 0:tn], lhsT=onesf[:, :], rhs=sq[:, 0:tn], start=True, stop=True),
                             [onesf, sq], [pq])
                        S.op("act", lambda e: e.activation(out=rs[:, 0:tn], in_=pq[:, 0:tn], func=AF.Sqrt, bias=epsc[:, 0:1]), [pq, epsc], [rs])
                        S.op("dve", lambda e: e.reciprocal(out=rs[:, 0:tn], in_=rs[:, 0:tn]), [rs], [rs])
                        if grp == 0:
                            S.op("dve", lambda e: e.scalar_tensor_tensor(out=dst[:, h, tb0:tb0 + tn], in0=O[:, tb0:tb0 + tn],
                                                                         scalar=float(128 ** -0.5), in1=rs[:, 0:tn],
                                                                         op0=ALU.mult, op1=ALU.mult), [O, rs], [dst])
                        else:
                            self.tt("dve", dst[:, h, tb0:tb0 + tn], O[:, tb0:tb0 + tn], rs[:, 0:tn], ALU.mult, [O, rs], [dst])
            S.barrier()
            S.dma(self.o_paconv[l], self.featT[0:1536, TP - 3:TP], reads=[self.featT], writes=[self.o_paconv])
            for s_ in range(4):
                S.dma(self.o_saconv[l, s_], self.featT[0:1536, TP + 32 * s_ + 5:TP + 32 * s_ + 8], reads=[self.featT], writes=[self.o_saconv])
            ab = self.sb(st, "a_ab", [128, NT, 8])
            S.dma(ab[:, :, :], self.tokM[:, TAB:TAB + 8].rearrange("(t p) c -> p t c", p=128), reads=[self.tokM], writes=[ab])
            par = self.sb(st, "a_par", [128, 8])
            S.dma(par[:, :], self.a_par[l:l + 1, :].to_broadcast([128, 8]), reads=[self.a_par], writes=[par])
            vld = self.sb(st, "a_vld", [128, NT])
            S.dma(vld[:, :], self.valid[:, :], reads=[self.valid], writes=[vld])
            negA = self.sb(st, "a_negA", [128, 4])
            S.op("act", lambda e: e.activation(out=negA[:, :], in_=par[:, 0:4], func=AF.Exp), [par], [negA])
            S.op("dve", lambda e: e.tensor_scalar(out=negA[:, :], in0=negA[:, :], scalar1=-1.0, scalar2=None, op0=ALU.mult), [negA], [negA])
            gg = self.sb(st, "a_gg", [128, NT, 4])
            bb = self.sb(st, "a_bb", [128, NT, 4])
            nbb = self.sb(st, "a_nbb", [128, NT, 4])
            vb4 = vld[:, :].unsqueeze(2).to_broadcast([128, NT, 4])
            self.tt("dve", gg[:, :, :], ab[:, :, 0:4], par[:, 4:8].unsqueeze(1).to_broadcast([128, NT, 4]), ALU.add, [ab, par], [gg])
            S.op("act", lambda e: e.activation(out=gg[:, :, :], in_=gg[:, :, :], func=AF.Exp), [gg], [gg])
            S.op("act", lambda e: e.activation(out=gg[:, :, :], in_=gg[:, :, :], func=AF.Ln, bias=1.0), [gg], [gg])
            self.tt("dve", gg[:, :, :], gg[:, :, :], negA[:, :].unsqueeze(1).to_broadcast([128, NT, 4]), ALU.mult, [gg, negA], [gg])
            self.tt("dve", gg[:, :, :], gg[:, :, :], vb4, ALU.mult, [gg, vld], [gg])
            S.op("act", lambda e: e.activation(out=bb[:, :, :], in_=ab[:, :, 4:8], func=AF.Sigmoid), [ab], [bb])
            self.tt("dve", bb[:, :, :], bb[:, :, :], vb4, ALU.mult, [bb, vld], [bb])
            S.op("dve", lambda e: e.tensor_scalar(out=nbb[:, :, :], in0=bb[:, :, :], scalar1=-1.0, scalar2=None, op0=ALU.mult), [bb], [nbb])
            gm = self.sb(st, "a_gm", [128, 3, 128])
            S.dma(gm[:, :, :], self.gmasks[:, :, :].rearrange("m p q -> p m q"), reads=[self.gmasks], writes=[gm])
            bsel = self.sb(st, "a_bsel", [128, 4])
            S.dma(bsel[:, :], self.rowmask[:, :], reads=[self.rowmask], writes=[bsel])
            bselb = self.sb(st, "a_bselb", [128, 4], BF16)
            S.op("dve", lambda e: e.tensor_copy(out=bselb[:, :], in_=bsel[:, :]), [bsel], [bselb])
            identb = self.sb(st, "a_identb", [128, 128], BF16)
            S.op("dve", lambda e: e.tensor_copy(out=identb[:, :], in_=self.identf[:, :]), [self.identf], [identb])
            nw = self.sb(st, "a_nw", [128, 128])
            S.dma(nw[:, :], self.a_nw[l:l + 1, :].to_broadcast([128, 128]), reads=[self.a_nw], writes=[nw])
            Sf = [self.sb(st, "a_Sf%d" % h, [128, 128]) for h in range(4)]
            Sb = [self.sb(st, "a_Sb%d" % h, [128, 128], BF16) for h in range(4)]
            for h in range(4):
                S.op("pool", lambda e, h=h: e.memset(Sf[h][:, :], 0.0), [], [Sf[h]])
                S.op("pool", lambda e, h=h: e.memset(Sb[h][:, :], 0.0), [], [Sb[h]])
            banks = [self.ps(st, "a_bank%d" % i, [128, 512]) for i in range(8)]
            def quarter(bk, qi):
                return Sub(banks[bk][:, qi * 128:(qi + 1) * 128], banks[bk], "a_q%d_%d" % (bk, qi))
            scr = {h: [quarter(h, qi) for qi in range(4)] for h in range(4)}
            scr_i = {h: 0 for h in range(4)}
            def pscr(h):
                b = scr[h][scr_i[h] % 4]
                scr_i[h] += 1
                return b
            hb = {h: [quarter(4 + h, qi) for qi in range(4)] for h in range(4)}
            def wk(name, shape, dt=F32):
                return [[self.sb(st, "a_%s_%d_%d" % (name, h, i), shape, dt) for i in range(2)] for h in range(4)]
            u_b = wk("u", [128, 128])
            wTm_b = wk("wTm", [128, 4, 128], BF16)
            qdTm_b = wk("qdTm", [128, 4, 128], BF16)
            kdm_b = wk("kdm", [128, 4, 128], BF16)
            qkm_b = wk("qkm", [128, 4, 128], BF16)
            gl_b = wk("gl", [128, 4])
            oacc_b = wk("oacc", [128, 128])
            for h in range(4):
                for i in range(2):
                    S.op("pool", lambda e, h=h, i=i: e.memset(wTm_b[h][i][:, :, :], 0.0), [], [wTm_b[h][i]])
                    S.op("pool", lambda e, h=h, i=i: e.memset(qdTm_b[h][i][:, :, :], 0.0), [], [qdTm_b[h][i]])
            def t128(name, dt=F32, n=128):
                return self.sb(st, "a_t_" + name, [128, n], dt)
            def mkscr(h):
                def t(name, dt=F32, n=128):
                    return self.sb(st, "a_t%d_%s" % (h, name), [128, n], dt)
                return dict(gbc=t("gbc"), sml=t("sml", F32, 8), colv=t("colv", F32, 8), d1=t("d1"), d2=t("d2"), Nm=t("N"), NmT=t("NT"),
                            PT=t("PT"), Mx=t("M"), MxT=t("MT"), qkf=t("qkf", BF16), vbt=t("vb"), kbg=t("kbg"),
                            kdec=t("kdec", BF16), qdtm=t("qdtm", BF16), onrm=t("onrm"), ssq=t("ssq", F32, 2), zt=t("zt"), og=t("og"))
            SCR = [mkscr(h) for h in range(4)]
            vnb = [t128("vnb%d" % h, BF16) for h in range(4)]

            def diag_ap(buf):
                a = buf[:, :, :]
                return AP(tensor=a.tensor, offset=a.offset, ap=[list(a.ap[0]), [128 + 32, 4], [1, 32]])

            def intra(h, tt_):
                X = SCR[h]
                gbc, sml, colv, d1, d2, Nm, NmT, PT, Mx, MxT, qkf, vbt, kbg, kdec, qdtm = [X[k_] for k_ in (
                    "gbc", "sml", "colv", "d1", "d2", "Nm", "NmT", "PT", "Mx", "MxT", "qkf", "vbt", "kbg", "kdec", "qdtm")]
                par_ = tt_ % 2
                cols = slice(tt_ * 128, (tt_ + 1) * 128)
                gcol = gg[:, tt_, h:h + 1]
                bcol = bb[:, tt_, h:h + 1]
                nbcol = nbb[:, tt_, h:h + 1]
                S.op("dve", lambda e: e.tensor_scalar(out=gbc[:, :], in0=onesf[:, :], scalar1=gcol, scalar2=None, op0=ALU.mult),
                     [onesf, gg], [gbc])
                pG, pS = pscr(h), pscr(h)
                S.op("pe", lambda e: e.matmul(pG[:, :], lhsT=gbc[:, :], rhs=gm[:, 0, :], start=True, stop=True), [gbc, gm], [pG])
                S.op("pe", lambda e: e.matmul(pS[:, 0:1], lhsT=gm[:, 0, :], rhs=gcol, start=True, stop=True), [gm, gg], [pS])
                S.op("pe", lambda e: e.matmul(pS[:, 1:2], lhsT=gm[:, 1, :], rhs=gcol, start=True, stop=True), [gm, gg], [pS])
                S.op("pe", lambda e: e.matmul(pS[:, 2:6], lhsT=gbc[:, :], rhs=bsel[:, :], start=True, stop=True), [gbc, bsel], [pS])
                S.op("act", lambda e: e.activation(out=sml[:, 0:6], in_=pS[:, 0:6], func=AF.Copy), [pS], [sml])
                gl = gl_b[h][par_]
                S.op("act", lambda e: e.activation(out=gl[:, :], in_=sml[:, 2:6], func=AF.Exp), [sml], [gl])
                S.op("act", lambda e: e.activation(out=colv[:, 0:1], in_=sml[:, 0:1], func=AF.Exp), [sml], [colv])
                S.op("dve", lambda e: e.tensor_tensor(out=colv[:, 3:4], in0=sml[:, 1:2], in1=sml[:, 0:1], op=ALU.subtract), [sml, colv], [colv])
                S.op("act", lambda e: e.activation(out=colv[:, 1:2], in_=colv[:, 3:4], func=AF.Exp), [colv], [colv])
                S.op("dve", lambda e: e.tensor_tensor(out=colv[:, 2:3], in0=colv[:, 0:1], in1=bcol, op=ALU.mult), [colv, bb], [colv])
                if c.intra_stop <= 1:
                    return
                S.op("dve", lambda e: e.tensor_scalar(out=d1[:, :], in0=pG[:, :], scalar1=sml[:, 0:1], scalar2=0.0,
                                                      op0=ALU.subtract, op1=ALU.max), [pG, sml], [d1])
                S.op("act", lambda e: e.activation(out=d1[:, :], in_=d1[:, :], func=AF.Exp, scale=-1.0), [d1], [d1])
                self.tt("pool", d1[:, :], d1[:, :], gm[:, 2, :], ALU.mult, [d1, gm], [d1])
                S.op("dve", lambda e: e.tensor_scalar(out=d2[:, :], in0=pG[:, :], scalar1=sml[:, 0:1], scalar2=0.0,
                                                      op0=ALU.subtract, op1=ALU.min), [pG, sml], [d2])
                S.op("act", lambda e: e.activation(out=d2[:, :], in_=d2[:, :], func=AF.Exp), [d2], [d2])
                self.tt("pool", d2[:, :], d2[:, :], gm[:, 0, :], ALU.mult, [d2, gm], [d2])
                if c.intra_stop <= 2:
                    return
                pK, pQ = pscr(h), pscr(h)
                S.op("pe", lambda e: e.matmul(pK[:, :], lhsT=kT[:, h, cols], rhs=kT[:, h, cols], start=True, stop=True), [kT], [pK])
                S.op("pe", lambda e: e.matmul(pQ[:, :], lhsT=kT[:, h, cols], rhs=qT[:, h, cols], start=True, stop=True), [kT, qT], [pQ])
                S.op("dve", lambda e: e.scalar_tensor_tensor(out=Nm[:, :], in0=pK[:, :], scalar=nbcol, in1=d1[:, :],
                                                             op0=ALU.mult, op1=ALU.mult), [pK, nbb, d1], [Nm])
                self.tt("dve", qkf[:, :], pQ[:, :], d2[:, :], ALU.mult, [pQ, d2], [qkf])
                qkm = qkm_b[h][par_]
                self.tt("pool", qkm[:, :, :], qkf[:, :].unsqueeze(1).to_broadcast([128, 4, 128]),
                        bselb[:, :].unsqueeze(2).to_broadcast([128, 4, 128]), ALU.mult, [qkf, bselb], [qkm])
                if c.intra_stop <= 3:
                    return
                pT = pscr(h)
                S.op("pe", lambda e: e.transpose(out=pT[:, :], in_=Nm[:, :], identity=self.identf[:, :]), [Nm, self.identf], [pT])
                S.op("act", lambda e: e.activation(out=NmT[:, :], in_=pT[:, :], func=AF.Copy), [pT], [NmT])
                self.tt("dve", PT[:, :], pT[:, :], self.identf[:, :], ALU.add, [pT, self.identf], [PT])
                cur, curT = Nm, NmT
                nxt = [(Mx, MxT), (Nm, NmT)]
                for step in range(c.chain_steps):
                    M_, MT_ = nxt[step % 2]
                    pM = pscr(h)
                    S.op("pe", lambda e, cur=cur, curT=curT, pM=pM: e.matmul(pM[:, :], lhsT=curT[:, :], rhs=cur[:, :], start=True, stop=True),
                         [cur, curT], [pM])
                    if step < 3:
                        pMT = pscr(h)
                        S.op("pe", lambda e, cur=cur, curT=curT, pMT=pMT: e.matmul(pMT[:, :], lhsT=cur[:, :], rhs=curT[:, :], start=True, stop=True),
                             [cur, curT], [pMT])
                    S.op("act", lambda e, M_=M_, pM=pM: e.activation(out=M_[:, :], in_=pM[:, :], func=AF.Copy), [pM], [M_])
                    if step < 3:
                        S.op("dve", lambda e, MT_=MT_, pMT=pMT: e.tensor_copy(out=MT_[:, :], in_=pMT[:, :]), [pMT], [MT_])
                    pP = pscr(h)
                    S.op("pe", lambda e, M_=M_, pP=pP: e.matmul(pP[:, :], lhsT=M_[:, :], rhs=PT[:, :], start=True, stop=True), [M_, PT], [pP])
                    self.tt("dve", PT[:, :], PT[:, :], pP[:, :], ALU.add, [PT, pP], [PT])
                    cur, curT = M_, MT_
                if c.intra_stop <= 4:
                    return
                pkt, pvt = pscr(h), pscr(h)
                pkt_b = Buf(pkt[:, :].bitcast(BF16)[:, 0:128], pkt.name)
                pkt_b.w, pkt_b.r = pkt.w, pkt.r
                S.op("pe", lambda e: e.transpose(out=pkt_b[:, :], in_=kT[:, h, cols], identity=identb[:, :]), [kT, identb], [pkt])
                pvt_b = Buf(pvt[:, :].bitcast(BF16)[:, 0:128], pvt.name)
                S.op("pe", lambda e: e.transpose(out=pvt_b[:, :], in_=vT[:, h, cols], identity=identb[:, :]), [vT, identb], [pvt])
                S.op("dve", lambda e: e.tensor_scalar(out=kbg[:, :], in0=pkt_b[:, :], scalar1=colv[:, 2:3], scalar2=None, op0=ALU.mult),
                     [pkt, colv], [kbg])
                S.op("act", lambda e: e.activation(out=kdec[:, :], in_=pkt_b[:, :], func=AF.Copy, scale=colv[:, 1:2]), [pkt, colv], [kdec])
                kdm = kdm_b[h][par_]
                self.tt("pool", kdm[:, :, :], kdec[:, :].unsqueeze(1).to_broadcast([128, 4, 128]),
                        bselb[:, :].unsqueeze(2).to_broadcast([128, 4, 128]), ALU.mult, [kdec, bselb], [kdm])
                S.op("dve", lambda e: e.tensor_scalar(out=vbt[:, :], in0=pvt_b[:, :], scalar1=bcol, scalar2=None, op0=ALU.mult),
                     [pvt, bb], [vbt])
                if c.intra_stop <= 5:
                    return
                pqt = pscr(h)
                pqt_b = Buf(pqt[:, :].bitcast(BF16)[:, 0:128], pqt.name)
                S.op("pe", lambda e: e.transpose(out=pqt_b[:, :], in_=qT[:, h, cols], identity=identb[:, :]), [qT, identb], [pqt])
                S.op("act", lambda e: e.activation(out=qdtm[:, :], in_=pqt_b[:, :], func=AF.Copy, scale=colv[:, 0:1]), [pqt, colv], [qdtm])
                pq2 = pscr(h)
                pq2_b = Buf(pq2[:, :].bitcast(BF16)[:, 0:128], pq2.name)
                S.op("pe", lambda e: e.transpose(out=pq2_b[:, :], in_=qdtm[:, :], identity=identb[:, :]), [qdtm, identb], [pq2])
                qdTm = qdTm_b[h][par_]
                S.op("dve", lambda e: e.tensor_copy(out=diag_ap(qdTm), in_=pq2_b[:, :].rearrange("p (c t) -> p c t", c=4)), [pq2], [qdTm])
                if c.intra_stop <= 6:
                    return
                pu, pw = pscr(h), pscr(h)
                S.op("pe", lambda e: e.matmul(pu[:, :], lhsT=PT[:, :], rhs=vbt[:, :], start=True, stop=True), [PT, vbt], [pu])
                S.op("pe", lambda e: e.matmul(pw[:, :], lhsT=kbg[:, :], rhs=PT[:, :], start=True, stop=True), [kbg, PT], [pw])
                u_ = u_b[h][par_]
                S.op("act", lambda e: e.activation(out=u_[:, :], in_=pu[:, :], func=AF.Copy), [pu], [u_])
                wTm = wTm_b[h][par_]
                S.op("dve", lambda e: e.tensor_copy(out=diag_ap(wTm), in_=pw[:, :].rearrange("p (c t) -> p c t", c=4)), [pw], [wTm])

            def inter(h, tt_, cb):
                par_ = tt_ % 2
                pa, po, pS_, _ = hb[h]
                wTm, qdTm, kdm, qkm, u_, gl, oacc = (wTm_b[h][par_], qdTm_b[h][par_], kdm_b[h][par_], qkm_b[h][par_],
                                                     u_b[h][par_], gl_b[h][par_], oacc_b[h][par_])
                sample = (tt_ == NTP)
                if sample:
                    S.dma(Sf[h][:, :], self.state_a[l, cb, h], reads=[self.state_a], writes=[Sf[h]])
                    S.op("act", lambda e: e.activation(out=Sb[h][:, :], in_=Sf[h][:, :], func=AF.Copy), [Sf[h]], [Sb[h]])
                S.op("pe", lambda e: e.matmul(pa[:, :], lhsT=wTm[:, cb, :], rhs=Sb[h][:, :], start=True, stop=True), [wTm, Sb[h]], [pa])
                self.tt("dve", vnb[h][:, :], u_[:, :], pa[:, :], ALU.subtract, [u_, pa], [vnb[h]])
                S.op("pe", lambda e: e.matmul(po[:, :], lhsT=qdTm[:, cb, :], rhs=Sb[h][:, :], start=True, stop=False), [qdTm, Sb[h]], [po])
                S.op("pe", lambda e: e.matmul(po[:, :], lhsT=qkm[:, cb, :], rhs=vnb[h][:, :], start=False, stop=True), [qkm, vnb[h]], [po])
                S.op("pe", lambda e: e.matmul(pS_[:, :], lhsT=kdm[:, cb, :], rhs=vnb[h][:, :], start=True, stop=True), [kdm, vnb[h]], [pS_])
                if cb == 0:
                    S.op("act", lambda e: e.activation(out=oacc[:, :], in_=po[:, :], func=AF.Copy), [po], [oacc])
                else:
                    self.tt("pool" if False else "dve", oacc[:, :], oacc[:, :], po[:, :], ALU.add, [oacc, po], [oacc])
                S.op("dve", lambda e: e.scalar_tensor_tensor(out=Sf[h][:, :], in0=Sf[h][:, :], scalar=gl[:, cb:cb + 1], in1=pS_[:, :],
                                                             op0=ALU.mult, op1=ALU.add), [Sf[h], gl, pS_], [Sf[h]])
                if sample:
                    S.dma(self.o_sastate[l, cb, h], Sf[h][:, :], reads=[Sf[h]], writes=[self.o_sastate])
                else:
                    S.op("act", lambda e: e.activation(out=Sb[h][:, :], in_=Sf[h][:, :], func=AF.Copy), [Sf[h]], [Sb[h]])

            def outp(h, tt_):
                par_ = tt_ % 2
                oacc = oacc_b[h][par_]
                cols = slice(tt_ * 128, (tt_ + 1) * 128)
                onrm, ssq = SCR[h]["onrm"], SCR[h]["ssq"]
                z = SCR[h]["zt"]
                S.dma(z[:, :], self.featT[FZ + 128 * h:FZ + 128 * h + 128, cols], reads=[self.featT], writes=[z])
                S.op("act", lambda e: e.activation(out=z[:, :], in_=z[:, :], func=AF.Silu), [z], [z])
                S.op("act", lambda e: e.activation(out=onrm[:, :], in_=oacc[:, :], func=AF.Square, accum_out=ssq[:, 0:1]), [oacc], [onrm, ssq])
                S.op("act", lambda e: e.activation(out=ssq[:, 1:2], in_=ssq[:, 0:1], func=AF.Sqrt, scale=1.0 / 128.0, bias=epsc2[:, 0:1]), [ssq, epsc2], [ssq])
                S.op("dve", lambda e: e.reciprocal(out=ssq[:, 1:2], in_=ssq[:, 1:2]), [ssq], [ssq])
                S.op("dve", lambda e: e.scalar_tensor_tensor(out=onrm[:, :], in0=oacc[:, :], scalar=ssq[:, 1:2], in1=nw[:, :],
                                                             op0=ALU.mult, op1=ALU.mult), [oacc, ssq, nw], [onrm])
                pO = pscr(h)
                S.op("pe", lambda e: e.transpose(out=pO[:, :], in_=onrm[:, :], identity=self.identf[:, :]), [onrm, self.identf], [pO])
                o_ = SCR[h]["og"]
                self.tt("dve", o_[:, :], pO[:, :], z[:, :], ALU.mult, [pO, z], [o_])
                S.dma(self.mixT[128 * h:128 * h + 128, cols], o_[:, :], reads=[o_], writes=[self.mixT], q="act")

            epsc2 = self.sb(st, "a_eps2", [128, 1])
            S.op("dve", lambda e: e.memset(epsc2[:, :], 1e-6), [], [epsc2])
            GL = c.gdn_level
            for tt_ in range(NT if GL >= 1 else 0):
                interleave(S, [lambda h=h: intra(h, tt_) for h in range(4)])
                for cb in range(4 if GL >= 2 else 0):
                    for h in range(4):
                        inter(h, tt_, cb)
                if GL >= 3:
                    interleave(S, [lambda h=h: outp(h, tt_) for h in range(4)])
                if tt_ == NTP - 1:
                    for h in range(4):
                        S.dma(self.o_pastate[l, h], Sf[h][:, :], reads=[Sf[h]], writes=[self.o_pastate])
        S.barrier()

    def convert_tables(self, l):
        S = self.S
        i = 0
        for (src, c0) in ((self.peer_u[l], 0), (self.peer_v[l], D)):
            for r0 in range(0, 16384, 2048):
                S.dma(self.uv16[r0:r0 + 2048, c0:c0 + D], src[r0:r0 + 2048, :], reads=[src], writes=[self.uv16], q="pool")
                i += 1

    def layernorm(self, r, g, b, out, junk, stt, epsc):
        S = self.S
        S.op("act", lambda e: e.activation(out=junk[:, :], in_=r[:, :], func=AF.Copy, accum_out=stt[:, 0:1]), [r], [junk, stt])
        S.op("dve", lambda e: e.tensor_scalar(out=stt[:, 1:2], in0=stt[:, 0:1], scalar1=-1.0 / D, scalar2=None, op0=ALU.mult), [stt], [stt])
        S.op("act", lambda e: e.activation(out=junk[:, :], in_=r[:, :], func=AF.Square, bias=stt[:, 1:2], accum_out=stt[:, 2:3]),
             [r, stt], [junk, stt])
        S.op("act", lambda e: e.activation(out=stt[:, 3:4], in_=stt[:, 2:3], func=AF.Sqrt, scale=1.0 / D, bias=epsc[:, 0:1]), [stt, epsc], [stt])
        S.op("dve", lambda e: e.reciprocal(out=stt[:, 4:5], in_=stt[:, 3:4]), [stt], [stt])
        S.op("dve", lambda e: e.tensor_scalar(out=out[:, :], in0=r[:, :], scalar1=stt[:, 1:2], scalar2=stt[:, 4:5],
                                              op0=ALU.add, op1=ALU.mult), [r, stt], [out])
        self.tt("pool", out[:, :], out[:, :], g[:, :], ALU.mult, [out, g], [out])
        self.tt("dve", out[:, :], out[:, :], b[:, :], ALU.add, [out, b], [out])

    def phase_post1(self, l):
        c = self.cfg
        S = self.S
        ALPHA = float(8 ** 0.25)
        with ExitStack() as st:
            wo = self.sb(st, "w1_wo", [128, 16, D], BF16)
            for cb in range(4):
                S.dma(wo[:, :, cb * 512:(cb + 1) * 512],
                      self.w_out[l, :, cb * 512:(cb + 1) * 512].rearrange("(kt p) c -> p kt c", p=128),
                      reads=[self.w_out], writes=[wo], q="pool")
            g1 = self.sb(st, "w1_g", [128, D])
            b1 = self.sb(st, "w1_b", [128, D])
            S.dma(g1[:, :], self.ln1_g[l:l + 1, :].to_broadcast([128, D]), reads=[self.ln1_g], writes=[g1])
            S.dma(b1[:, :], self.ln1_b[l:l + 1, :].to_broadcast([128, D]), reads=[self.ln1_b], writes=[b1])
            epsc = self.sb(st, "w1_eps", [128, 1])
            S.op("dve", lambda e: e.memset(epsc[:, :], 1e-5), [], [epsc])
            mt = [self.sb(st, "w1_mt%d" % i, [128, 16, 128], BF16) for i in range(2)]
            xr = [self.sb(st, "w1_xr%d" % i, [128, D]) for i in range(2)]
            rb = [self.sb(st, "w1_r0", [128, D])] * 2
            xo = [self.sb(st, "w1_xo%d" % i, [128, D]) for i in range(2)]
            junk = self.sb(st, "w1_junk", [128, D], BF16)
            stt = [self.sb(st, "w1_st%d" % i, [128, 8]) for i in range(2)]
            pb = [self.ps(st, "w1_pb%d" % i, [128, 512]) for i in range(4)]
            ptr = [self.ps(st, "w1_ptr%d" % i, [128, 512]) for i in range(2)]
            for tt in range(c.NT):
                k = tt % 2
                cols = slice(tt * 128, (tt + 1) * 128)
                S.dma(mt[k][:, :, :], self.mixT[:, cols].rearrange("(kt p) t -> p kt t", p=128), reads=[self.mixT], writes=[mt[k]], q="pool")
                S.dma(xr[k][:, :], self.xres[cols, :], reads=[self.xres], writes=[xr[k]])
                for cb in range(4):
                    for kt in range(16):
                        S.op("pe", lambda e, cb=cb, kt=kt: e.matmul(pb[cb][:, :], lhsT=mt[k][:, kt, :], rhs=wo[:, kt, cb * 512:(cb + 1) * 512],
                                                                    start=(kt == 0), stop=(kt == 15)), reads=[mt[k], wo], writes=[pb[cb]])
                    S.op("dve", lambda e, cb=cb: e.scalar_tensor_tensor(out=rb[k][:, cb * 512:(cb + 1) * 512], in0=xr[k][:, cb * 512:(cb + 1) * 512],
                                                                        scalar=ALPHA, in1=pb[cb][:, :], op0=ALU.mult, op1=ALU.add),
                         [xr[k], pb[cb]], [rb[k]])
                self.layernorm(rb[k], g1, b1, xo[k], junk, stt[k], epsc)
                S.dma(self.xres[cols, :], xo[k][:, :], reads=[xo[k]], writes=[self.xres], q="act")
                self.transpose_tile_to_xT(st, xo[k], tt, ptr, self.xT)
        S.barrier()

    def phase_peer(self, l):
        c = self.cfg
        S = self.S
        NT, NTOK = c.NT, c.NTOK
        ALPHA = float(8 ** 0.25)
        last = (l == c.L - 1)
        with ExitStack() as st:
            kT = self.sb(st, "p1_kT", [128, 16, 128], BF16)
            S.dma(kT[:, :, :], self.peer_keysT[l], reads=[self.peer_keysT], writes=[kT], q="pool")
            wq = [self.sb(st, "p1_wq%d" % i, [128, 16, 128], BF16) for i in range(2)]
            qT = [self.sb(st, "p1_qT%d" % i, [128, 512], BF16) for i in range(2)]
            ssb = [self.sb(st, "p1_s%d" % i, [128, 4, 128]) for i in range(2)]
            pq = [self.ps(st, "p1_pq%d" % i, [128, 512]) for i in range(2)]
            psc = [self.ps(st, "p1_ps%d" % i, [128, 512]) for i in range(2)]
            it = 0
            for hc in range(16):
                w = wq[hc % 2]
                S.dma(w[:, :, :], self.peer_wq[l, :, hc * 128:(hc + 1) * 128].rearrange("(kt p) c -> p kt c", p=128),
                      reads=[self.peer_wq], writes=[w], q="pool")
                for tb0 in range(0, NTOK, 512):
                    tn = min(512, NTOK - tb0)
                    k = it % 2
                    it += 1
                    for kt in range(16):
                        S.op("pe", lambda e, kt=kt: e.matmul(pq[k][:, 0:tn], lhsT=w[:, kt, :], rhs=self.xT[:, kt, tb0:tb0 + tn],
                                                             start=(kt == 0), stop=(kt == 15)), reads=[w, self.xT], writes=[pq[k]])
                    self.evac(qT[k][:, 0:tn], pq[k][:, 0:tn], [pq[k]], [qT[k]])
                    nti = tn // 128
                    for ti in range(nti):
                        S.op("pe", lambda e, ti=ti: e.matmul(psc[k][:, ti * 128:(ti + 1) * 128], lhsT=qT[k][:, ti * 128:(ti + 1) * 128],
                                                             rhs=kT[:, hc, :], start=True, stop=True), reads=[qT[k], kT], writes=[psc[k]])
                    self.evac(ssb[k][:, 0:nti, :], psc[k][:, 0:tn].rearrange("p (a b) -> p a b", b=128), [psc[k]], [ssb[k]])
                    S.dma(self.sc_d[tb0:tb0 + tn, hc * 128:(hc + 1) * 128].rearrange("(a p) k -> p a k", p=128), ssb[k][:, 0:nti, :],
                          reads=[ssb[k]], writes=[self.sc_d], q="act")
        S.barrier()
        with ExitStack() as st:
            g2 = self.sb(st, "p2_g", [128, D])
            b2 = self.sb(st, "p2_b", [128, D])
            S.dma(g2[:, :], self.ln2_g[l:l + 1, :].to_broadcast([128, D]), reads=[self.ln2_g], writes=[g2])
            S.dma(b2[:, :], self.ln2_b[l:l + 1, :].to_broadcast([128, D]), reads=[self.ln2_b], writes=[b2])
            epsc = self.sb(st, "p2_eps", [128, 1])
            S.op("dve", lambda e: e.memset(epsc[:, :], 1e-5), [], [epsc])
            io16 = self.sb(st, "p2_io16", [128, 16])
            S.dma(io16[:, :], self.iota16[:, :], reads=[self.iota16], writes=[io16])
            ssb = self.sb(st, "p2_s", [128, D])
            x1 = self.sb(st, "p2_x1", [128, D])
            NUB = 6
            ub = [self.sb(st, "p2_ub%d" % i, [128, 2 * D], BF16) for i in range(NUB)]
            hq = [self.sb(st, "p2_hq%d" % i, [128, 4]) for i in range(4)]
            acc = self.sb(st, "p2_acc", [128, D])
            x1b = self.sb(st, "p2_x1b", [128, D], BF16)
            junkb = self.sb(st, "p2_junkb", [128, D], BF16)
            idb = self.sb(st, "p2_idb", [128, 128], BF16)
            S.op("dve", lambda e: e.tensor_copy(out=idb[:, :], in_=self.identf[:, :]), [self.identf], [idb])
            dg = [self.sb(st, "p2_dg%d" % i, [128, 128], BF16) for i in range(4)]
            pout = [self.ps(st, "p2_po%d" % i, [128, 512]) for i in range(4)]
            xo = self.sb(st, "p2_xo", [128, D])
            stt = self.sb(st, "p2_st", [128, 8])
            v = self.sb(st, "p2_v", [128, 16, 16])
            iu = self.sb(st, "p2_iu", [128, 16, 16], U32)
            i12 = self.sb(st, "p2_i12", [128, 16, 16])
            tmp = self.sb(st, "p2_tmp", [128, 256])
            cand = self.sb(st, "p2_cand", [128, 8, 256])
            scv = self.sb(st, "p2_scv", [128, 8, 16])
            cu = self.sb(st, "p2_cu", [128, 8, 16], U32)
            ca = self.sb(st, "p2_ca", [128, 8, 16], U32)
            cb_ = self.sb(st, "p2_cb", [128, 8, 16], U32)
            caf = self.sb(st, "p2_caf", [128, 8, 16])
            cbf = self.sb(st, "p2_cbf", [128, 8, 16])
            oh = self.sb(st, "p2_oh", [128, 16, 16])
            isel = self.sb(st, "p2_isel", [128, 2, 8, 16])
            eidf = self.sb(st, "p2_eidf", [128, 128])
            eid = self.sb(st, "p2_eid", [128, 128], I32)
            gate = self.sb(st, "p2_gate", [128, 8, 16])
            zs = self.sb(st, "p2_zs", [128, 8])
            hh = self.sb(st, "p2_hh", [128, 128])
            coef = self.sb(st, "p2_coef", [128, 128])
            ptr = [self.ps(st, "p2_ptr%d" % i, [128, 512]) for i in range(2)]
            ui = 0
            for tt in range(NT):
                rows = slice(tt * 128, (tt + 1) * 128)
                S.dma(ssb[:, :], self.sc_d[rows, :], reads=[self.sc_d], writes=[ssb])
                S.dma(x1[:, :], self.xres[rows, :], reads=[self.xres], writes=[x1], q="act")
                for hc in range(16):
                    sv = ssb[:, hc * 128:(hc + 1) * 128]
                    S.op("dve", lambda e: e.max(out=v[:, hc, 0:8], in_=sv), [ssb], [v])
                    S.op("dve", lambda e: e.max_index(out=iu[:, hc, 0:8], in_max=v[:, hc, 0:8], in_values=sv), [ssb, v], [iu])
                    S.op("dve", lambda e: e.match_replace(out=tmp[:, 0:128], in_to_replace=v[:, hc, 0:8], in_values=sv, imm_value=NEG),
                         [ssb, v], [tmp])
                    S.op("dve", lambda e: e.max(out=v[:, hc, 8:16], in_=tmp[:, 0:128]), [tmp], [v])
                    S.op("dve", lambda e: e.max_index(out=iu[:, hc, 8:16], in_max=v[:, hc, 8:16], in_values=tmp[:, 0:128]), [tmp, v], [iu])
                S.op("dve", lambda e: e.tensor_copy(out=i12[:, :, :], in_=iu[:, :, :]), [iu], [i12])
                for h in range(8):
                    cv = cand[:, h, :]
                    S.op("dve", lambda e: e.tensor_tensor(out=cv.rearrange("p (a b) -> p a b", a=16),
                                                          in0=v[:, 2 * h, :].unsqueeze(2).to_broadcast([128, 16, 16]),
                                                          in1=v[:, 2 * h + 1, :].unsqueeze(1).to_broadcast([128, 16, 16]), op=ALU.add),
                         [v], [cand])
                    S.op("dve", lambda e: e.max(out=scv[:, h, 0:8], in_=cv), [cand], [scv])
                    S.op("dve", lambda e: e.max_index(out=cu[:, h, 0:8], in_max=scv[:, h, 0:8], in_values=cv), [cand, scv], [cu])
                    S.op("dve", lambda e: e.match_replace(out=tmp[:, :], in_to_replace=scv[:, h, 0:8], in_values=cv, imm_value=NEG),
                         [cand, scv], [tmp])
                    S.op("dve", lambda e: e.max(out=scv[:, h, 8:16], in_=tmp[:, :]), [tmp], [scv])
                    S.op("dve", lambda e: e.max_index(out=cu[:, h, 8:16], in_max=scv[:, h, 8:16], in_values=tmp[:, :]), [tmp, scv], [cu])
                self.tt("dve", gate[:, :, :], scv[:, :, :], scv[:, :, 0:1].to_broadcast([128, 8, 16]), ALU.subtract, [scv], [gate])
                S.op("act", lambda e: e.activation(out=gate[:, :, :], in_=gate[:, :, :], func=AF.Exp), [gate], [gate])
                S.op("dve", lambda e: e.tensor_reduce(out=zs[:, :], in_=gate[:, :, :], axis=AX.X, op=ALU.add), [gate], [zs])
                S.op("dve", lambda e: e.reciprocal(out=zs[:, :], in_=zs[:, :]), [zs], [zs])
                self.tt("dve", gate[:, :, :], gate[:, :, :], zs[:, :].unsqueeze(2).to_broadcast([128, 8, 16]), ALU.mult, [gate, zs], [gate])
                S.op("dve", lambda e: e.tensor_single_scalar(out=ca[:, :, :], in_=cu[:, :, :], scalar=4, op=ALU.logical_shift_right), [cu], [ca])
                S.op("dve", lambda e: e.tensor_single_scalar(out=cb_[:, :, :], in_=cu[:, :, :], scalar=15, op=ALU.bitwise_and), [cu], [cb_])
                S.op("dve", lambda e: e.tensor_copy(out=caf[:, :, :], in_=ca[:, :, :]), [ca], [caf])
                S.op("dve", lambda e: e.tensor_copy(out=cbf[:, :, :], in_=cb_[:, :, :]), [cb_], [cbf])
                for h in range(8):
                    for half, cf in ((0, caf), (1, cbf)):
                        self.tt("dve", oh[:, :, :], cf[:, h, :].unsqueeze(2).to_broadcast([128, 16, 16]),
                                io16[:, :].unsqueeze(1).to_broadcast([128, 16, 16]), ALU.is_equal, [cf, io16], [oh])
                        self.tt("dve", oh[:, :, :], oh[:, :, :], i12[:, 2 * h + half, :].unsqueeze(1).to_broadcast([128, 16, 16]),
                                ALU.mult, [oh, i12], [oh])
                        S.op("dve", lambda e, half=half: e.tensor_reduce(out=isel[:, half, h, :], in_=oh[:, :, :], axis=AX.X, op=ALU.add),
                             [oh], [isel])
                S.op("dve", lambda e: e.scalar_tensor_tensor(out=eidf[:, :], in0=isel[:, 0, :, :].rearrange("p a b -> p (a b)"), scalar=128.0,
                                                             in1=isel[:, 1, :, :].rearrange("p a b -> p (a b)"), op0=ALU.mult, op1=ALU.add),
                     [isel], [eidf])
                S.op("dve", lambda e: e.tensor_copy(out=eid[:, :], in_=eidf[:, :]), [eidf], [eid])
                S.op("act", lambda e: e.activation(out=x1b[:, :], in_=x1[:, :], func=AF.Copy), [x1], [x1b])
                gflat = gate[:, :, :].rearrange("p a b -> p (a b)")
                for sl in range(128):
                    u_ = ub[ui % NUB]
                    q_ = hq[ui % 4]
                    ui += 1
                    S.dma(None, None, reads=[eid, self.uv16], writes=[u_], q="pool",
                          fn=lambda g, u_=u_, sl=sl: g.indirect_dma_start(
                              out=u_[:, :], out_offset=None, in_=self.uv16[:, :],
                              in_offset=bass.IndirectOffsetOnAxis(ap=eid[:, sl:sl + 1], axis=0),
                              ))
                    S.op("dve", lambda e, u_=u_, q_=q_: e.scalar_tensor_tensor(out=junkb[:, :], in0=u_[:, 0:D], scalar=1.0, in1=x1b[:, :],
                                                                               op0=ALU.mult, op1=ALU.mult, accum_out=q_[:, 0:1]),
                         [u_, x1b], [junkb, q_])
                    S.op("act", lambda e, q_=q_: e.activation(out=q_[:, 1:2], in_=q_[:, 0:1], func=AF.Gelu), [q_], [q_])
                    S.op("dve", lambda e, q_=q_, sl=sl: e.tensor_tensor(out=q_[:, 2:3], in0=q_[:, 1:2], in1=gflat[:, sl:sl + 1], op=ALU.mult),
                         [q_, gate], [q_])
                    d_ = dg[sl % 4]
                    S.op("act", lambda e, d_=d_, q_=q_: e.activation(out=d_[:, :], in_=idb[:, :], func=AF.Copy, scale=q_[:, 2:3]),
                         [idb, q_], [d_])
                    for cb in range(4):
                        S.op("pe", lambda e, d_=d_, u_=u_, cb=cb, sl=sl: e.matmul(pout[cb][:, :], lhsT=d_[:, :],
                                                                                  rhs=u_[:, D + cb * 512:D + (cb + 1) * 512],
                                                                                  start=(sl == 0), stop=(sl == 127)),
                             reads=[d_, u_], writes=[pout[cb]])
                for cb in range(4):
                    cs_ = slice(cb * 512, (cb + 1) * 512)
                    S.op("dve", lambda e, cb=cb, cs_=cs_: e.scalar_tensor_tensor(out=acc[:, cs_], in0=x1[:, cs_], scalar=ALPHA, in1=pout[cb][:, :],
                                                                                 op0=ALU.mult, op1=ALU.add), [x1, pout[cb]], [acc])
                self.layernorm(acc, g2, b2, xo, junkb, stt, epsc)
                S.dma(self.xres[rows, :], xo[:, :], reads=[xo], writes=[self.xres], q="act")
                if last:
                    S.dma(self.y[rows, :], xo[:, :], reads=[xo], writes=[self.y], q="act")
                else:
                    self.transpose_tile_to_xT(st, xo, tt, ptr, self.xT)
        S.barrier()

    def build(self):
        c = self.cfg
        S = self.S
        self.declare()
        with ExitStack() as gst:
            self.identf = self.sb(gst, "identf", [128, 128])
            self.identb = self.identf
            S.dma(self.identf[:], self.ident[:, :], reads=[self.ident], writes=[self.identf])
            xst = ExitStack()
            self.xT = self.sb(xst, "xT", [128, 16, c.NTOK], BF16)
            self.phase0()
            self.phase_proj(0)
            xst.close()
            S.barrier()
            sg = c.stages
            for l in range(c.L):
                if l in c.epoch_at:
                    S.new_epoch()
                if "p" in sg or "v" in sg:
                    self.convert_tables(l)
                if "b" in sg:
                    self.phase_attn_b(l)
                if "d" in sg:
                    self.phase_attn_d(l)
                if "c" in sg:
                    self.phase_s5(l)
                if "a" in sg:
                    self.phase_gdn(l)
                xst = ExitStack()
                self.xT = self.sb(xst, "xT", [128, 16, c.NTOK], BF16)
                if "w" in sg:
                    self.phase_post1(l)
                if "p" in sg:
                    self.phase_peer(l)
                if l < c.L - 1:
                    self.phase_proj(l + 1)
                xst.close()
                S.barrier()
            S.finish()
        return self.nc


def _bw(d):
    d = np.asarray(d)
    w = ((d >= 0) & (d <= 128)).astype(np.float32)
    w += ((d >= 0) & (d <= 512) & (d % 4 == 0))
    w += ((d >= 0) & (d <= 2048) & (d % 16 == 0))
    return w.astype(np.float32)


def host_consts():
    kl = np.arange(128)[:, None]
    ql = np.arange(512)[None, :]
    bmask = np.stack([_bw(128 * m + ql - kl) for m in range(-3, 13)])
    t8 = np.arange(8)[None, :]
    bs = [_bw(2048 + t8 - kt * 128 - kl) for kt in range(16)]
    bs.append(_bw(t8 - kl) * (kl < 8))
    bmask_s = np.stack(bs)
    dm = []
    for m in (1, 0, -1, -2, -3):
        d = 128 * m + ql - kl
        dm.append(((d >= 0) & (d <= 127)).astype(np.float32))
    dmask = np.stack(dm)
    d0 = 128 + t8 - kl
    d1 = t8 - kl
    dmask_s = np.stack([((d0 >= 0) & (d0 <= 127)).astype(np.float32),
                        ((d1 >= 0) & (d1 <= 127) & (kl < 8)).astype(np.float32)])
    return {"bmask": bmask, "bmask_s": bmask_s, "dmask": dmask, "dmask_s": dmask_s,
            "ident": np.eye(128, dtype=np.float32),
            "rowmask": np.ascontiguousarray((np.arange(128)[:, None] // 32 == np.arange(4)[None, :]).astype(np.float32))}


def host_inputs(cfg, core, inp):
    c = cfg
    pc = core % 4
    xin = np.zeros((c.NTOK, D), np.float32)
    xin[0:c.TP] = inp["x_prompt"][pc]
    for s in range(4):
        xin[c.TP + 32 * s:c.TP + 32 * s + 8] = inp["x_sample"][4 * core + s]
    m = {"xin": xin, "w_in": inp["w_in"][:c.L]}
    m.update(host_consts())
    sl = slice(4 * core, 4 * core + 4)
    L = c.L
    m["cache_b_kT"] = np.ascontiguousarray(inp["cache_b_k"][:L, sl].transpose(0, 1, 3, 4, 2))
    m["cache_b_v"] = np.ascontiguousarray(inp["cache_b_v"][:L, sl].reshape(L, 4, -1, 512))
    m["cache_d_kT"] = np.ascontiguousarray(inp["cache_d_k"][:L, sl].transpose(0, 1, 3, 4, 2))
    m["cache_d_v"] = np.ascontiguousarray(inp["cache_d_v"][:L, sl].reshape(L, 4, 128, 128))
    m["d_sinks"] = np.ascontiguousarray(inp["d_sinks"][:L].reshape(L, 8))
    blk = np.arange(128) // 32
    same = blk[:, None] == blk[None, :]
    ii = np.arange(128)
    m["gmasks"] = np.stack([(same & (ii[:, None] <= ii[None, :])), same, (same & (ii[:, None] > ii[None, :]))]).astype(np.float32)
    valid = np.ones((128, c.NT), np.float32)
    valid[:, c.NT - 1] = (np.arange(128) % 32 < 8)
    m["valid"] = valid
    m["a_cw"] = np.ascontiguousarray(inp["a_conv_w"][:L].reshape(L, 4, 12, 128).transpose(0, 3, 2, 1))
    m["a_cc"] = np.ascontiguousarray(inp["cache_a_conv"][:L, sl].reshape(L, 4, 3, 12, 128).transpose(0, 4, 3, 1, 2))
    m["a_par"] = np.ascontiguousarray(np.concatenate([inp["a_log"][:L], inp["a_dt_bias"][:L]], axis=1))
    m["a_nw"] = inp["a_norm_w"][:L]
    m["state_a"] = np.ascontiguousarray(inp["state_a"][:L, sl])
    def st_layout(a):
        sh = a.shape
        a = a.reshape((sh[0], 16, 2, 64) + sh[3:])
        a = np.moveaxis(a, 1, 3)
        return np.ascontiguousarray(a.reshape((sh[0], 128, 16) + sh[3:]))
    m["c_lre"] = st_layout(inp["c_lambda_re"][:L])
    m["c_lim"] = st_layout(inp["c_lambda_im"][:L])
    m["c_lst"] = st_layout(np.repeat(inp["c_log_step"][:L, :, None], 64, axis=2))
    def blockdiag(a):
        o = np.zeros((L, 2, 64, 16, 2, 16), np.float32)
        ar = a.reshape(L, 16, 2, 64, 16)
        for gl in range(2):
            o[:, gl, :, :, gl, :] = np.moveaxis(ar[:, :, gl], 1, 2)
        return np.ascontiguousarray(o.reshape(L, 128, 16, 32))
    m["c_bsd_re"] = blockdiag(inp["c_b_re"][:L])
    m["c_bsd_im"] = blockdiag(inp["c_b_im"][:L])
    m["c_csd_re"] = blockdiag(np.swapaxes(inp["c_c_re"][:L], 2, 3))
    m["c_csd_im"] = blockdiag(np.swapaxes(inp["c_c_im"][:L], 2, 3))
    m["c_dT"] = np.ascontiguousarray(inp["c_d"][:L].reshape(L, 4, 128).transpose(0, 2, 1))
    m["c_glu_w"] = inp["c_glu_w"][:L]
    m["c_glu_bT"] = np.ascontiguousarray(inp["c_glu_b"][:L].reshape(L, 8, 128).transpose(0, 2, 1))
    m["w_out"] = inp["w_out"][:L]
    for k in ("ln1_g", "ln1_b", "ln2_g", "ln2_b", "peer_wq"):
        m[k] = inp[k][:L]
    for i in range(L):
        m["peer_u%d" % i] = inp["peer_u"][i]
        m["peer_v%d" % i] = inp["peer_v"][i]
    m["peer_keysT"] = np.ascontiguousarray(inp["peer_sub_keys"][:L].reshape(L, 16, 128, 128).transpose(0, 3, 1, 2))
    m["iota16"] = np.ascontiguousarray(np.broadcast_to(np.arange(16, dtype=np.float32)[None, :], (128, 16)))
    m["c_h0re"] = np.ascontiguousarray(np.moveaxis(st_layout(np.moveaxis(inp["state_c_re"][:L, sl], 1, 3)), 3, 3))
    m["c_h0im"] = np.ascontiguousarray(np.moveaxis(st_layout(np.moveaxis(inp["state_c_im"][:L, sl], 1, 3)), 3, 3))
    return m


def unstate(a):
    sh = a.shape
    a = a.reshape((2, 64, 16) + sh[2:])
    a = np.moveaxis(a, 2, 0)
    return np.ascontiguousarray(a.reshape((32, 64) + sh[2:]))


def kernel(**inp):
    cfg = Cfg()
    b = Builder(cfg)
    nc = b.build()
    inp = {k: np.ascontiguousarray(np.asarray(v)) for k, v in inp.items()}
    in_maps = [host_inputs(cfg, core, inp) for core in range(8)]
    res = run_bass_kernel_spmd(nc, in_maps, core_ids=list(range(8))).results
    L, TP = cfg.L, cfg.TP
    f = np.float32
    y_p = np.zeros((4, TP, D), f)
    y_s = np.zeros((32, 8, D), f)
    p_a_conv = np.zeros((L, 4, 3, 1536), f)
    p_a_state = np.zeros((L, 4, 4, 128, 128), f)
    p_b_k = np.zeros((L, 4, TP, 8, 64), f)
    p_b_v = np.zeros((L, 4, TP, 8, 64), f)
    p_c_re = np.zeros((L, 4, 32, 64), f)
    p_c_im = np.zeros((L, 4, 32, 64), f)
    p_d_k = np.zeros((L, 4, 128, 2, 64), f)
    p_d_v = np.zeros((L, 4, 128, 2, 64), f)
    s_a_conv = np.zeros((L, 32, 3, 1536), f)
    s_a_state = np.zeros((L, 32, 4, 128, 128), f)
    s_b_k = np.zeros((L, 32, 8, 8, 64), f)
    s_b_v = np.zeros((L, 32, 8, 8, 64), f)
    s_c_re = np.zeros((L, 32, 32, 64), f)
    s_c_im = np.zeros((L, 32, 32, 64), f)
    s_d_k = np.zeros((L, 32, 8, 2, 64), f)
    s_d_v = np.zeros((L, 32, 8, 2, 64), f)
    for core in range(8):
        r = {k: np.asarray(v) for k, v in res[core].items()}
        sl = slice(4 * core, 4 * core + 4)
        for s_ in range(4):
            y_s[4 * core + s_] = r["y"][TP + 32 * s_:TP + 32 * s_ + 8]
        s_a_conv[:, sl] = np.swapaxes(r["s_a_convT"], 2, 3)
        s_a_state[:, sl] = r["s_a_state"]
        s_b_k[:, sl] = r["s_b_k"].reshape(L, 4, 8, 8, 64)
        s_b_v[:, sl] = r["s_b_v"].reshape(L, 4, 8, 8, 64)
        s_d_k[:, sl] = r["s_d_k"].reshape(L, 4, 8, 2, 64)
        s_d_v[:, sl] = r["s_d_v"].reshape(L, 4, 8, 2, 64)
        for l in range(L):
            s_c_re[l, sl] = np.moveaxis(unstate(r["s_c_re"][l]), 2, 0)
            s_c_im[l, sl] = np.moveaxis(unstate(r["s_c_im"][l]), 2, 0)
        if core < 4:
            pc = core
            y_p[pc] = r["y"][0:TP]
            p_a_conv[:, pc] = np.swapaxes(r["p_a_convT"], 1, 2)
            p_a_state[:, pc] = r["p_a_state"]
            p_b_k[:, pc] = r["p_b_k"].reshape(L, TP, 8, 64)
            p_b_v[:, pc] = r["p_b_v"].reshape(L, TP, 8, 64)
            p_d_k[:, pc] = r["p_d_k"].reshape(L, 128, 2, 64)
            p_d_v[:, pc] = r["p_d_v"].reshape(L, 128, 2, 64)
            for l in range(L):
                p_c_re[l, pc] = unstate(r["p_c_re"][l])
                p_c_im[l, pc] = unstate(r["p_c_im"][l])
    return (y_p, y_s, p_a_conv, p_a_state, p_b_k, p_b_v, p_c_re, p_c_im, p_d_k, p_d_v,
            s_a_conv, s_a_state, s_b_k, s_b_v, s_c_re, s_c_im, s_d_k, s_d_v)
```
